# Optimizing a Trainium2 kernel written in Bass

```python
import jax, jax.numpy as jnp
from jax import lax
import numpy as np

D_MODEL = 1024
BATCH = 16
SEQ = 2048
DEPTH = 4
DEC_BATCH = 16
DEC_SEQ = 32
PAST_LEN = 2048

CHUNK = 64
N_META = 16
HEAD_DIM = 64
H_A = 8
KV_A = 2
G_A = H_A // KV_A
WINDOW = 128
WIN_CHUNKS = WINDOW // CHUNK
H_B = 4
DK_B = 64
DV_B = 128
GATE_RANK = 16
GATE_TAU = 16.0
H_C = 16
Q_BLOCK = 128
D_FF = 4 * D_MODEL
ROPE_THETA = 10000.0
EPS = 1e-6
NEG = -1e30
SCALE = HEAD_DIM ** -0.5
N_EVEN = (DEPTH + 1) // 2
N_ODD = DEPTH // 2
AB_SIZES = (H_A * HEAD_DIM, KV_A * HEAD_DIM, KV_A * HEAD_DIM, H_B * DK_B, H_B * DK_B, H_B * DV_B, H_B * DV_B, GATE_RANK)
P_AB = sum(AB_SIZES)
MIX_AB = H_A * HEAD_DIM + H_B * DV_B
C_SIZES = (H_C * HEAD_DIM, H_C * HEAD_DIM, H_C * HEAD_DIM, H_C)
P_C = sum(C_SIZES)
MIX_C = H_C * HEAD_DIM

kernel_name = 'hybrid_stream_swa_gla_fox_step'


def split_cols(z, sizes):
    idx = [int(i) for i in np.cumsum(sizes)[:-1]]
    return jnp.split(z, idx, axis=-1)


def rmsnorm(x, g):
    xf = x.astype(jnp.float32)
    y = xf * lax.rsqrt(jnp.mean(xf * xf, axis=-1, keepdims=True) + EPS)
    return (y * g.astype(jnp.float32)).astype(x.dtype)


def rope(x, pos):
    half = HEAD_DIM // 2
    inv = ROPE_THETA ** (-jnp.arange(half, dtype=jnp.float32) / half)
    ang = pos.astype(jnp.float32)[:, None] * inv[None, :]
    cos = jnp.cos(ang)[None, :, None, :]
    sin = jnp.sin(ang)[None, :, None, :]
    xf = x.astype(jnp.float32)
    x1, x2 = xf[..., :half], xf[..., half:]
    return jnp.concatenate([x1 * cos - x2 * sin, x2 * cos + x1 * sin], axis=-1).astype(x.dtype)


def mlp(h, w_up, w_down):
    u = jax.nn.relu(h @ w_up)
    return (u * u) @ w_down


def ab_project(h, w_in, qn, kn, w_gate, b_gate, pos):
    B_, L = h.shape[:2]
    qa, ka, va, qb, kb, vb, rb, glr = split_cols(h @ w_in, AB_SIZES)
    qa = rope(rmsnorm(qa.reshape(B_, L, H_A, HEAD_DIM), qn), pos)
    ka = rope(rmsnorm(ka.reshape(B_, L, KV_A, HEAD_DIM), kn), pos)
    va = va.reshape(B_, L, KV_A, HEAD_DIM)
    qb = qb.reshape(B_, L, H_B, DK_B) * (DK_B ** -0.5)
    kb = kb.reshape(B_, L, H_B, DK_B)
    vb = vb.reshape(B_, L, H_B, DV_B)
    gb = jax.nn.log_sigmoid((glr @ w_gate + b_gate).astype(jnp.float32)) / GATE_TAU
    gb = gb.reshape(B_, L, H_B, DK_B)
    return qa, ka, va, qb, kb, vb, rb, gb


def sink_attend(s, sink, v, eq):
    m = jnp.maximum(jnp.max(s, axis=-1, keepdims=True), sink)
    p = jnp.exp(s - m)
    den = jnp.sum(p, axis=-1, keepdims=True) + jnp.exp(sink - m)
    return jnp.einsum(eq, p / den, v.astype(jnp.float32))


def swa_prompt(q, k, v, sink):
    B_, L = q.shape[:2]
    lead = CHUNK - N_META
    nb = (L + lead) // CHUNK
    front = lead + WIN_CHUNKS * CHUNK
    qb = jnp.pad(q, ((0, 0), (lead, 0), (0, 0), (0, 0))).reshape(B_, nb, CHUNK, KV_A, G_A, HEAD_DIM)
    kp = jnp.pad(k, ((0, 0), (front, 0), (0, 0), (0, 0))).reshape(B_, nb + WIN_CHUNKS, CHUNK, KV_A, HEAD_DIM)
    vp = jnp.pad(v, ((0, 0), (front, 0), (0, 0), (0, 0))).reshape(B_, nb + WIN_CHUNKS, CHUNK, KV_A, HEAD_DIM)
    valid = (jnp.arange((nb + WIN_CHUNKS) * CHUNK) >= front).reshape(nb + WIN_CHUNKS, CHUNK)
    kband = jnp.concatenate([kp[:, i:i + nb] for i in range(WIN_CHUNKS + 1)], axis=2)
    vband = jnp.concatenate([vp[:, i:i + nb] for i in range(WIN_CHUNKS + 1)], axis=2)
    vmask = jnp.concatenate([valid[i:i + nb] for i in range(WIN_CHUNKS + 1)], axis=1)
    s = jnp.einsum('bnqhgd,bnshd->bnhgqs', qb, kband, preferred_element_type=jnp.float32) * SCALE
    s = jnp.where(vmask[None, :, None, None, None, :], s, NEG)
    sk = sink.astype(jnp.float32).reshape(1, 1, KV_A, G_A, 1, 1)
    o = sink_attend(s, sk, vband, 'bnhgqs,bnshd->bnqhgd')
    return o.reshape(B_, nb * CHUNK, H_A * HEAD_DIM)[:, lead:].astype(q.dtype)


def swa_sample(q, k_new, v_new, ck, cv, sink):
    B_, T = q.shape[:2]
    kk = jnp.concatenate([ck.astype(k_new.dtype), k_new], axis=1)
    vv = jnp.concatenate([cv.astype(v_new.dtype), v_new], axis=1)
    qg = q.reshape(B_, T, KV_A, G_A, HEAD_DIM)
    s = jnp.einsum('bqhgd,bshd->bhgqs', qg, kk, preferred_element_type=jnp.float32) * SCALE
    sk = sink.astype(jnp.float32).reshape(1, KV_A, G_A, 1, 1)
    o = sink_attend(s, sk, vv, 'bhgqs,bshd->bqhgd')
    return o.reshape(B_, T, H_A * HEAD_DIM).astype(q.dtype), kk[:, -WINDOW:], vv[:, -WINDOW:]


def gla_chunk(S, q, k, v, g):
    C = q.shape[2]
    b = jnp.cumsum(g, axis=2)
    causal = jnp.tril(jnp.ones((C, C), dtype=bool))
    diff = b[:, :, :, None, :] - b[:, :, None, :, :]
    decay = jnp.exp(jnp.where(causal[:, :, None], diff, -jnp.inf))
    att = jnp.einsum('bhtd,bhsd,bhtsd->bhts', q, k, decay)
    o = jnp.einsum('bhtd,bhdv->bhtv', q * jnp.exp(b), S) + jnp.einsum('bhts,bhsv->bhtv', att, v)
    bl = b[:, :, -1:, :]
    S_new = jnp.exp(bl[:, :, 0, :, None]) * S + jnp.einsum('bhsd,bhsv->bhdv', k * jnp.exp(bl - b), v)
    return S_new, o


def gla_prompt(q, k, v, g):
    B_, L = q.shape[:2]
    lead = CHUNK - N_META
    f32 = jnp.float32
    padf = lambda t: jnp.pad(t.astype(f32), ((0, 0), (lead, 0), (0, 0), (0, 0)))
    Lp = L + lead
    nb = Lp // CHUNK
    to_chunks = lambda t: t.reshape(B_, nb, CHUNK, H_B, t.shape[-1]).transpose(1, 0, 3, 2, 4)
    xs = (to_chunks(padf(q)), to_chunks(padf(k)), to_chunks(padf(v)), to_chunks(padf(g)))
    S0 = jnp.zeros((B_, H_B, DK_B, DV_B), f32)
    S, o = lax.scan(lambda S, c: gla_chunk(S, *c), S0, xs)
    o = o.transpose(1, 0, 3, 2, 4).reshape(B_, Lp, H_B, DV_B)[:, lead:]
    return o.astype(q.dtype), S


def gla_sample(q, k, v, g, S):
    f32 = jnp.float32
    tr = lambda t: t.astype(f32).transpose(0, 2, 1, 3)
    S_new, o = gla_chunk(S.astype(f32), tr(q), tr(k), tr(v), tr(g))
    return o.transpose(0, 2, 1, 3).astype(q.dtype), S_new


def ab_output(oa, ob, rb, onorm, w_out):
    B_, L = oa.shape[:2]
    ob = rmsnorm(ob, onorm).reshape(B_, L, H_B * DV_B) * jax.nn.silu(rb)
    return jnp.concatenate([oa, ob], axis=-1) @ w_out


def c_project(h, w_in, b_f, qn, kn):
    B_, L = h.shape[:2]
    q, k, v, fl = split_cols(h @ w_in, C_SIZES)
    q = rmsnorm(q.reshape(B_, L, H_C, HEAD_DIM), qn)
    k = rmsnorm(k.reshape(B_, L, H_C, HEAD_DIM), kn)
    v = v.reshape(B_, L, H_C, HEAD_DIM)
    logf = jax.nn.log_sigmoid((fl + b_f).astype(jnp.float32))
    return q, k, v, logf


def fox_prompt(q, k, v, logf):
    B_, L = q.shape[:2]
    Lp = -(-L // Q_BLOCK) * Q_BLOCK
    padL = lambda t: jnp.pad(t, ((0, 0), (0, Lp - L)) + ((0, 0),) * (t.ndim - 2))
    qp, kp, vp = padL(q), padL(k), padL(v).astype(jnp.float32)
    ct = padL(jnp.cumsum(logf, axis=1)).transpose(0, 2, 1)
    kpos = jnp.arange(Lp)

    def block(i):
        start = i * Q_BLOCK
        qb = lax.dynamic_slice_in_dim(qp, start, Q_BLOCK, axis=1)
        cq = lax.dynamic_slice_in_dim(ct, start, Q_BLOCK, axis=2)
        s = jnp.einsum('bqhd,bkhd->bhqk', qb, kp, preferred_element_type=jnp.float32) * SCALE
        s = s + cq[..., None] - ct[:, :, None, :]
        qpos = start + jnp.arange(Q_BLOCK)
        s = jnp.where(kpos[None, :] <= qpos[:, None], s, NEG)
        p = jax.nn.softmax(s, axis=-1)
        return jnp.einsum('bhqk,bkhd->bqhd', p, vp)

    o = lax.map(block, jnp.arange(Lp // Q_BLOCK))
    o = o.transpose(1, 0, 2, 3, 4).reshape(B_, Lp, MIX_C)[:, :L]
    return o.astype(q.dtype)


def fox_sample(q, k_new, v_new, logf_new, ck, cv, clogf):
    B_, T = q.shape[:2]
    P = ck.shape[1]
    kk = jnp.concatenate([ck.astype(k_new.dtype), k_new], axis=1)
    vv = jnp.concatenate([cv.astype(v_new.dtype), v_new], axis=1).astype(jnp.float32)
    c = jnp.cumsum(jnp.concatenate([clogf.astype(jnp.float32), logf_new], axis=1), axis=1).transpose(0, 2, 1)
    s = jnp.einsum('bqhd,bkhd->bhqk', q, kk, preferred_element_type=jnp.float32) * SCALE
    s = s + c[:, :, P:, None] - c[:, :, None, :]
    mask = jnp.arange(P + T)[None, :] <= (P + jnp.arange(T))[:, None]
    p = jax.nn.softmax(jnp.where(mask, s, NEG), axis=-1)
    o = jnp.einsum('bhqk,bkhd->bqhd', p, vv)
    return o.reshape(B_, T, MIX_C).astype(q.dtype)


def setup_inputs(seed: int = 0) -> dict:
    key = jax.random.key(seed)
    ks = jax.random.split(key, 32)
    f32 = jnp.float32
    nrm = lambda k, shape, s=1.0: jax.random.normal(k, shape, f32) * s
    return {
        'x_prompt': nrm(ks[0], (BATCH, SEQ, D_MODEL)),
        'x_sample': nrm(ks[1], (DEC_BATCH, DEC_SEQ, D_MODEL)),
        'cache_a_k': nrm(ks[2], (N_EVEN, DEC_BATCH, WINDOW, KV_A, HEAD_DIM)),
        'cache_a_v': nrm(ks[3], (N_EVEN, DEC_BATCH, WINDOW, KV_A, HEAD_DIM)),
        'state_b': nrm(ks[4], (N_EVEN, DEC_BATCH, H_B, DK_B, DV_B), 0.5),
        'cache_c_k': nrm(ks[5], (N_ODD, DEC_BATCH, PAST_LEN, H_C, HEAD_DIM)),
        'cache_c_v': nrm(ks[6], (N_ODD, DEC_BATCH, PAST_LEN, H_C, HEAD_DIM)),
        'cache_c_logf': jax.nn.log_sigmoid(2.0 + nrm(ks[7], (N_ODD, DEC_BATCH, PAST_LEN, H_C))),
        'meta_tokens': nrm(ks[8], (N_META, D_MODEL)),
        'norm_mix': 1.0 + nrm(ks[9], (DEPTH, D_MODEL), 0.1),
        'norm_mlp': 1.0 + nrm(ks[10], (DEPTH, D_MODEL), 0.1),
        'w_in_ab': nrm(ks[11], (N_EVEN, D_MODEL, P_AB), D_MODEL ** -0.5),
        'qnorm_a': 1.0 + nrm(ks[12], (N_EVEN, HEAD_DIM), 0.1),
        'knorm_a': 1.0 + nrm(ks[13], (N_EVEN, HEAD_DIM), 0.1),
        'sink_a': nrm(ks[14], (N_EVEN, H_A), 0.5),
        'w_gate_b': nrm(ks[15], (N_EVEN, GATE_RANK, H_B * DK_B), GATE_RANK ** -0.5),
        'b_gate_b': nrm(ks[16], (N_EVEN, H_B * DK_B), 0.1),
        'onorm_b': 1.0 + nrm(ks[17], (N_EVEN, DV_B), 0.1),
        'w_out_ab': nrm(ks[18], (N_EVEN, MIX_AB, D_MODEL), MIX_AB ** -0.5),
        'w_in_c': nrm(ks[19], (N_ODD, D_MODEL, P_C), D_MODEL ** -0.5),
        'b_f_c': 2.0 + nrm(ks[20], (N_ODD, H_C), 0.1),
        'qnorm_c': 1.0 + nrm(ks[21], (N_ODD, HEAD_DIM), 0.1),
        'knorm_c': 1.0 + nrm(ks[22], (N_ODD, HEAD_DIM), 0.1),
        'w_out_c': nrm(ks[23], (N_ODD, MIX_C, D_MODEL), MIX_C ** -0.5),
        'w_up': nrm(ks[24], (DEPTH, D_MODEL, D_FF), D_MODEL ** -0.5),
        'w_down': nrm(ks[25], (DEPTH, D_FF, D_MODEL), D_FF ** -0.5),
    }


def reference(x_prompt, x_sample, cache_a_k, cache_a_v, state_b, cache_c_k, cache_c_v, cache_c_logf,
              meta_tokens, norm_mix, norm_mlp, w_in_ab, qnorm_a, knorm_a, sink_a, w_gate_b, b_gate_b,
              onorm_b, w_out_ab, w_in_c, b_f_c, qnorm_c, knorm_c, w_out_c, w_up, w_down):
    Bp = x_prompt.shape[0]
    meta = jnp.broadcast_to(meta_tokens.astype(x_prompt.dtype)[None], (Bp, N_META, D_MODEL))
    xp = jnp.concatenate([meta, x_prompt], axis=1)
    xs = x_sample
    T = xs.shape[1]
    pos_p = jnp.arange(xp.shape[1])
    pos_s = N_META + PAST_LEN + jnp.arange(T)
    akp, avp, bp, ckp, cvp, cfp = [], [], [], [], [], []
    aks, avs, bs, cks, cvs, cfs = [], [], [], [], [], []
    for l in range(DEPTH):
        i = l // 2
        hp = rmsnorm(xp, norm_mix[l])
        hs = rmsnorm(xs, norm_mix[l])
        if l % 2 == 0:
            qa, ka, va, qb, kb, vb, rb, gb = ab_project(hp, w_in_ab[i], qnorm_a[i], knorm_a[i], w_gate_b[i], b_gate_b[i], pos_p)
            oa = swa_prompt(qa, ka, va, sink_a[i])
            ob, Sp = gla_prompt(qb, kb, vb, gb)
            xp = xp + ab_output(oa, ob, rb, onorm_b[i], w_out_ab[i])
            akp.append(ka[:, -WINDOW:])
            avp.append(va[:, -WINDOW:])
            bp.append(Sp.astype(xp.dtype))
            qa, ka, va, qb, kb, vb, rb, gb = ab_project(hs, w_in_ab[i], qnorm_a[i], knorm_a[i], w_gate_b[i], b_gate_b[i], pos_s)
            oa, nk, nv = swa_sample(qa, ka, va, cache_a_k[i], cache_a_v[i], sink_a[i])
            ob, Ss = gla_sample(qb, kb, vb, gb, state_b[i])
            xs = xs + ab_output(oa, ob, rb, onorm_b[i], w_out_ab[i])
            aks.append(nk)
            avs.append(nv)
            bs.append(Ss.astype(xs.dtype))
        else:
            q, k, v, lf = c_project(hp, w_in_c[i], b_f_c[i], qnorm_c[i], knorm_c[i])
            xp = xp + fox_prompt(q, k, v, lf) @ w_out_c[i]
            ckp.append(k)
            cvp.append(v)
            cfp.append(lf.astype(xp.dtype))
            q, k, v, lf = c_project(hs, w_in_c[i], b_f_c[i], qnorm_c[i], knorm_c[i])
            xs = xs + fox_sample(q, k, v, lf, cache_c_k[i], cache_c_v[i], cache_c_logf[i]) @ w_out_c[i]
            cks.append(k)
            cvs.append(v)
            cfs.append(lf.astype(xs.dtype))
        xp = xp + mlp(rmsnorm(xp, norm_mlp[l]), w_up[l], w_down[l])
        xs = xs + mlp(rmsnorm(xs, norm_mlp[l]), w_up[l], w_down[l])
    y_prompt = xp[:, N_META:]
    return (y_prompt, xs,
            jnp.stack(akp), jnp.stack(avp), jnp.stack(bp), jnp.stack(ckp), jnp.stack(cvp), jnp.stack(cfp),
            jnp.stack(aks), jnp.stack(avs), jnp.stack(bs), jnp.stack(cks), jnp.stack(cvs), jnp.stack(cfs))
```

```python
import math
import numpy as np
from contextlib import ExitStack
import concourse.bass as bass
import concourse.mybir as mybir
from concourse.bass_utils import run_bass_kernel_spmd

F32 = mybir.dt.float32
BF16 = mybir.dt.bfloat16
AF = mybir.ActivationFunctionType
ALU = mybir.AluOpType

D = 1024
KC = 8
SEQ = 2048
NMETA = 16
TP = NMETA + SEQ
DEPTH = 4
DFF = 4096
EPS = 1e-6
NCORES = 8
P_AB = 2320
P_C = 3088

ENG = ('pe', 'act', 'dve', 'pool', 'sp')
EPOCH = 16000
NDSEM = 8


class Ins:
    __slots__ = ('eng', 'fn', 'deps', 'inc', 'dma', 'sem', 'cnt', 'waits', 'ep', 'force')


class Rec:
    def __init__(self):
        self.q = {e: [] for e in ENG}
        self.lastw = {}
        self.readers = {}
        self.ring = {e: {'next': 0, 'last': [None] * NDSEM, 'count': [0] * NDSEM} for e in ('sp', 'act', 'pool')}

    def op(self, eng, fn, reads=(), writes=(), dma=False, force=()):
        ins = Ins()
        ins.eng = eng; ins.fn = fn; ins.dma = dma; ins.inc = False; ins.deps = {}; ins.sem = None; ins.cnt = 0
        ins.force = set(force)
        deps = ins.deps
        for d in force:
            deps[d] = True
        for k in reads:
            w = self.lastw.get(k)
            if w is not None:
                deps[w] = True
            if k[0] == 'ps':
                for r in self.readers.get(k, ()):
                    if r.eng != eng:
                        deps[r] = True
        for k in writes:
            w = self.lastw.get(k)
            if w is not None and w not in deps:
                deps[w] = False
            for r in self.readers.get(k, ()):
                if r not in deps:
                    deps[r] = False
        if dma:
            ring = self.ring[eng]
            s = ring['next']; ring['next'] = (s + 1) % NDSEM
            prev = ring['last'][s]
            if prev is not None:
                deps[prev] = True
            ring['count'][s] += 16
            ins.sem = (eng, s); ins.cnt = ring['count'][s]; ring['last'][s] = ins
        for k in writes:
            self.lastw[k] = ins
            self.readers[k] = []
        for k in reads:
            lst = self.readers.get(k)
            if lst is None:
                self.readers[k] = [ins]
            else:
                if not dma:
                    lst[:] = [r for r in lst if r.dma or r.eng != eng]
                lst.append(ins)
        self.q[eng].append(ins)
        return ins

    @staticmethod
    def _needed(ins, d, raw):
        if d.dma or d in ins.force:
            return True
        if d.eng == ins.eng:
            if ins.eng == 'pe':
                return False
            return True
        return True

    def finalize(self):
        for e in ENG:
            for ins in self.q[e]:
                for d, raw in ins.deps.items():
                    if not d.dma and self._needed(ins, d, raw):
                        d.inc = True
        self.nep = {}
        for e in ENG:
            c = 0; ep = 0
            for ins in self.q[e]:
                if ins.dma:
                    continue
                if ins.inc:
                    c += 1
                    if c > EPOCH:
                        ep += 1; c = 1
                ins.cnt = c; ins.ep = ep
            self.nep[e] = ep + 1
        for e in ENG:
            waited = {}
            for ins in self.q[e]:
                w = {}
                for d, raw in ins.deps.items():
                    if not self._needed(ins, d, raw):
                        continue
                    if d.dma:
                        key = ('dma',) + d.sem; val = d.cnt
                    else:
                        key = ('tl', d.eng, d.ep); val = d.cnt
                    if waited.get(key, 0) >= val:
                        continue
                    if w.get(key, 0) < val:
                        w[key] = val
                for k, v in w.items():
                    waited[k] = v
                ins.waits = w


class Pass:
    pass


N_EVEN = 2
N_ODD = 2
NH_C = 16
HD = 64
SCALE = 0.125
NW = 3
PF = 2
ARENA = 16768


class Builder:
    def __init__(self, layers=(0, 1, 2, 3), n_seq=2, do_sample=True, mixers=True, do_mlp=True, dbg=False, x_full=False, do_prompt=True):
        self.layers = layers; self.n_seq = n_seq; self.do_sample = do_sample; self.mixers = mixers; self.dbg = dbg
        self.do_mlp = do_mlp; self.x_full = x_full; self.do_prompt = do_prompt
        self.nc = bass.Bass("TRN2", target_bir_lowering=False)
        self.R = Rec()
        self.es = ExitStack()
        self._rr = {}
        self.arena_keys = set()
        self.ps_last = {}

    def sb(self, name, shape, dt):
        return self.es.enter_context(self.nc.sbuf_tensor(name, shape, dt))

    def din(self, name, shape, dt=F32):
        return self.nc.dram_tensor(name, list(shape), dt, kind="ExternalInput").ap()

    def dout(self, name, shape, dt=F32):
        return self.nc.dram_tensor(name, list(shape), dt, kind="ExternalOutput").ap()

    def rr(self, name, n):
        v = self._rr.get(name, 0) % n
        self._rr[name] = v + 1
        return v

    def bank(self, role):
        lst = self.bank_roles[role]
        return lst[self.rr('bank_' + role, len(lst))]

    def fbuf(self):
        i = self.rr('scrF', len(self.scrF))
        return self.scrF[i], ('sF', i)

    def bbuf(self):
        i = self.rr('scrB', len(self.scrB))
        return self.scrB[i], ('sB', i)

    def pbuf(self):
        i = self.rr('ptb', len(self.ptb))
        return self.ptb[i], ('ptb', i)

    def obuf(self):
        i = self.rr('ost', len(self.ost))
        return self.ost[i], ('ost', i)

    def ak(self, key):
        self.arena_keys.add(key)
        lf = getattr(self, 'last_fence', None)
        if lf is not None and key not in self.R.lastw:
            self.R.lastw[key] = lf
            self.R.readers[key] = []
        return key

    def fence(self):
        d = self.dummy
        self.last_fence = self.R.op('dve', lambda e: e.memset(d[:, 0:1], 0.0), (), list(self.arena_keys) + [('dummy',)])

    def _pe(self, fn, r, w, rg):
        force = []
        banks = [k[1] for k in w if k[0] == 'ps']
        for b in banks:
            last = self.ps_last.get(b)
            if last is not None and last[1] != rg:
                force.append(last[0])
        ins = self.R.op('pe', fn, r, w, force=force)
        for b in banks:
            self.ps_last[b] = (ins, rg)

    def mm(self, out, lhsT, rhs, start, stop, r, w, **kw):
        rg = kw['tile_position'][0] if 'tile_position' in kw else 0
        self._pe(lambda e: e.matmul(out, lhsT, rhs, start=start, stop=stop, **kw), r, w, rg)

    def tr(self, out, in_, ident, r, w):
        self._pe(lambda e: e.transpose(out, in_, ident), r, w, 0)

    def act(self, out, in_, func, r, w, bias=None, scale=None):
        kw = {}
        if bias is not None:
            kw['bias'] = bias
        if scale is not None:
            kw['scale'] = scale
        self.R.op('act', lambda e: e.activation(out=out, in_=in_, func=func, **kw), r, w)

    def tt(self, eng, out, in0, in1, op, r, w):
        self.R.op(eng, lambda e: e.tensor_tensor(out=out, in0=in0, in1=in1, op=op), r, w)

    def ts(self, eng, out, in0, s1, s2, op0, op1, r, w):
        if op1 is None:
            self.R.op(eng, lambda e: e.tensor_scalar(out=out, in0=in0, scalar1=s1, scalar2=None, op0=op0), r, w)
        else:
            self.R.op(eng, lambda e: e.tensor_scalar(out=out, in0=in0, scalar1=s1, scalar2=s2, op0=op0, op1=op1), r, w)

    def stt(self, eng, out, in0, scalar, in1, op0, op1, r, w):
        self.R.op(eng, lambda e: e.scalar_tensor_tensor(out=out, in0=in0, scalar=scalar, in1=in1, op0=op0, op1=op1), r, w)

    def copy(self, eng, out, in_, r, w):
        if eng == 'act':
            self.R.op('act', lambda e: e.copy(out=out, in_=in_), r, w)
        else:
            self.R.op(eng, lambda e: e.tensor_copy(out=out, in_=in_), r, w)

    def recip(self, out, in_, r, w, eng='dve'):
        if eng == 'act':
            self.act(out, in_, AF.Ln, r, w)
            self.act(out, out, AF.Exp, w, w, scale=-1.0)
        else:
            self.R.op('dve', lambda e: e.reciprocal(out=out, in_=in_), r, w)

    def memset(self, eng, ap, val, w):
        self.R.op(eng, lambda e: e.memset(ap, val), (), w)

    def dma(self, q, out, in_, r, w, slow=False):
        if slow:
            self.R.op(q, lambda e: e.dma_start(out=out, in_=in_, allow_slow_non_contiguous=True), r, w, dma=True)
        else:
            self.R.op(q, lambda e: e.dma_start(out=out, in_=in_), r, w, dma=True)

    def build(self):
        nc = self.nc
        ns = self.n_seq
        if self.x_full:
            self.x_full_in = self.din("x_full", [ns, TP, D])
        else:
            self.x_prompt = self.din("x_prompt", [ns, SEQ, D])
            self.meta_tokens = self.din("meta_tokens", [NMETA, D])
        self.norm_mix = self.din("norm_mix", [DEPTH, D])
        self.norm_mlp = self.din("norm_mlp", [DEPTH, D])
        self.w_up = self.din("w_up", [DEPTH, D, DFF])
        self.w_down = self.din("w_down", [DEPTH, DFF, D])
        self.w_in_c = self.din("w_in_c", [N_ODD, D, P_C])
        self.b_f_c = self.din("b_f_c", [N_ODD, NH_C])
        self.qnorm_c = self.din("qnorm_c", [N_ODD, HD])
        self.knorm_c = self.din("knorm_c", [N_ODD, HD])
        self.w_out_c = self.din("w_out_c", [N_ODD, D, D])
        self.w_in_ab = self.din("w_in_ab", [N_EVEN, D, P_AB])
        self.qnorm_a = self.din("qnorm_a", [N_EVEN, HD])
        self.knorm_a = self.din("knorm_a", [N_EVEN, HD])
        self.sink_a = self.din("sink_a", [N_EVEN, 8])
        self.w_gate_b = self.din("w_gate_b", [N_EVEN, 16, 256])
        self.b_gate_b = self.din("b_gate_b", [N_EVEN, 256])
        self.onorm_b = self.din("onorm_b", [N_EVEN, 128])
        self.w_out_ab = self.din("w_out_ab", [N_EVEN, D, D])
        self.x_sample = self.din("x_sample", [ns, 32, D])
        self.cache_a_k = self.din("cache_a_k", [N_EVEN, ns, 128, 2, HD])
        self.cache_a_v = self.din("cache_a_v", [N_EVEN, ns, 128, 2, HD])
        self.state_b = self.din("state_b", [N_EVEN, ns, 4, 64, 128])
        self.cache_c_k = self.din("cache_c_k", [N_ODD, ns, 2048, NH_C, HD])
        self.cache_c_v = self.din("cache_c_v", [N_ODD, ns, 2048, NH_C, HD])
        self.cache_c_logf = self.din("cache_c_logf", [N_ODD, ns, 2048, NH_C])
        self.y_prompt = self.dout("y_prompt", [ns, SEQ, D])
        self.y_sample = self.dout("y_sample", [ns, 32, D])
        self.o_aks = self.dout("new_a_k_s", [N_EVEN, ns, 128, 2, HD])
        self.o_avs = self.dout("new_a_v_s", [N_EVEN, ns, 128, 2, HD])
        self.o_bs = self.dout("new_b_s", [N_EVEN, ns, 4, 64, 128])
        self.o_cks = self.dout("new_c_k_s", [N_ODD, ns, 32, NH_C, HD])
        self.o_cvs = self.dout("new_c_v_s", [N_ODD, ns, 32, NH_C, HD])
        self.o_cfs = self.dout("new_c_logf_s", [N_ODD, ns, 32, NH_C])
        self.o_ak = self.dout("new_a_k_p", [N_EVEN, ns, 128, 2, HD])
        self.o_av = self.dout("new_a_v_p", [N_EVEN, ns, 128, 2, HD])
        self.o_b = self.dout("new_b_p", [N_EVEN, ns, 4, 64, 128])
        self.o_ck = self.dout("new_c_k_p", [N_ODD, ns, TP, NH_C, HD])
        self.o_cv = self.dout("new_c_v_p", [N_ODD, ns, TP, NH_C, HD])
        self.o_cf = self.dout("new_c_logf_p", [N_ODD, ns, TP, NH_C])
        if self.dbg:
            self.dbg_x = self.dout("dbg_x", [ns, 128, KC, TP])
            self.dbg_mix = self.dout("dbg_mix", [4, 128, 2, TP], BF16)

        self.xT = self.sb("xT", [128, KC, TP], F32)
        self.hT = self.sb("hT", [128, KC, TP], BF16)
        self.arena = self.sb("arena", [128, ARENA], BF16)
        ar = self.arena
        self.ub = [ar[:, i * 4 * TP:(i + 1) * 4 * TP].rearrange("p (k t) -> p k t", k=4) for i in range(2)]
        self.QA = ar[:, 0:2 * TP].rearrange("p (h t) -> p h t", h=2)
        self.KA = ar[:, 2 * TP:4 * TP].rearrange("p (h t) -> p h t", h=2)
        o = 4 * TP
        self.VA = ar[:, o:o + 17 * 256].rearrange("p (j h d) -> p j h d", j=17, h=2)
        o += 17 * 256
        self.mixq = ar[:, o:o + 2 * TP].rearrange("p (h t) -> p h t", h=2)
        o += 2 * TP
        assert o <= ARENA
        self.wsl = [self.sb(f"wsl{i}", [128, 4096], BF16) for i in range(NW)]
        self.scrF = [self.sb(f"scrF{i}", [128, 512], F32) for i in range(4)]
        self.scrB = [self.sb(f"scrB{i}", [128, 512], BF16) for i in range(4)]
        self.ptb = [self.sb(f"ptb{i}", [128, 512], BF16) for i in range(6)]
        self.amb = [self.sb(f"amb{i}", [128, 128], BF16) for i in range(3)]
        self.ost = [self.sb(f"ost{i}", [128, 512], F32) for i in range(4)]
        self.xscr = self.sb("xscr", [128, 2080], F32)
        self.cpos = self.xscr[0:16, :]
        self.rope = self.sb("rope", [128, 2, TP], BF16)
        self.Rm = self.sb("Rm", [128, 128], BF16)
        self.trimask2 = self.sb("trimask2", [128, 64], BF16)
        self.trimask3 = self.sb("trimask3", [64, 32], BF16)
        self.ropeS = self.sb("ropeS", [128, 2, 64], BF16)
        self.segmask = self.sb("segmask", [128, 512], F32)
        self.vsink = self.sb("vsink", [1, 128], BF16)
        self.cga = self.sb("cga", [128, 10], F32)
        self.sink16 = self.sb("sink16", [1, 16], F32)
        self.sinkrow = self.sb("sinkrow", [1, 2, 512], BF16)
        self.sinkrow_m = self.sb("sinkrow_m", [1, 2, 64], BF16)
        self.wgb = self.sb("wgb", [16, 2, 256], BF16)
        self.pcol = self.sb("pcol", [128, 4], F32)
        self.invc = self.sb("invc", [128, 1], F32)
        self.kint = self.ost[3][:, :].bitcast(mybir.dt.int32)
        self.EL = self.sb("EL", [128, 33], F32)
        self.Sf = self.sb("Sf", [128, 128], F32)
        self.Sb = self.sb("Sb", [128, 128], BF16)
        self.ones_bf = self.sb("ones_bf", [128, 128], BF16)
        self.onesF = self.sb("onesF", [128, 512], F32)
        self.blockones = self.sb("blockones", [128, 128], BF16)
        self.trimask = self.sb("trimask", [128, 128], BF16)
        self.ident = self.sb("ident", [128, 128], F32)
        self.eps_col = self.sb("eps_col", [128, 1], F32)
        self.one_col = self.sb("one_col", [128, 1], F32)
        self.dummy = self.sb("fence_dummy", [128, 1], F32)
        self.gmix = self.sb("gmix", [128, DEPTH, KC], F32)
        self.gmlp = self.sb("gmlp", [128, DEPTH, KC], F32)
        self.cg = self.sb("cg", [128, 8], F32)
        self.ps = [self.es.enter_context(nc.psum_tensor(f"ps{i}", [128, 512], F32)) for i in range(8)]
        self.roles_mlp = {'pj': [0, 1, 2, 3], 'st': [4, 5], 'tr': [6, 7]}
        self.roles_mix = {'pj': [0, 1], 'sc': [2, 3], 'oa': [4, 5], 'st': [6, 7], 'tr': [7], 'sc4': [2, 3, 6, 7], 'su': [6, 7], 'pjw': [0, 1, 2, 3, 4, 5], 'oa4': [4, 5, 0, 1]}
        self.bank_roles = self.roles_mlp

        self.consts()
        if self.mixers and any(l % 2 == 0 for l in self.layers):
            self.ab_consts()

        tiles = [(0, NMETA)] + [(NMETA + 512 * i, 512) for i in range(4)]
        if self.do_sample:
            self.sample_pass()
        for s in range(ns if self.do_prompt else 0):
            P = Pass(); P.s = s; P.T = TP; P.tiles = tiles
            self.make_plan(P)
            self.load_x(P)
            for l in self.layers:
                if self.mixers:
                    self.bank_roles = self.roles_mix
                    if l % 2 == 1:
                        self.fox_layer(P, l)
                    else:
                        self.ab_layer(P, l)
                    self.bank_roles = self.roles_mlp
                if self.dbg and not self.do_mlp:
                    self.dma('sp', self.dbg_x[s], self.xT[:], [('x', kc, ti) for kc in range(KC) for ti in range(len(tiles))], [('dbgx', s)])
                if self.do_mlp:
                    self.mlp(P, l)
            if self.dbg and self.do_mlp:
                self.dma('sp', self.dbg_x[s], self.xT[:], [('x', kc, ti) for kc in range(KC) for ti in range(len(tiles))], [('dbgx', s)])
            self.store_y(P)
        self.emit()
        return nc

    def consts(self):
        self.memset('pool', self.ones_bf[:], 1.0, [('c', 'ones')])
        self.memset('pool', self.ident[:], 0.0, [('c', 'ident')])
        ident = self.ident
        self.R.op('pool', lambda e: e.affine_select(out=ident[:], in_=ident[:], pattern=[[-1, 128]], compare_op=ALU.not_equal,
                                                   fill=1.0, base=0, channel_multiplier=1), [('c', 'ident')], [('c', 'ident')])
        self.memset('pool', self.eps_col[:], EPS, [('c', 'eps')])
        self.memset('pool', self.onesF[:], 1.0, [('c', 'onesF')])
        self.memset('pool', self.one_col[:], 1.0, [('c', 'one')])
        self.memset('pool', self.blockones[:], 0.0, [('c', 'bo')])
        self.memset('pool', self.blockones[0:64, 0:64], 1.0, [('c', 'bo')])
        self.memset('pool', self.blockones[64:128, 64:128], 1.0, [('c', 'bo')])
        self.memset('pool', self.trimask[:], 1.0, [('c', 'tri')])
        tri = self.trimask
        self.R.op('pool', lambda e: e.affine_select(out=tri[:], in_=tri[:], pattern=[[1, 128]], compare_op=ALU.is_ge,
                                                   fill=0.0, base=0, channel_multiplier=-1), [('c', 'tri')], [('c', 'tri')])
        self.load_cols(self.gmix[:].rearrange("p l k -> p (l k)"), [(0, 128, self.norm_mix.rearrange("l (k p) -> (l k) p", p=128))], DEPTH * KC, 'gmix')
        self.load_cols(self.gmlp[:].rearrange("p l k -> p (l k)"), [(0, 128, self.norm_mlp.rearrange("l (k p) -> (l k) p", p=128))], DEPTH * KC, 'gmlp')
        self.load_cols(self.cg[:, 0:2], [(0, 64, self.qnorm_c[:, :]), (64, 128, self.qnorm_c[:, :])], 2, 'cg')
        self.load_cols(self.cg[:, 2:4], [(0, 64, self.knorm_c[:, :]), (64, 128, self.knorm_c[:, :])], 2, 'cg')
        self.load_cols(self.cg[:, 4:6], [(0, 16, self.b_f_c[:, :])], 2, 'cg', cols=16)
        self.ts('dve', self.cg[:, 0:2], self.cg[:, 0:2], SCALE, None, ALU.mult, None, [('c', 'cg')], [('c', 'cg')])
        self.ts('dve', self.cg[0:16, 4:6], self.cg[0:16, 4:6], -1.0, None, ALU.mult, None, [('c', 'cg')], [('c', 'cg')])

    def load_cols(self, dst, parts, r, name, cols=128):
        st, sk = self.obuf()
        for (c0, c1, src) in parts:
            self.dma('sp', st[0:r, c0:c1], src, [], [sk])
        bank = self.bank('tr')
        self.tr(self.ps[bank][0:cols, 0:r], st[0:r, 0:cols], self.ident[0:r, 0:r], [sk, ('c', 'ident')], [('ps', bank)])
        self.copy('dve', dst[0:cols, :] if cols != 128 else dst, self.ps[bank][0:cols, 0:r], [('ps', bank)], [('c', name)])

    def load_x(self, P):
        s = P.s
        if self.x_full:
            src_meta = self.x_full_in[s, 0:NMETA, :]
            src_fr = self.x_full_in[s, NMETA:TP, :]
        else:
            src_meta = self.meta_tokens[:, :]
            src_fr = self.x_prompt[s]
        for half in range(2):
            st, sk = self.obuf()
            self.dma('sp', st[0:NMETA, :], src_meta[:, half * 512:(half + 1) * 512], [], [sk])
            bank = self.bank('tr')
            for q in range(4):
                self.tr(self.ps[bank][:, q * 128:q * 128 + NMETA], st[0:NMETA, q * 128:(q + 1) * 128], self.ident[0:NMETA, 0:NMETA],
                        [sk, ('c', 'ident')], [('ps', bank)])
            self.copy('act' if half else 'dve', self.xT[:, half * 4:(half + 1) * 4, 0:NMETA],
                      self.ps[bank][:, :].rearrange("p (q t) -> p q t", q=4)[:, :, 0:NMETA],
                      [('ps', bank)], [('x', half * 4 + q, 0) for q in range(4)])
        for j in range(SEQ // 128):
            ti = 1 + j // 4
            t0 = NMETA + j * 128
            for half in range(2):
                st, sk = self.obuf()
                self.dma('sp', st[:], src_fr[j * 128:(j + 1) * 128, half * 512:(half + 1) * 512], [], [sk])
                bank = self.bank('tr')
                for q in range(4):
                    self.tr(self.ps[bank][:, q * 128:(q + 1) * 128], st[:, q * 128:(q + 1) * 128], self.ident[:],
                            [sk, ('c', 'ident')], [('ps', bank)])
                self.copy('act' if half else 'dve', self.xT[:, half * 4:(half + 1) * 4, t0:t0 + 128],
                          self.ps[bank][:, :].rearrange("p (q t) -> p q t", q=4),
                          [('ps', bank)], [('x', half * 4 + q, ti) for q in range(4)])

    def store_y(self, P):
        s = P.s
        for j in range(SEQ // 128):
            ti = 1 + j // 4
            t0 = NMETA + j * 128
            for half in range(2):
                st, sk = self.obuf()
                bank = self.bank('tr')
                for q in range(4):
                    kc = half * 4 + q
                    self.tr(self.ps[bank][:, q * 128:(q + 1) * 128], self.xT[:, kc, t0:t0 + 128], self.ident[:],
                            [('x', kc, ti), ('c', 'ident')], [('ps', bank)])
                self.copy('act' if half else 'dve', st[:], self.ps[bank][:, :], [('ps', bank)], [sk])
                self.dma('sp', self.y_prompt[s, j * 128:(j + 1) * 128, half * 512:(half + 1) * 512], st[:], [sk], [('y', s, j, half)])

    def rmsnorm(self, P, gtab, l):
        for ti, (t0, n) in enumerate(P.tiles):
            bank = self.bank('st')
            for kc in range(KC):
                sq, sqk = self.bbuf()
                self.act(sq[:, 0:n], self.xT[:, kc, t0:t0 + n], AF.Square, [('x', kc, ti)], [sqk])
                self.mm(self.ps[bank][:, 0:n], self.ones_bf[:, :], sq[:, 0:n], kc == 0, kc == KC - 1,
                        [sqk, ('c', 'ones')], [('ps', bank)])
            rs, rk = self.fbuf()
            self.act(rs[:, 0:n], self.ps[bank][:, 0:n], AF.Ln, [('ps', bank), ('c', 'eps')], [rk], bias=self.eps_col[:, 0:1], scale=1.0 / D)
            self.act(rs[:, 0:n], rs[:, 0:n], AF.Exp, [rk], [rk], scale=-0.5)
            for kc in range(KC):
                self.stt('dve', self.hT[:, kc, t0:t0 + n], self.xT[:, kc, t0:t0 + n], gtab[:, l, kc:kc + 1], rs[:, 0:n], ALU.mult, ALU.mult,
                         [('x', kc, ti), rk, ('c', 'gmix'), ('c', 'gmlp')], [('h', kc, ti)])

    def wblock(self, k, c, parts):
        return {'k': k, 'c': c, 'parts': parts}

    def make_plan(self, P, sample=False):
        plan = []
        for l in self.layers:
            i = l // 2
            if self.mixers and l % 2 == 1:
                W = self.w_in_c[i]
                r3 = lambda ap: ap.rearrange("(k p) c -> p k c", p=128)
                plan.append(self.wblock(8, 16, [(0, 8, 0, 16, r3(W[:, 3072:3088]))]))
                for hp in [h_ for _ in range(self.n_seq if sample else 1) for h_ in range(8)]:
                    plan.append(self.wblock(8, 384, [(0, 8, 128 * j, 128 * (j + 1), r3(W[:, 1024 * j + 128 * hp:1024 * j + 128 * (hp + 1)])) for j in range(3)]))
                    if hp % 2 == 1:
                        plan.append(self.wblock(2, 1024, [(0, 2, 0, 1024, r3(self.w_out_c[i, 128 * (hp - 1):128 * (hp + 1), :]))]))
            if self.mixers and l % 2 == 0:
                W = self.w_in_ab[i]
                r3 = lambda ap: ap.rearrange("(k p) c -> p k c", p=128)
                plan.append(self.wblock(8, 384, [(0, 8, 0, 128, r3(W[:, 512:640])), (0, 8, 128, 256, r3(W[:, 640:768])),
                                                 (0, 8, 256, 320, r3(W[:, 576:640])), (0, 8, 320, 384, r3(W[:, 512:576]))]))
                for h in range(2):
                    plan.append(self.wblock(8, 256, [(0, 8, 0, 256, r3(W[:, 256 * h:256 * (h + 1)]))]))
                    plan.append(self.wblock(2, 1024, [(0, 2, 0, 1024, r3(self.w_out_ab[i, 256 * h:256 * (h + 1), :]))]))
                for c in range(2):
                    plan.append(self.wblock(8, 272, [(0, 8, 0, 128, r3(W[:, 768 + 128 * c:768 + 128 * (c + 1)])),
                                                     (0, 8, 128, 256, r3(W[:, 1024 + 128 * c:1024 + 128 * (c + 1)])),
                                                     (0, 8, 256, 272, r3(W[:, 2304:2320]))]))
                    plan.append(self.wblock(8, 512, [(0, 8, 0, 256, r3(W[:, 1280 + 256 * c:1280 + 256 * (c + 1)])),
                                                     (0, 8, 256, 512, r3(W[:, 1792 + 256 * c:1792 + 256 * (c + 1)]))]))
                    plan.append(self.wblock(2, 1024, [(0, 2, 0, 1024, r3(self.w_out_ab[i, 512 + 256 * c:512 + 256 * (c + 1), :]))]))
            if self.do_mlp:
                NG = DFF // 512
                upb = lambda g: self.wblock(8, 512, [(0, 4, 0, 512, self.w_up[l, 0:512, g * 512:(g + 1) * 512].rearrange("(k p) c -> p k c", p=128)),
                                                     (4, 8, 0, 512, self.w_up[l, 512:1024, g * 512:(g + 1) * 512].rearrange("(k p) c -> p k c", p=128))])
                dnb = lambda g: self.wblock(4, 1024, [(0, 2, 0, 1024, self.w_down[l, g * 512:g * 512 + 256, :].rearrange("(k p) c -> p k c", p=128)),
                                                      (2, 4, 0, 1024, self.w_down[l, g * 512 + 256:(g + 1) * 512, :].rearrange("(k p) c -> p k c", p=128))])
                plan.append(upb(0))
                for g in range(1, NG):
                    plan.append(upb(g))
                    plan.append(dnb(g - 1))
                plan.append(dnb(NG - 1))
        self.plan = plan
        self.wi = 0
        self.wissued = 0

    def take(self, keep=1):
        limit = min(len(self.plan), self.wi + PF + 1, self.wi - keep + 1 + NW)
        while self.wissued < limit:
            blk = self.plan[self.wissued]
            slot = self.rr('wsl', NW)
            k, c = blk['k'], blk['c']
            view = self.wsl[slot][:, 0:k * c].rearrange("p (k c) -> p k c", k=k)
            blk['slot'] = slot; blk['view'] = view
            blk['keys'] = [('w', slot, pi) for pi in range(len(blk['parts']))]
            for pi, (k0, k1, c0, c1, src) in enumerate(blk['parts']):
                self.dma('pool', view[:, k0:k1, c0:c1], src, [], [('w', slot, pi)])
            self.wissued += 1
        assert self.wissued > self.wi
        blk = self.plan[self.wi]
        self.wi += 1
        return blk['keys'], blk['view']

    def mlp(self, P, l):
        self.rmsnorm(P, self.gmlp, l)
        self.fence()
        tiles = P.tiles
        NG = DFF // 512

        def up(g):
            wk, wv = self.take()
            u = self.ub[g % 2]
            for mc in range(4):
                for ti, (t0, n) in enumerate(tiles):
                    bank = self.bank('pj')
                    for kc in range(KC):
                        self.mm(self.ps[bank][:, 0:n], wv[:, kc, mc * 128:(mc + 1) * 128], self.hT[:, kc, t0:t0 + n], kc == 0, kc == KC - 1,
                                wk + [('h', kc, ti)], [('ps', bank)])
                    tb, tk = self.fbuf()
                    self.act(tb[:, 0:n], self.ps[bank][:, 0:n], AF.Relu, [('ps', bank)], [tk])
                    self.tt('dve', u[:, mc, t0:t0 + n], tb[:, 0:n], tb[:, 0:n], ALU.mult,
                            [tk], [self.ak(('u', g % 2, mc, ti))])

        def down(g):
            wk, wv = self.take()
            u = self.ub[g % 2]
            for dc in range(KC):
                for ti, (t0, n) in enumerate(tiles):
                    bank = self.bank('pj')
                    for kc in range(4):
                        self.mm(self.ps[bank][:, 0:n], wv[:, kc, dc * 128:(dc + 1) * 128], u[:, kc, t0:t0 + n], kc == 0, kc == 3,
                                wk + [('u', g % 2, kc, ti)], [('ps', bank)])
                    self.tt('dve', self.xT[:, dc, t0:t0 + n], self.xT[:, dc, t0:t0 + n], self.ps[bank][:, 0:n], ALU.add,
                            [('ps', bank), ('x', dc, ti)], [('x', dc, ti)])

        up(0)
        for g in range(1, NG):
            up(g)
            down(g - 1)
        down(NG - 1)

    def amul(self, out, in_, m, r, w):
        self.R.op('act', lambda e: e.mul(out=out, in_=in_, mul=m), r, w)

    def fox_layer(self, P, l):
        i = l // 2; s = P.s; tiles = P.tiles
        NT = len(tiles)
        self.rmsnorm(P, self.gmix, l)
        self.fence()
        QA, KA, VA, mixq = self.QA, self.KA, self.VA, self.mixq
        ident = ('c', 'ident')
        self.memset('pool', QA[64:70, :, :], 1.0, [self.ak(('QAa', 0)), self.ak(('QAa', 1))])
        self.memset('pool', KA[64:70, :, :], -1.0, [self.ak(('KAa', 0)), self.ak(('KAa', 1))])
        self.memset('pool', VA[:, :, :, 64:128], 1.0, [self.ak(('VAo',))])

        wk, wv = self.take()
        cp = self.cpos
        for ti, (t0, n) in enumerate(tiles):
            bank = self.bank('st')
            for kc in range(KC):
                self.mm(self.ps[bank][0:16, 0:n], wv[:, kc, 0:16], self.hT[:, kc, t0:t0 + n], kc == 0, kc == KC - 1,
                        wk + [('h', kc, ti)], [('ps', bank)])
            eb, ek = self.fbuf()
            self.act(eb[0:16, 0:n], self.ps[bank][0:16, 0:n], AF.Exp, [('ps', bank), ('c', 'cg')], [ek], bias=self.cg[0:16, 4 + i:5 + i], scale=-1.0)
            self.act(cp[0:16, t0:t0 + n], eb[0:16, 0:n], AF.Ln, [ek, ('c', 'one')], [self.ak(('cpos', ti))], bias=self.one_col[0:16, 0:1], scale=1.0)
        bank = self.bank('tr')
        self.tr(self.ps[bank][0:16, 0:16], cp[0:16, 0:16], self.ident[0:16, 0:16], [('cpos', 0), ident], [('ps', bank)])
        for j in range(16):
            tok0 = NMETA + 128 * j
            self.tr(self.ps[bank][:, 16 * (j + 1):16 * (j + 2)], cp[0:16, tok0:tok0 + 128], self.ident[0:16, 0:16],
                    [('cpos', 1 + j // 4), ident], [('ps', bank)])
        st, sk = self.obuf()
        self.amul(st[0:16, 0:16], self.ps[bank][0:16, 0:16], -1.0, [('ps', bank)], [sk])
        self.amul(st[:, 16:272], self.ps[bank][:, 16:272], -1.0, [('ps', bank)], [sk])
        self.dma('sp', self.o_cf[i, s, 0:16, :], st[0:16, 0:16], [sk], [('ocf', i, s, 0)])
        for hf in range(2):
            self.dma('sp', self.o_cf[i, s, 16 + 1024 * hf:16 + 1024 * (hf + 1), :].rearrange("(j t) h -> t j h", t=128),
                     st[:, 16 + 128 * hf:16 + 128 * (hf + 1)].rearrange("p (j h) -> p j h", h=16), [sk], [('ocf', i, s, 1 + hf)])
        xb = self.xscr
        cs = [xb[32 + 32 * j:48 + 32 * j, :].bitcast(BF16)[:, 0:TP] for j in range(3)]
        onesF = self.onesF
        for ti, (t0, n) in enumerate(tiles):
            c = cp[0:16, t0:t0 + n]
            init = 0.0 if ti == 0 else cp[0:16, t0 - 1:t0]
            rd = [('cpos', ti), ('c', 'onesF')] + ([('cpos', ti - 1)] if ti else [])
            self.R.op('dve', (lambda c=c, n=n, init=init: (lambda e: e.tensor_tensor_scan(out=c, data0=onesF[0:16, 0:n], data1=c, initial=init,
                                                                                          op0=ALU.mult, op1=ALU.add)))(), rd, [self.ak(('cpos', ti))])
        for ti, (t0, n) in enumerate(tiles):
            c = cp[0:16, t0:t0 + n]
            for j in range(2):
                mb, mk = self.bbuf()
                self.copy('dve', mb[0:16, 0:n], c, [self.ak(('cpos', ti))], [mk])
                self.copy('pool', cs[j][:, t0:t0 + n], mb[0:16, 0:n], [mk], [self.ak(('cs', ti))])
                self.tt('dve', c, c, mb[0:16, 0:n], ALU.subtract, [('cpos', ti), mk], [self.ak(('cpos', ti))])
            self.copy('dve', cs[2][:, t0:t0 + n], c, [self.ak(('cpos', ti))], [self.ak(('cs', ti))])
        cskeys = [('cs', ti) for ti in range(NT)]

        for hp in range(8):
            wk, wv = self.take()
            for hl in range(2):
                h = 2 * hp + hl
                for j in range(3):
                    self.dma('sp', KA[64 + j:65 + j, hl, :], cs[j][h:h + 1, :], cskeys, [self.ak(('KAa', hl))])
                    self.dma('sp', QA[67 + j:68 + j, hl, :], cs[j][h:h + 1, :], cskeys, [self.ak(('QAa', hl))])
            units = [(which, ti) for which in range(2) for ti in range(NT)]
            st1 = {}; st2 = {}

            def s1(u):
                which, ti = units[u]
                t0, n = tiles[ti]
                bank = self.bank('pjw')
                for kc in range(KC):
                    self.mm(self.ps[bank][:, 0:n], wv[:, kc, which * 128:(which + 1) * 128], self.hT[:, kc, t0:t0 + n], kc == 0, kc == KC - 1,
                            wk + [('h', kc, ti)], [('ps', bank)])
                sq, sqk = self.bbuf()
                self.act(sq[:, 0:n], self.ps[bank][:, 0:n], AF.Square, [('ps', bank)], [sqk])
                st1[u] = (bank, sq, sqk)

            def s2(u):
                which, ti = units[u]
                t0, n = tiles[ti]
                bank, sq, sqk = st1.pop(u)
                gcol = self.cg[:, which * 2 + i:which * 2 + i + 1]
                dstT = QA if which == 0 else KA
                kname = 'QA' if which == 0 else 'KA'
                b2 = self.bank('st')
                self.mm(self.ps[b2][:, 0:n], self.blockones[:, :], sq[:, 0:n], True, True, [sqk, ('c', 'bo')], [('ps', b2)])
                rs, rk = self.fbuf()
                self.act(rs[:, 0:n], self.ps[b2][:, 0:n], AF.Ln, [('ps', b2), ('c', 'eps')], [rk], bias=self.eps_col[:, 0:1], scale=1.0 / HD)
                self.act(rs[:, 0:n], rs[:, 0:n], AF.Exp, [rk], [rk], scale=-0.5)
                for hl in range(2):
                    p0 = 64 * hl
                    self.stt('dve', dstT[0:64, hl, t0:t0 + n], self.ps[bank][p0:p0 + 64, 0:n], gcol[p0:p0 + 64, :], rs[p0:p0 + 64, 0:n],
                             ALU.mult, ALU.mult, [('ps', bank), rk, ('c', 'cg')], [self.ak((kname, hl, ti))])
                if which == 1:
                    kf, kfk = self.fbuf()
                    self.stt('dve', kf[:, 0:n], self.ps[bank][:, 0:n], gcol, rs[:, 0:n], ALU.mult, ALU.mult,
                             [('ps', bank), rk, ('c', 'cg')], [kfk])
                    st2[u] = (kf, kfk)

            def s3(u):
                if u in st2:
                    which, ti = units[u]
                    t0, n = tiles[ti]
                    kf, kfk = st2.pop(u)
                    self.tm_out(self.o_ck[i, s], hp, t0, n, kf, kfk)

            NU = len(units)
            for u in range(NU + 2):
                if u < NU:
                    s1(u)
                if 0 <= u - 1 < NU:
                    s2(u - 1)
                if 0 <= u - 2 < NU:
                    s3(u - 2)
            for G in range(5):
                bank = self.bank('pjw')
                if G == 0:
                    toks = [(0, NMETA, 0)]
                else:
                    toks = [(NMETA + 128 * (4 * (G - 1) + g), 128, G) for g in range(4)]
                for g, (tok0, nt, ti) in enumerate(toks):
                    for kc in range(KC):
                        self.mm(self.ps[bank][0:nt, g * 128:(g + 1) * 128], self.hT[:, kc, tok0:tok0 + nt], wv[:, kc, 256:384], kc == 0, kc == KC - 1,
                                wk + [('h', kc, ti)], [('ps', bank)])
                st, sk = self.obuf()
                ov = self.o_cv[i, s]
                if G == 0:
                    self.copy('act', st[0:16, 0:128], self.ps[bank][0:16, 0:128], [('ps', bank)], [sk])
                    self.copy('dve', VA[0:16, 0, :, 0:64], self.ps[bank][0:16, 0:128].rearrange("p (h d) -> p h d", h=2), [('ps', bank)], [self.ak(('VA', 0))])
                    self.dma('sp', ov[0:16, 2 * hp:2 * hp + 2, :].rearrange("t h d -> t (h d)"), st[0:16, 0:128], [sk], [('ocv', i, s, hp, 0)])
                else:
                    self.copy('act', st[:], self.ps[bank][:, :], [('ps', bank)], [sk])
                    self.copy('dve', VA[:, 1 + 4 * (G - 1):1 + 4 * G, :, 0:64], self.ps[bank][:, :].rearrange("p (g h d) -> p g h d", g=4, h=2),
                              [('ps', bank)], [self.ak(('VA', G))])
                    tb = NMETA + 512 * (G - 1)
                    self.dma('sp', ov[tb:tb + 512, 2 * hp:2 * hp + 2, :].rearrange("(g t) h d -> t g (h d)", t=128),
                             st[:].rearrange("p (g f) -> p g f", g=4), [sk], [('ocv', i, s, hp, G)])
            for hl in range(2):
                self.fox_attend(P, hp, hl)
            if hp % 2 == 1:
                wk2, wo = self.take()
                for dc in range(KC):
                    for ti, (t0, n) in enumerate(tiles):
                        bank = self.bank('pjw')
                        for kc in range(2):
                            self.mm(self.ps[bank][:, 0:n], wo[:, kc, dc * 128:(dc + 1) * 128], mixq[:, kc, t0:t0 + n], kc == 0, kc == 1,
                                    wk2 + [('mq', kc, ti)], [('ps', bank)])
                        self.tt('dve', self.xT[:, dc, t0:t0 + n], self.xT[:, dc, t0:t0 + n], self.ps[bank][:, 0:n], ALU.add,
                                [('ps', bank), ('x', dc, ti)], [('x', dc, ti)])

    def tm_out(self, odram, hp, t0, n, src, srck):
        bank = self.bank('tr')
        st, sk = self.obuf()
        ident = ('c', 'ident')
        if n < 128:
            self.tr(self.ps[bank][0:n, 0:128], src[:, 0:n], self.ident[:, :], [srck, ident], [('ps', bank)])
            self.copy('act', st[0:n, 0:128], self.ps[bank][0:n, 0:128], [('ps', bank)], [sk])
            self.dma('sp', odram[t0:t0 + n, 2 * hp:2 * hp + 2, :].rearrange("t h d -> t (h d)"), st[0:n, 0:128], [sk], [('otm', id(odram), hp, t0)])
        else:
            for g in range(4):
                self.tr(self.ps[bank][:, g * 128:(g + 1) * 128], src[:, g * 128:(g + 1) * 128], self.ident[:, :], [srck, ident], [('ps', bank)])
            self.copy('act', st[:], self.ps[bank][:, :], [('ps', bank)], [sk])
            self.dma('sp', odram[t0:t0 + 512, 2 * hp:2 * hp + 2, :].rearrange("(g t) h d -> t g (h d)", t=128),
                     st[:].rearrange("p (g f) -> p g f", g=4), [sk], [('otm', id(odram), hp, t0)])

    def fox_attend(self, P, hp, hl):
        QA, KA, VA, mixq = self.QA, self.KA, self.VA, self.mixq
        qts = [(0, NMETA)] + [(NMETA + 512 * i, 512) for i in range(4)]
        for qi, (q0, qn) in enumerate(qts):
            if qi == 0:
                kts = [(0, 0, NMETA, 0, True, 0, 0)]
            else:
                kts = [(0, 0, NMETA, 0, False, 0, 0)]
                for j in range(4 * (qi - 1)):
                    kts.append((1 + j, NMETA + 128 * j, 128, 0, False, 1 + j // 4, 1 + j // 4))
                for d in range(4):
                    j = 4 * (qi - 1) + d
                    kts.append((1 + j, NMETA + 128 * j, 128, 128 * d, True, qi, qi))
            ob = self.bank('oa4')
            nk_t = len(kts)
            pend = []
            LOOK = 4

            def pv(item, idx):
                (vt, tok0, nk, co, diag, pti, vg), pt, ptk = item
                self.mm(self.ps[ob][:, co:qn], VA[0:nk, vt, hl, :], pt[0:nk, co:qn], idx == 0, idx == nk_t - 1,
                        [ptk, ('VA', vg), ('VAo',)], [('ps', ob)])

            for idx, kt in enumerate(kts):
                (vt, tok0, nk, co, diag, pti, vg) = kt
                sbk = self.bank('sc4')
                self.mm(self.ps[sbk][0:nk, co:qn], KA[0:70, hl, tok0:tok0 + nk], QA[0:70, hl, q0 + co:q0 + qn], True, True,
                        [('KA', hl, pti), ('KAa', hl), ('QA', hl, qi), ('QAa', hl)], [('ps', sbk)])
                pt, ptk = self.pbuf()
                self.act(pt[0:nk, co:qn], self.ps[sbk][0:nk, co:qn], AF.Exp, [('ps', sbk)], [ptk])
                if diag:
                    w = min(128, qn - co)
                    self.tt('dve', pt[0:nk, co:co + w], pt[0:nk, co:co + w], self.trimask[0:nk, 0:w], ALU.mult, [ptk, ('c', 'tri')], [ptk])
                pend.append(((kt, pt, ptk), idx))
                if len(pend) > LOOK:
                    it, ix = pend.pop(0)
                    pv(it, ix)
            for it, ix in pend:
                pv(it, ix)
            rc, rck = self.fbuf()
            self.recip(rc[0:64, 0:qn], self.ps[ob][64:128, 0:qn], [('ps', ob)], [rck])
            self.tt('dve', mixq[64 * hl:64 * hl + 64, hp % 2, q0:q0 + qn], self.ps[ob][0:64, 0:qn], rc[0:64, 0:qn], ALU.mult,
                    [('ps', ob), rck], [self.ak(('mq', hp % 2, qi))])

    def xbuf(self):
        i = self.rr('xscr', 4)
        return self.xscr[:, i * 512:(i + 1) * 512], self.ak(('xs', i))

    def ab_consts(self):
        pcol = self.pcol; invc = self.invc
        self.R.op('pool', lambda e: e.iota(pcol[:, 0:1], pattern=[[0, 1]], base=0, channel_multiplier=1, allow_small_or_imprecise_dtypes=True), [], [('c', 'pcol')])
        ki = self.kint
        self.ts('dve', pcol[:, 1:2], pcol[:, 0:1], -15.5, 1.0 / 32, ALU.add, ALU.mult, [('c', 'pcol')], [('c', 'pcol1')])
        self.copy('dve', ki[:, 0:1], pcol[:, 1:2], [('c', 'pcol1')], [('ost', 3)])
        self.copy('dve', pcol[:, 2:3], ki[:, 0:1], [('ost', 3)], [('c', 'pcol2')])
        self.stt('dve', pcol[:, 3:4], pcol[:, 2:3], -32.0, pcol[:, 0:1], ALU.mult, ALU.add, [('c', 'pcol2'), ('c', 'pcol')], [('c', 'pcol3')])
        self.act(invc[:, 0:1], pcol[:, 3:4], AF.Exp, [('c', 'pcol3')], [('c', 'inv')], scale=-math.log(10000.0) / 32)
        for ti in range(5):
            t0, n = (0, NMETA) if ti == 0 else (NMETA + 512 * (ti - 1), 512)
            pos, pk = self.fbuf()
            self.R.op('pool', (lambda pos=pos, t0=t0, n=n: (lambda e: e.iota(pos[:, 0:n], pattern=[[1, n]], base=t0, channel_multiplier=0,
                                                                             allow_small_or_imprecise_dtypes=True)))(), [], [pk])
            ang, ak_ = self.fbuf()
            self.ts('dve', ang[:, 0:n], pos[:, 0:n], invc[:, 0:1], None, ALU.mult, None, [pk, ('c', 'inv')], [ak_])
            for which in (1, 0):
                tq, tk = self.fbuf()
                off = 0.0 if which == 1 else math.pi / 2
                self.ts('dve', tq[:, 0:n], ang[:, 0:n], off, 1.0 / (2 * math.pi), ALU.add, ALU.mult, [ak_], [tk])
                self.copy('dve', ki[:, 0:n], tq[:, 0:n], [tk], [('ost', 3)])
                self.copy('dve', tq[:, 0:n], ki[:, 0:n], [('ost', 3)], [tk])
                self.stt('dve', tq[:, 0:n], tq[:, 0:n], -2 * math.pi, ang[:, 0:n], ALU.mult, ALU.add, [tk, ak_], [tk])
                if which == 0:
                    self.ts('dve', tq[:, 0:n], tq[:, 0:n], math.pi / 2, None, ALU.add, None, [tk], [tk])
                self.ts('dve', tq[:, 0:n], tq[:, 0:n], 3.14159, -3.14159, ALU.min, ALU.max, [tk], [tk])
                self.act(self.rope[:, which, t0:t0 + n], tq[:, 0:n], AF.Sin, [tk], [('c', 'rope', ti)])
        pos, pk = self.fbuf()
        self.R.op('pool', lambda e: e.iota(pos[:, 0:32], pattern=[[1, 32]], base=NMETA + 2048, channel_multiplier=0, allow_small_or_imprecise_dtypes=True), [], [pk])
        ang, ak_ = self.fbuf()
        self.ts('dve', ang[:, 0:32], pos[:, 0:32], invc[:, 0:1], None, ALU.mult, None, [pk, ('c', 'inv')], [ak_])
        for which in (1, 0):
            tq, tk = self.fbuf()
            off = 0.0 if which == 1 else math.pi / 2
            self.ts('dve', tq[:, 0:32], ang[:, 0:32], off, 1.0 / (2 * math.pi), ALU.add, ALU.mult, [ak_], [tk])
            self.copy('dve', ki[:, 0:32], tq[:, 0:32], [tk], [('ost', 3)])
            self.copy('dve', tq[:, 0:32], ki[:, 0:32], [('ost', 3)], [tk])
            self.stt('dve', tq[:, 0:32], tq[:, 0:32], -2 * math.pi, ang[:, 0:32], ALU.mult, ALU.add, [tk, ak_], [tk])
            if which == 0:
                self.ts('dve', tq[:, 0:32], tq[:, 0:32], math.pi / 2, None, ALU.add, None, [tk], [tk])
            self.ts('dve', tq[:, 0:32], tq[:, 0:32], 3.14159, -3.14159, ALU.min, ALU.max, [tk], [tk])
            for rep in range(2):
                self.act(self.ropeS[:, which, 32 * rep:32 * (rep + 1)], tq[:, 0:32], AF.Sin, [tk], [('c', 'ropeS')])
        t3 = self.trimask3
        self.memset('pool', t3[:], 1.0, [('c', 'tri3')])
        self.R.op('pool', lambda e: e.affine_select(out=t3[:], in_=t3[:], pattern=[[1, 32]], compare_op=ALU.is_ge, fill=0.0, base=32,
                                                   channel_multiplier=-1), [('c', 'tri3')], [('c', 'tri3')])
        self.copy('pool', t3[0:32, :], self.trimask[0:32, 0:32], [('c', 'tri')], [('c', 'tri3')])
        rm = self.Rm
        self.memset('pool', rm[:], 0.0, [('c', 'rm')])
        self.R.op('pool', lambda e: e.affine_select(out=rm[:], in_=rm[:], pattern=[[-1, 128]], compare_op=ALU.not_equal, fill=-1.0, base=-32,
                                                   channel_multiplier=1), [('c', 'rm')], [('c', 'rm')])
        self.R.op('pool', lambda e: e.affine_select(out=rm[:], in_=rm[:], pattern=[[-1, 128]], compare_op=ALU.not_equal, fill=1.0, base=32,
                                                   channel_multiplier=1), [('c', 'rm')], [('c', 'rm')])
        self.tt('pool', rm[:], rm[:], self.blockones[:], ALU.mult, [('c', 'rm'), ('c', 'bo')], [('c', 'rm')])
        t2 = self.trimask2
        self.memset('pool', t2[:], 1.0, [('c', 'tri2')])
        self.R.op('pool', lambda e: e.affine_select(out=t2[:], in_=t2[:], pattern=[[1, 64]], compare_op=ALU.is_ge, fill=0.0, base=64,
                                                   channel_multiplier=-1), [('c', 'tri2')], [('c', 'tri2')])
        self.copy('pool', t2[0:64, :], self.trimask[0:64, 0:64], [('c', 'tri')], [('c', 'tri2')])
        sm = self.segmask
        self.memset('pool', sm[:], 1.0, [('c', 'seg')])
        self.memset('pool', sm[:].rearrange("p (a b) -> p a b", b=64)[:, :, 0:1], 0.0, [('c', 'seg')])
        self.memset('pool', self.vsink[0:1, 0:64], 0.0, [('c', 'vsink')])
        self.memset('pool', self.vsink[0:1, 64:128], 1.0, [('c', 'vsink')])
        self.load_cols(self.cga[:, 0:2], [(0, 64, self.qnorm_a[:, :]), (64, 128, self.qnorm_a[:, :])], 2, 'cga')
        self.load_cols(self.cga[:, 2:4], [(0, 64, self.knorm_a[:, :]), (64, 128, self.knorm_a[:, :])], 2, 'cga')
        self.ts('dve', self.cga[:, 0:2], self.cga[:, 0:2], SCALE, None, ALU.mult, None, [('c', 'cga')], [('c', 'cga')])
        self.load_cols(self.cga[:, 4:8], [(0, 128, self.b_gate_b.rearrange("i (c p) -> (i c) p", p=128))], 4, 'cga')
        self.ts('dve', self.cga[:, 4:8], self.cga[:, 4:8], -1.0, None, ALU.mult, None, [('c', 'cga')], [('c', 'cga')])
        self.load_cols(self.cga[:, 8:10], [(0, 128, self.onorm_b[:, :])], 2, 'cga')
        self.dma('sp', self.sink16[0:1, :], self.sink_a.rearrange("(o i) h -> o (i h)", o=1), [], [('c', 'sink16')])
        self.dma('pool', self.wgb[:, :, :], self.w_gate_b.rearrange("i r f -> r i f"), [], [('c', 'wgb')])

    def ab_layer(self, P, l):
        i = l // 2; s = P.s; tiles = P.tiles
        NT = len(tiles)
        self.rmsnorm(P, self.gmix, l)
        self.fence()
        QA, KA, VA, mixq = self.QA, self.KA, self.VA, self.mixq
        ident = ('c', 'ident')
        self.memset('pool', VA[:, :, :, 64:128], 1.0, [self.ak(('VAo',))])
        for h in range(2):
            for p in range(2):
                for c in range(2):
                    idx = 8 * i + 4 * h + 2 * c + p
                    g = p * 2 + c
                    self.act(self.sinkrow[0:1, h, g * 128:(g + 1) * 128], self.onesF[0:1, 0:128], AF.Exp, [('c', 'onesF'), ('c', 'sink16')],
                             [('sinkrow', h)], bias=self.sink16[0:1, idx:idx + 1], scale=0.0)
                    self.act(self.sinkrow_m[0:1, h, g * 16:(g + 1) * 16], self.onesF[0:1, 0:16], AF.Exp, [('c', 'onesF'), ('c', 'sink16')],
                             [('sinkrow', h)], bias=self.sink16[0:1, idx:idx + 1], scale=0.0)
        wk, wv = self.take()
        gk = self.cga[:, 2 + i:3 + i]
        kunits = []
        for a in range(2):
            for ti, (t0, n) in enumerate(tiles):
                outf = True if (a == 0 and ti == NT - 1) else None
                kunits.append(dict(wk=wk, wv=wv[:, :, (0 if a == 0 else 256):(128 if a == 0 else 384)], ti=ti, t0=t0, n=n, gcol=gk,
                                   dst=KA[:, a, t0:t0 + n], dkey=self.ak(('KA', a, ti)), outf=outf, ropetab=None))
        self.qk_rope_many(P, kunits, i)
        for G in range(5):
            bank = self.bank('pj')
            if G == 0:
                toks = [(0, NMETA, 0)]
            else:
                toks = [(NMETA + 128 * (4 * (G - 1) + g), 128, G) for g in range(4)]
            for g, (tok0, nt, ti) in enumerate(toks):
                for kc in range(KC):
                    self.mm(self.ps[bank][0:nt, g * 128:(g + 1) * 128], self.hT[:, kc, tok0:tok0 + nt], wv[:, kc, 128:256], kc == 0, kc == KC - 1,
                            wk + [('h', kc, ti)], [('ps', bank)])
            if G == 0:
                self.copy('dve', VA[0:16, 0, :, 0:64], self.ps[bank][0:16, 0:128].rearrange("p (h d) -> p h d", h=2), [('ps', bank)], [self.ak(('VA', 0))])
            else:
                self.copy('dve', VA[:, 1 + 4 * (G - 1):1 + 4 * G, :, 0:64], self.ps[bank][:, :].rearrange("p (g h d) -> p g h d", g=4, h=2),
                          [('ps', bank)], [self.ak(('VA', G))])
                if G == 4:
                    st, sk = self.obuf()
                    self.copy('act', st[:, 0:128], self.ps[bank][:, 384:512], [('ps', bank)], [sk])
                    self.dma('sp', self.o_av[i, s].rearrange("t h d -> t (h d)"), st[:, 0:128], [sk], [('oav', i, s)])
        gq = self.cga[:, i:i + 1]
        for h in range(2):
            wkq, wq = self.take()
            qunits = []
            for c in range(2):
                for ti, (t0, n) in enumerate(tiles):
                    qunits.append(dict(wk=wkq, wv=wq[:, :, 128 * c:128 * (c + 1)], ti=ti, t0=t0, n=n, gcol=gq, dst=QA[:, c, t0:t0 + n],
                                       dkey=self.ak(('QA', c, ti)), outf=None, ropetab=None))
            self.qk_rope_many(P, qunits, i)
            self.swa_attend(P, h)
            self.out_quad(P, self.take(), tiles)
        self.fence()
        for c in range(2):
            self.gla_pair(P, l, c)
            self.out_quad(P, self.take(), tiles)

    def out_quad(self, P, blk, tiles):
        wk2, wo = blk
        if self.dbg:
            qi_ = getattr(self, '_dbgq', 0)
            self._dbgq = qi_ + 1
            if qi_ < 4:
                self.dma('sp', self.dbg_mix[qi_], self.mixq[:, :, :], [('mq', kc, ti) for kc in range(2) for ti in range(len(tiles))], [('dbgmix', qi_)])
        for dc in range(KC):
            for ti, (t0, n) in enumerate(tiles):
                bank = self.bank('pjw')
                for kc in range(2):
                    self.mm(self.ps[bank][:, 0:n], wo[:, kc, dc * 128:(dc + 1) * 128], self.mixq[:, kc, t0:t0 + n], kc == 0, kc == 1,
                            wk2 + [('mq', kc, ti)], [('ps', bank)])
                self.tt('dve', self.xT[:, dc, t0:t0 + n], self.xT[:, dc, t0:t0 + n], self.ps[bank][:, 0:n], ALU.add,
                        [('ps', bank), ('x', dc, ti)], [('x', dc, ti)])

    def qk_rope_many(self, P, units, i):
        sa = {}; sb_ = {}

        def A(u):
            wk, wv, ti, t0, n = u['wk'], u['wv'], u['ti'], u['t0'], u['n']
            bank = self.bank('pjw')
            for kc in range(KC):
                self.mm(self.ps[bank][:, 0:n], wv[:, kc, :], self.hT[:, kc, t0:t0 + n], kc == 0, kc == KC - 1, wk + [('h', kc, ti)], [('ps', bank)])
            sq, sqk = self.bbuf()
            self.act(sq[:, 0:n], self.ps[bank][:, 0:n], AF.Square, [('ps', bank)], [sqk])
            sa[id(u)] = (bank, sq, sqk)

        def B(u):
            n = u['n']
            bank, sq, sqk = sa.pop(id(u))
            b2 = self.bank('st')
            self.mm(self.ps[b2][:, 0:n], self.blockones[:, :], sq[:, 0:n], True, True, [sqk, ('c', 'bo')], [('ps', b2)])
            rs, rk = self.fbuf()
            self.act(rs[:, 0:n], self.ps[b2][:, 0:n], AF.Ln, [('ps', b2), ('c', 'eps')], [rk], bias=self.eps_col[:, 0:1], scale=1.0 / HD)
            self.act(rs[:, 0:n], rs[:, 0:n], AF.Exp, [rk], [rk], scale=-0.5)
            qn, qk_ = self.xbuf()
            self.stt('dve', qn[:, 0:n], self.ps[bank][:, 0:n], u['gcol'], rs[:, 0:n], ALU.mult, ALU.mult, [('ps', bank), rk, ('c', 'cga')], [qk_])
            qb, qbk = self.bbuf()
            self.copy('act', qb[:, 0:n], qn[:, 0:n], [qk_], [qbk])
            sb_[id(u)] = (qn, qk_, qb, qbk)

        def C(u):
            ti, t0, n, dst, dkey, outf, ropetab = u['ti'], u['t0'], u['n'], u['dst'], u['dkey'], u['outf'], u['ropetab']
            qn, qk_, qb, qbk = sb_.pop(id(u))
            b3 = self.bank('st')
            self.mm(self.ps[b3][:, 0:n], self.Rm[:, :], qb[:, 0:n], True, True, [qbk, ('c', 'rm')], [('ps', b3)])
            if ropetab is None:
                rcos, rsin, rkeys = self.rope[:, 0, t0:t0 + n], self.rope[:, 1, t0:t0 + n], [('c', 'rope', ti)]
            else:
                rcos, rsin, rkeys = ropetab
            self.tt('pool', qn[:, 0:n], qn[:, 0:n], rcos, ALU.mult, [qk_] + rkeys, [qk_])
            rb_, rbk = self.fbuf()
            self.tt('dve', rb_[:, 0:n], self.ps[b3][:, 0:n], rsin, ALU.mult, [('ps', b3)] + rkeys, [rbk])
            self.tt('dve', dst, qn[:, 0:n], rb_[:, 0:n], ALU.add, [qk_, rbk], [dkey])
            if outf == 'sample':
                kf, kfk = self.fbuf()
                self.tt('dve', kf[:, 0:n], qn[:, 0:n], rb_[:, 0:n], ALU.add, [qk_, rbk], [kfk])
                bank2 = self.bank('tr')
                self.tr(self.ps[bank2][0:n, 0:128], kf[:, 0:n], self.ident[:, :], [kfk, ('c', 'ident')], [('ps', bank2)])
                st, sk = self.obuf()
                self.copy('act', st[0:n, 0:128], self.ps[bank2][0:n, 0:128], [('ps', bank2)], [sk])
                for b in range(self.n_seq):
                    self.dma('sp', self.o_aks[i, b, 96:128].rearrange("t h d -> t (h d)"), st[32 * b:32 * b + 32, 0:128], [sk], [('oaks', i, b)])
            elif outf:
                kf, kfk = self.fbuf()
                self.tt('dve', kf[:, 0:128], qn[:, n - 128:n], rb_[:, n - 128:n], ALU.add, [qk_, rbk], [kfk])
                bank2 = self.bank('tr')
                self.tr(self.ps[bank2][:, 0:128], kf[:, 0:128], self.ident[:, :], [kfk, ('c', 'ident')], [('ps', bank2)])
                st, sk = self.obuf()
                self.copy('act', st[:, 0:128], self.ps[bank2][:, 0:128], [('ps', bank2)], [sk])
                self.dma('sp', self.o_ak[i, P.s].rearrange("t h d -> t (h d)"), st[:, 0:128], [sk], [('oak', i, P.s)])

        NU = len(units)
        for j in range(NU + 2):
            if j < NU:
                A(units[j])
            if 0 <= j - 1 < NU:
                B(units[j - 1])
            if 0 <= j - 2 < NU:
                C(units[j - 2])

    def qk_rope(self, P, wk, wv, ti, t0, n, gcol, dst, dkey, outf, i, ropetab=None):
        self.qk_rope_many(P, [dict(wk=wk, wv=wv, ti=ti, t0=t0, n=n, gcol=gcol, dst=dst, dkey=dkey, outf=outf, ropetab=ropetab)], i)

    def swa_attend(self, P, h):
        QA, KA, VA, mixq = self.QA, self.KA, self.VA, self.mixq
        tl = []
        for qj in range(-1, 16):
            if qj < 0:
                tl.append(dict(q0=0, nq=NMETA, qti=0, kts=[(0, 0, NMETA, None, 0)], meta=True))
            else:
                q0, nq, qti = NMETA + 128 * qj, 128, 1 + qj // 4
                if qj == 0:
                    kts = [(0, 0, NMETA, None, 0), (1, NMETA, 128, (64, 0), 1)]
                else:
                    kts = [(qj, NMETA + 128 * (qj - 1), 128, (0, 64), 1 + (qj - 1) // 4), (1 + qj, NMETA + 128 * qj, 128, (64, 0), 1 + qj // 4)]
                tl.append(dict(q0=q0, nq=nq, qti=qti, kts=kts, meta=False))

        def X(t):
            q0, nq, qti = t['q0'], t['nq'], t['qti']
            W = 4 * nq
            pts = []
            for idx, (vt, tok0, nk, zq, kti) in enumerate(t['kts']):
                pt, ptk = self.pbuf()
                for p in range(2):
                    a = h ^ p
                    sbk = self.bank('sc4')
                    self.mm(self.ps[sbk][0:nk, 0:2 * nq], KA[64 * p:64 * p + 64, a, tok0:tok0 + nk], QA[64 * p:64 * p + 64, :, q0:q0 + nq],
                            True, True, [('KA', a, kti), ('QA', 0, qti), ('QA', 1, qti)], [('ps', sbk)], tile_position=(64 * p, 0))
                    self.act(pt[0:nk, p * 2 * nq:(p + 1) * 2 * nq], self.ps[sbk][0:nk, 0:2 * nq], AF.Exp, [('ps', sbk)], [ptk])
                if zq is not None:
                    r0, c0 = zq
                    self.memset('pool', pt[r0:r0 + 64, 0:W].rearrange("p (g q) -> p g q", q=nq)[:, :, c0:c0 + 64], 0.0, [ptk])
                pts.append((pt, ptk))
            t['pts'] = pts

        def Y(t):
            q0, nq, qti = t['q0'], t['nq'], t['qti']
            W = 4 * nq
            ob = self.bank('oa')
            for idx, (vt, tok0, nk, zq, kti) in enumerate(t['kts']):
                pt, ptk = t['pts'][idx]
                self.mm(self.ps[ob][:, 0:W], VA[0:nk, vt, h, :], pt[0:nk, 0:W], idx == 0, False, [ptk, ('VA', kti), ('VAo',)], [('ps', ob)])
            srow = self.sinkrow_m[0:1, h, 0:W] if t['meta'] else self.sinkrow[0:1, h, 0:W]
            self.mm(self.ps[ob][:, 0:W], self.vsink[0:1, 0:128], srow, False, True, [('sinkrow', h), ('c', 'vsink')], [('ps', ob)])
            rc, rck = self.fbuf()
            self.recip(rc[0:64, 0:W], self.ps[ob][64:128, 0:W], [('ps', ob)], [rck], eng='act')
            for p in range(2):
                self.tt('dve', mixq[64 * p:64 * p + 64, :, q0:q0 + nq], self.ps[ob][0:64, p * 2 * nq:(p + 1) * 2 * nq].rearrange("p (c q) -> p c q", c=2),
                        rc[0:64, p * 2 * nq:(p + 1) * 2 * nq].rearrange("p (c q) -> p c q", c=2), ALU.mult,
                        [('ps', ob), rck], [self.ak(('mq', 0, qti)), self.ak(('mq', 1, qti))])

        X(tl[0])
        for ix, t in enumerate(tl):
            if ix + 1 < len(tl):
                X(tl[ix + 1])
            Y(t)

    def gla_pair(self, P, l, c):
        i = l // 2; s = P.s; tiles = P.tiles
        ar = self.arena
        QG = ar[:, 0:TP]; KG = ar[:, TP:2 * TP]
        KT = ar[:, 2 * TP:2 * TP + 17 * 128].rearrange("p (j f) -> p j f", j=17)
        o = 2 * TP + 17 * 128
        VB = ar[:, o:o + 17 * 256].rearrange("p (j f) -> p j f", j=17)
        mixq = self.mixq
        EL = self.EL; Sf = self.Sf; Sb = self.Sb
        wk1, w1 = self.take()
        wk2, w2 = self.take(keep=2)
        negbg = self.cga[:, 4 + 2 * i + c:5 + 2 * i + c]
        onc = self.cga[:, 8 + i:9 + i]
        self.memset('pool', Sf[:, :], 0.0, [('Sf',)])
        self.memset('pool', Sb[:, :], 0.0, [('Sb',)])
        pa = {}; pb = {}

        def A1(ti):
            t0, n = tiles[ti]
            b_g = self.bank('st')
            for kc in range(KC):
                self.mm(self.ps[b_g][0:16, 0:n], w1[:, kc, 256:272], self.hT[:, kc, t0:t0 + n], kc == 0, kc == KC - 1, wk1 + [('h', kc, ti)], [('ps', b_g)])
            gl, glk = self.bbuf()
            self.copy('dve', gl[0:16, 0:n], self.ps[b_g][0:16, 0:n], [('ps', b_g)], [glk])
            b_q = self.bank('pjw')
            for kc in range(KC):
                self.mm(self.ps[b_q][:, 0:n], w1[:, kc, 0:128], self.hT[:, kc, t0:t0 + n], kc == 0, kc == KC - 1, wk1 + [('h', kc, ti)], [('ps', b_q)])
            b_k = self.bank('pjw')
            for kc in range(KC):
                self.mm(self.ps[b_k][:, 0:n], w1[:, kc, 128:256], self.hT[:, kc, t0:t0 + n], kc == 0, kc == KC - 1, wk1 + [('h', kc, ti)], [('ps', b_k)])
            pa[ti] = (gl, glk, b_q, b_k)

        def B1(ti):
            t0, n = tiles[ti]
            cw = 64 if n >= 64 else n
            nchk = n // cw
            gl, glk, b_q, b_k = pa.pop(ti)
            b_z = self.bank('st')
            self.mm(self.ps[b_z][:, 0:n], self.wgb[0:16, i, 128 * c:128 * (c + 1)], gl[0:16, 0:n], True, True, [glk, ('c', 'wgb')], [('ps', b_z)])
            cum, cumk = self.xbuf()
            self.act(cum[:, 0:n], self.ps[b_z][:, 0:n], AF.Exp, [('ps', b_z), ('c', 'cga')], [cumk], bias=negbg, scale=-1.0)
            self.act(cum[:, 0:n], cum[:, 0:n], AF.Ln, [cumk, ('c', 'one')], [cumk], bias=self.one_col[:, 0:1], scale=1.0)
            self.scan(cum[:, 0:n], self.segmask[:, 0:n], cum[:, 0:n], 0.0, [cumk, ('c', 'seg')], [cumk])
            eb, ebk = self.xbuf()
            self.act(eb[:, 0:n], cum[:, 0:n], AF.Exp, [cumk], [ebk], scale=-1.0 / 16)
            c0 = 0 if ti == 0 else 1 + 8 * (ti - 1)
            self.copy('dve', EL[:, c0:c0 + nchk], eb[:, 0:n].rearrange("p (a b) -> p a b", b=cw)[:, :, cw - 1], [ebk], [('EL', ti)])
            self.stt('dve', QG[:, t0:t0 + n], self.ps[b_q][:, 0:n], SCALE, eb[:, 0:n], ALU.mult, ALU.mult, [('ps', b_q), ebk], [self.ak(('QG', ti))])
            en, enk = self.xbuf()
            self.act(en[:, 0:n], cum[:, 0:n], AF.Exp, [cumk], [enk], scale=1.0 / 16)
            self.tt('dve', KG[:, t0:t0 + n], self.ps[b_k][:, 0:n], en[:, 0:n], ALU.mult, [('ps', b_k), enk], [self.ak(('KG', ti))])
            c3 = cum[:, 0:n].rearrange("p (a b) -> p a b", b=cw)
            self.tt('dve', c3, c3, c3[:, :, cw - 1:cw].to_broadcast([128, nchk, cw]), ALU.subtract, [cumk], [cumk])
            self.act(en[:, 0:n], cum[:, 0:n], AF.Exp, [cumk, enk], [enk], scale=1.0 / 16)
            kh, khk = self.fbuf()
            self.tt('dve', kh[:, 0:n], self.ps[b_k][:, 0:n], en[:, 0:n], ALU.mult, [('ps', b_k), enk], [khk])
            pb[ti] = (kh, khk)

        def C1(ti):
            t0, n = tiles[ti]
            kh, khk = pb.pop(ti)
            b_t = self.bank('tr')
            if ti == 0:
                self.tr(self.ps[b_t][0:n, 0:128], kh[:, 0:n], self.ident[:, :], [khk, ('c', 'ident')], [('ps', b_t)])
                self.copy('act', KT[0:n, 0, :], self.ps[b_t][0:n, 0:128], [('ps', b_t)], [self.ak(('KT', 0))])
            else:
                for g in range(4):
                    self.tr(self.ps[b_t][:, g * 128:(g + 1) * 128], kh[:, g * 128:(g + 1) * 128], self.ident[:, :], [khk, ('c', 'ident')], [('ps', b_t)])
                self.copy('act', KT[:, 1 + 4 * (ti - 1):1 + 4 * ti, :], self.ps[b_t][:, :].rearrange("p (g f) -> p g f", g=4), [('ps', b_t)], [self.ak(('KT', ti))])

        NTl = len(tiles)
        for j in range(NTl + 2):
            if j < NTl:
                A1(j)
            if 0 <= j - 1 < NTl:
                B1(j - 1)
            if 0 <= j - 2 < NTl:
                C1(j - 2)
        for G in range(9):
            bank = self.bank('pjw')
            if G == 0:
                toks = [(0, NMETA, 0, 0)]
            else:
                toks = [(NMETA + 128 * (2 * (G - 1) + g), 128, 1 + (2 * (G - 1) + g) // 4, 1 + 2 * (G - 1) + g) for g in range(2)]
            for g, (tok0, nt, ti, vt) in enumerate(toks):
                for kc in range(KC):
                    self.mm(self.ps[bank][0:nt, g * 256:(g + 1) * 256], self.hT[:, kc, tok0:tok0 + nt], w2[:, kc, 0:256], kc == 0, kc == KC - 1,
                            wk2 + [('h', kc, ti)], [('ps', bank)])
            if G == 0:
                self.copy('act', VB[0:NMETA, 0, :], self.ps[bank][0:NMETA, 0:256], [('ps', bank)], [self.ak(('VB', 0))])
            else:
                self.copy('act' if G % 2 else 'dve', VB[:, 1 + 2 * (G - 1):1 + 2 * G, :], self.ps[bank][:, :].rearrange("p (g f) -> p g f", g=2),
                          [('ps', bank)], [self.ak(('VB', G))])
        roles = self.bank_roles
        at_bs = roles['sc']
        chunks = []
        for ti, (t0, n) in enumerate(tiles):
            for q in range(1 if ti == 0 else 8):
                if ti == 0:
                    chunks.append(dict(ti=0, q=0, nidx=0, tok0=0, nt=NMETA, vt=0, p0=0, vg=0))
                else:
                    fj = 4 * (ti - 1) + q // 2
                    chunks.append(dict(ti=ti, q=q, nidx=1 + 8 * (ti - 1) + q, tok0=t0 + 64 * q, nt=64, vt=1 + fj, p0=64 * (q % 2), vg=1 + fj // 2))

        def stage_a(ch):
            tok0, nt, p0, vt, ti_, vg = ch['tok0'], ch['nt'], ch['p0'], ch['vt'], ch['ti'], ch['vg']
            ai = self.rr('amb', 3)
            am, amk = self.amb[ai], ('amb', ai)
            for hl in range(2):
                at_b = at_bs[hl]
                self.mm(self.ps[at_b][p0:p0 + nt, 0:nt], KG[64 * hl:64 * hl + 64, tok0:tok0 + nt], QG[64 * hl:64 * hl + 64, tok0:tok0 + nt],
                        True, True, [('KG', ti_), ('QG', ti_)], [('ps', at_b)], tile_position=(64 * hl, p0))
                self.tt('dve', am[p0:p0 + nt, hl * 64:hl * 64 + nt], self.ps[at_b][p0:p0 + nt, 0:nt], self.trimask2[p0:p0 + nt, 0:nt], ALU.mult,
                        [('ps', at_b), ('c', 'tri2')], [amk])
            su_b = self.bank('su')
            for hl in range(2):
                self.mm(self.ps[su_b][64 * hl:64 * hl + 64, 0:128], KT[p0:p0 + nt, vt, 64 * hl:64 * hl + 64], VB[p0:p0 + nt, vt, 128 * hl:128 * hl + 128],
                        True, True, [('KT', ti_), ('VB', vg)], [('ps', su_b)], tile_position=(p0, 64 * hl))
            ch['am'] = (am, amk); ch['su'] = su_b

        stage_a(chunks[0])
        obs = None
        for ci, ch in enumerate(chunks):
            ti, q, nidx, tok0, nt, vt, p0, vg = ch['ti'], ch['q'], ch['nidx'], ch['tok0'], ch['nt'], ch['vt'], ch['p0'], ch['vg']
            t0, n = tiles[ti]
            if q == 0:
                obs = [self.bank('oa') for _ in range(1 if ti == 0 else 2)]
            if ci + 1 < len(chunks):
                stage_a(chunks[ci + 1])
            am, amk = ch['am']; su_b = ch['su']
            ob = obs[q // 4]
            col0 = (q % 4) * 128
            for hl in range(2):
                first = (nidx == 0)
                self.mm(self.ps[ob][:, col0 + hl * 64:col0 + hl * 64 + nt], VB[p0:p0 + nt, vt, 128 * hl:128 * hl + 128],
                        am[p0:p0 + nt, hl * 64:hl * 64 + nt], True, first, [amk, ('VB', vg)], [('ps', ob)], tile_position=(p0, 0))
                if not first:
                    self.mm(self.ps[ob][:, col0 + hl * 64:col0 + hl * 64 + nt], Sb[64 * hl:64 * hl + 64, :], QG[64 * hl:64 * hl + 64, tok0:tok0 + nt],
                            False, True, [('Sb',), ('QG', ti)], [('ps', ob)], tile_position=(64 * hl, 0))
            self.stt('dve', Sf[:, :], Sf[:, :], EL[:, nidx:nidx + 1], self.ps[su_b][:, 0:128], ALU.mult, ALU.add, [('Sf',), ('EL', ti), ('ps', su_b)], [('Sf',)])
            self.copy('act', Sb[:, :], Sf[:, :], [('Sf',)], [('Sb',)])
            if q != (0 if ti == 0 else 7):
                continue
            fin = {}
            for hl in range(2):
                of, ofk = self.xbuf()
                if ti == 0:
                    self.copy('act', of[:, 0:n], self.ps[obs[0]][:, hl * 64:hl * 64 + n], [('ps', obs[0])], [ofk])
                else:
                    for bq in range(2):
                        self.copy('act', of[:, bq * 256:(bq + 1) * 256].rearrange("p (a t) -> p a t", a=4),
                                  self.ps[obs[bq]][:, :].rearrange("p (a h t) -> p a h t", a=4, h=2)[:, :, hl, :], [('ps', obs[bq])], [ofk])
                sq, sqk = self.bbuf()
                self.act(sq[:, 0:n], of[:, 0:n], AF.Square, [ofk], [sqk])
                fin[hl] = [of, ofk, sq, sqk]
            for hl in range(2):
                b_r = self.bank('pj')
                for kc in range(KC):
                    self.mm(self.ps[b_r][:, 0:n], w2[:, kc, 256 + 128 * hl:256 + 128 * (hl + 1)], self.hT[:, kc, t0:t0 + n], kc == 0, kc == KC - 1,
                            wk2 + [('h', kc, ti)], [('ps', b_r)])
                sg, sgk = self.fbuf()
                self.act(sg[:, 0:n], self.ps[b_r][:, 0:n], AF.Exp, [('ps', b_r)], [sgk], scale=-1.0)
                self.act(sg[:, 0:n], sg[:, 0:n], AF.Ln, [sgk, ('c', 'one')], [sgk], bias=self.one_col[:, 0:1], scale=1.0)
                self.act(sg[:, 0:n], sg[:, 0:n], AF.Exp, [sgk], [sgk], scale=-1.0)
                self.tt('dve', sg[:, 0:n], self.ps[b_r][:, 0:n], sg[:, 0:n], ALU.mult, [('ps', b_r), sgk], [sgk])
                fin[hl] += [sg, sgk]
            for hl in range(2):
                of, ofk, sq, sqk, sg, sgk = fin[hl]
                b2 = self.bank('pj')
                self.mm(self.ps[b2][:, 0:n], self.ones_bf[:, :], sq[:, 0:n], True, True, [sqk, ('c', 'ones')], [('ps', b2)])
                rs, rk = self.fbuf()
                self.act(rs[:, 0:n], self.ps[b2][:, 0:n], AF.Ln, [('ps', b2), ('c', 'eps')], [rk], bias=self.eps_col[:, 0:1], scale=1.0 / 128)
                self.act(rs[:, 0:n], rs[:, 0:n], AF.Exp, [rk], [rk], scale=-0.5)
                self.stt('dve', of[:, 0:n], of[:, 0:n], onc, rs[:, 0:n], ALU.mult, ALU.mult, [ofk, rk, ('c', 'cga')], [ofk])
                self.tt('dve', mixq[:, hl, t0:t0 + n], of[:, 0:n], sg[:, 0:n], ALU.mult, [ofk, sgk], [self.ak(('mq', hl, ti))])
        self.dma('sp', self.o_b[i, s, 2 * c:2 * c + 2].rearrange("h k v -> (h k) v"), Sf[:, :], [('Sf',)], [('ob', i, s, c)])

    def sample_pass(self):
        P = Pass(); P.s = -1; P.T = 64; P.tiles = [(0, 64)]
        self.make_plan(P, sample=True)
        ns = self.n_seq
        for half in range(2):
            st, sk = self.obuf()
            self.dma('sp', st[0:64, :], self.x_sample.rearrange("b t d -> (b t) d")[:, half * 512:(half + 1) * 512], [], [sk])
            bank = self.bank('tr')
            for q in range(4):
                self.tr(self.ps[bank][:, q * 128:q * 128 + 64], st[0:64, q * 128:(q + 1) * 128], self.ident[0:64, 0:64], [sk, ('c', 'ident')], [('ps', bank)])
            self.copy('act' if half else 'dve', self.xT[:, half * 4:(half + 1) * 4, 0:64],
                      self.ps[bank][:, :].rearrange("p (q t) -> p q t", q=4)[:, :, 0:64], [('ps', bank)], [('x', half * 4 + q, 0) for q in range(4)])
        for l in self.layers:
            if self.mixers:
                self.bank_roles = self.roles_mix
                if l % 2 == 1:
                    self.fox_sample(P, l)
                else:
                    self.ab_sample(P, l)
                self.bank_roles = self.roles_mlp
            if self.do_mlp:
                self.mlp(P, l)
        for half in range(2):
            st, sk = self.obuf()
            bank = self.bank('tr')
            for q in range(4):
                kc = half * 4 + q
                self.tr(self.ps[bank][0:64, q * 128:(q + 1) * 128], self.xT[:, kc, 0:64], self.ident[:, :], [('x', kc, 0), ('c', 'ident')], [('ps', bank)])
            self.copy('act' if half else 'dve', st[0:64, :], self.ps[bank][0:64, :], [('ps', bank)], [sk])
            self.dma('sp', self.y_sample.rearrange("b t d -> (b t) d")[:, half * 512:(half + 1) * 512], st[0:64, :], [sk], [('ys', half)])

    def ab_sample(self, P, l):
        i = l // 2; tiles = P.tiles
        self.rmsnorm(P, self.gmix, l)
        self.fence()
        QA, KA, VA, mixq = self.QA, self.KA, self.VA, self.mixq
        self.memset('pool', VA[:, 0:4, :, 64:128], 1.0, [self.ak(('VAo',))])
        for h in range(2):
            for p in range(2):
                for c in range(2):
                    idx = 8 * i + 4 * h + 2 * c + p
                    g = p * 2 + c
                    self.act(self.sinkrow[0:1, h, g * 128:(g + 1) * 128], self.onesF[0:1, 0:128], AF.Exp, [('c', 'onesF'), ('c', 'sink16')],
                             [('sinkrow', h)], bias=self.sink16[0:1, idx:idx + 1], scale=0.0)
        ropeS = (self.ropeS[:, 0, :], self.ropeS[:, 1, :], [('c', 'ropeS')])
        wk, wv = self.take()
        gk = self.cga[:, 2 + i:3 + i]
        for a in range(2):
            self.qk_rope(P, wk, wv[:, :, (0 if a == 0 else 256):(128 if a == 0 else 384)], 0, 0, 64, gk, KA[:, a, 0:64],
                         self.ak(('KA', a, 0)), 'sample' if a == 0 else None, i, ropetab=ropeS)
        bank = self.bank('pj')
        for kc in range(KC):
            self.mm(self.ps[bank][0:64, 0:128], self.hT[:, kc, 0:64], wv[:, kc, 128:256], kc == 0, kc == KC - 1, wk + [('h', kc, 0)], [('ps', bank)])
        self.copy('dve', VA[0:64, 0, :, 0:64], self.ps[bank][0:64, 0:128].rearrange("p (h d) -> p h d", h=2), [('ps', bank)], [self.ak(('VA', 0))])
        st, sk = self.obuf()
        self.copy('act', st[0:64, 0:128], self.ps[bank][0:64, 0:128], [('ps', bank)], [sk])
        for b in range(self.n_seq):
            self.dma('sp', self.o_avs[i, b, 96:128].rearrange("t h d -> t (h d)"), st[32 * b:32 * b + 32, 0:128], [sk], [('oavs', i, b)])
        for b in range(self.n_seq):
            self.dma('sp', self.o_aks[i, b, 0:96], self.cache_a_k[i, b, 32:128], [], [('oaks0', i, b)])
            self.dma('sp', self.o_avs[i, b, 0:96], self.cache_a_v[i, b, 32:128], [], [('oavs0', i, b)])
            self.dma('pool', VA[:, 2 + b, :, 0:64], self.cache_a_v[i, b], [], [self.ak(('VA', 2 + b))])
            for a in range(2):
                st, sk = self.obuf()
                if a == 0:
                    self.dma('sp', st[:, 0:128], self.cache_a_k[i, b].rearrange("t h d -> t (h d)"), [], [sk])
                else:
                    self.dma('sp', st[:, 0:64], self.cache_a_k[i, b, :, 1, :], [], [sk])
                    self.dma('sp', st[:, 64:128], self.cache_a_k[i, b, :, 0, :], [], [sk])
                bank = self.bank('tr')
                self.tr(self.ps[bank][:, 0:128], st[:, 0:128], self.ident[:, :], [sk, ('c', 'ident')], [('ps', bank)])
                self.copy('act', KA[:, a, 64 + 128 * b:64 + 128 * (b + 1)], self.ps[bank][:, 0:128], [('ps', bank)], [self.ak(('KAc', a, b))])
        gq = self.cga[:, i:i + 1]
        for h in range(2):
            wkq, wq = self.take()
            for c in range(2):
                self.qk_rope(P, wkq, wq[:, :, 128 * c:128 * (c + 1)], 0, 0, 64, gq, QA[:, c, 0:64], self.ak(('QA', c, 0)), None, i, ropetab=ropeS)
            for b in range(self.n_seq):
                nq = 32; W = 128; q0 = 32 * b
                ob = self.bank('oa')
                kts = [(2 + b, 64 + 128 * b, 128, 0, ('KAc', b)), (0, 32 * b, 32, 32 * b, ('KA', 0))]
                for idx, (vt, tok0, nk, pb, kk) in enumerate(kts):
                    pt, ptk = self.bbuf()
                    for p in range(2):
                        a = h ^ p
                        sbk = self.bank_roles['sc'][p]
                        kkey = (kk[0], a, kk[1])
                        self.mm(self.ps[sbk][pb:pb + nk, 0:2 * nq], KA[64 * p:64 * p + 64, a, tok0:tok0 + nk], QA[64 * p:64 * p + 64, :, q0:q0 + nq],
                                True, True, [kkey, ('QA', 0, 0), ('QA', 1, 0)], [('ps', sbk)], tile_position=(64 * p, pb))
                        self.act(pt[pb:pb + nk, p * 2 * nq:(p + 1) * 2 * nq], self.ps[sbk][pb:pb + nk, 0:2 * nq], AF.Exp, [('ps', sbk)], [ptk])
                    self.mm(self.ps[ob][:, 0:W], VA[pb:pb + nk, vt, h, :], pt[pb:pb + nk, 0:W], idx == 0, False, [ptk, ('VA', vt), ('VAo',)], [('ps', ob)],
                            tile_position=(pb, 0))
                srow = self.sinkrow[0:1, h, :].rearrange("o (g q) -> o g q", q=128)[:, :, 0:nq]
                self.mm(self.ps[ob][:, 0:W], self.vsink[0:1, 0:128], srow, False, True, [('sinkrow', h), ('c', 'vsink')], [('ps', ob)])
                rc, rck = self.fbuf()
                self.recip(rc[0:64, 0:W], self.ps[ob][64:128, 0:W], [('ps', ob)], [rck])
                for p in range(2):
                    self.tt('dve', mixq[64 * p:64 * p + 64, :, q0:q0 + nq], self.ps[ob][0:64, p * 2 * nq:(p + 1) * 2 * nq].rearrange("p (c q) -> p c q", c=2),
                            rc[0:64, p * 2 * nq:(p + 1) * 2 * nq].rearrange("p (c q) -> p c q", c=2), ALU.mult,
                            [('ps', ob), rck], [self.ak(('mq', 0, 0)), self.ak(('mq', 1, 0))])
            self.out_quad(P, self.take(), tiles)
        self.fence()
        for c in range(2):
            self.gla_sample_pair(P, l, c)
            self.out_quad(P, self.take(), tiles)

    def gla_sample_pair(self, P, l, c):
        i = l // 2
        ar = self.arena
        QG = ar[:, 0:TP]; KG = ar[:, TP:2 * TP]
        KT = ar[:, 2 * TP:2 * TP + 17 * 128].rearrange("p (j f) -> p j f", j=17)
        o = 2 * TP + 17 * 128
        VB = ar[:, o:o + 17 * 256].rearrange("p (j f) -> p j f", j=17)
        mixq = self.mixq
        EL = self.EL; Sf = self.Sf; Sb = self.Sb
        wk1, w1 = self.take()
        wk2, w2 = self.take(keep=2)
        negbg = self.cga[:, 4 + 2 * i + c:5 + 2 * i + c]
        onc = self.cga[:, 8 + i:9 + i]
        n = 64; cw = 32; nchk = 2; t0 = 0; ti = 0
        b_g = self.bank('st')
        for kc in range(KC):
            self.mm(self.ps[b_g][0:16, 0:n], w1[:, kc, 256:272], self.hT[:, kc, 0:n], kc == 0, kc == KC - 1, wk1 + [('h', kc, 0)], [('ps', b_g)])
        gl, glk = self.bbuf()
        self.copy('dve', gl[0:16, 0:n], self.ps[b_g][0:16, 0:n], [('ps', b_g)], [glk])
        b_z = self.bank('st')
        self.mm(self.ps[b_z][:, 0:n], self.wgb[0:16, i, 128 * c:128 * (c + 1)], gl[0:16, 0:n], True, True, [glk, ('c', 'wgb')], [('ps', b_z)])
        cum, cumk = self.xbuf()
        self.act(cum[:, 0:n], self.ps[b_z][:, 0:n], AF.Exp, [('ps', b_z), ('c', 'cga')], [cumk], bias=negbg, scale=-1.0)
        self.act(cum[:, 0:n], cum[:, 0:n], AF.Ln, [cumk, ('c', 'one')], [cumk], bias=self.one_col[:, 0:1], scale=1.0)
        self.scan(cum[:, 0:n], self.segmask[:, 32:96], cum[:, 0:n], 0.0, [cumk, ('c', 'seg')], [cumk])
        eb, ebk = self.xbuf()
        self.act(eb[:, 0:n], cum[:, 0:n], AF.Exp, [cumk], [ebk], scale=-1.0 / 16)
        self.copy('dve', EL[:, 0:nchk], eb[:, 0:n].rearrange("p (a b) -> p a b", b=cw)[:, :, cw - 1], [ebk], [('EL', 0)])
        b_q = self.bank('pj')
        for kc in range(KC):
            self.mm(self.ps[b_q][:, 0:n], w1[:, kc, 0:128], self.hT[:, kc, 0:n], kc == 0, kc == KC - 1, wk1 + [('h', kc, 0)], [('ps', b_q)])
        self.stt('dve', QG[:, 0:n], self.ps[b_q][:, 0:n], SCALE, eb[:, 0:n], ALU.mult, ALU.mult, [('ps', b_q), ebk], [self.ak(('QG', 0))])
        b_k = self.bank('pj')
        for kc in range(KC):
            self.mm(self.ps[b_k][:, 0:n], w1[:, kc, 128:256], self.hT[:, kc, 0:n], kc == 0, kc == KC - 1, wk1 + [('h', kc, 0)], [('ps', b_k)])
        en, enk = self.xbuf()
        self.act(en[:, 0:n], cum[:, 0:n], AF.Exp, [cumk], [enk], scale=1.0 / 16)
        self.tt('dve', KG[:, 0:n], self.ps[b_k][:, 0:n], en[:, 0:n], ALU.mult, [('ps', b_k), enk], [self.ak(('KG', 0))])
        c3 = cum[:, 0:n].rearrange("p (a b) -> p a b", b=cw)
        e3 = en[:, 0:n].rearrange("p (a b) -> p a b", b=cw)
        self.tt('dve', e3, c3, c3[:, :, cw - 1:cw].to_broadcast([128, nchk, cw]), ALU.subtract, [cumk, enk], [enk])
        self.act(en[:, 0:n], en[:, 0:n], AF.Exp, [enk], [enk], scale=1.0 / 16)
        kh, khk = self.fbuf()
        self.tt('dve', kh[:, 0:n], self.ps[b_k][:, 0:n], en[:, 0:n], ALU.mult, [('ps', b_k), enk], [khk])
        b_t = self.bank('tr')
        self.tr(self.ps[b_t][0:n, 0:128], kh[:, 0:n], self.ident[:, :], [khk, ('c', 'ident')], [('ps', b_t)])
        self.copy('act', KT[0:n, 0, :], self.ps[b_t][0:n, 0:128], [('ps', b_t)], [self.ak(('KT', 0))])
        bank = self.bank('pj')
        for kc in range(KC):
            self.mm(self.ps[bank][0:n, 0:256], self.hT[:, kc, 0:n], w2[:, kc, 0:256], kc == 0, kc == KC - 1, wk2 + [('h', kc, 0)], [('ps', bank)])
        self.copy('act', VB[0:n, 0, :], self.ps[bank][0:n, 0:256], [('ps', bank)], [self.ak(('VB', 0))])
        roles = self.bank_roles
        at_bs, su_b = roles['sc'], roles['tr'][0]
        for b in range(self.n_seq):
            self.dma('sp', Sf[:, :], self.state_b[i, b, 2 * c:2 * c + 2].rearrange("h k v -> (h k) v"), [], [('Sf',)])
            self.copy('act', Sb[:, :], Sf[:, :], [('Sf',)], [('Sb',)])
            tok0, nt, p0 = 32 * b, 32, 32 * b
            am, amk = self.bbuf()
            for hl in range(2):
                at_b = at_bs[hl]
                self.mm(self.ps[at_b][p0:p0 + nt, 0:nt], KG[64 * hl:64 * hl + 64, tok0:tok0 + nt], QG[64 * hl:64 * hl + 64, tok0:tok0 + nt],
                        True, True, [('KG', 0), ('QG', 0)], [('ps', at_b)], tile_position=(64 * hl, p0))
                self.tt('dve', am[p0:p0 + nt, hl * 32:hl * 32 + nt], self.ps[at_b][p0:p0 + nt, 0:nt], self.trimask3[p0:p0 + nt, 0:nt], ALU.mult,
                        [('ps', at_b), ('c', 'tri3')], [amk])
            ob = self.bank('oa')
            for hl in range(2):
                self.mm(self.ps[ob][:, hl * 32:hl * 32 + nt], VB[p0:p0 + nt, 0, 128 * hl:128 * hl + 128], am[p0:p0 + nt, hl * 32:hl * 32 + nt], True, False,
                        [amk, ('VB', 0)], [('ps', ob)], tile_position=(p0, 0))
                self.mm(self.ps[ob][:, hl * 32:hl * 32 + nt], Sb[64 * hl:64 * hl + 64, :], QG[64 * hl:64 * hl + 64, tok0:tok0 + nt], False, True,
                        [('Sb',), ('QG', 0)], [('ps', ob)], tile_position=(64 * hl, 0))
            for hl in range(2):
                self.mm(self.ps[su_b][64 * hl:64 * hl + 64, 0:128], KT[p0:p0 + nt, 0, 64 * hl:64 * hl + 64], VB[p0:p0 + nt, 0, 128 * hl:128 * hl + 128],
                        True, True, [('KT', 0), ('VB', 0)], [('ps', su_b)], tile_position=(p0, 64 * hl))
            self.stt('dve', Sf[:, :], Sf[:, :], EL[:, b:b + 1], self.ps[su_b][:, 0:128], ALU.mult, ALU.add, [('Sf',), ('EL', 0), ('ps', su_b)], [('Sf',)])
            self.dma('sp', self.o_bs[i, b, 2 * c:2 * c + 2].rearrange("h k v -> (h k) v"), Sf[:, :], [('Sf',)], [('obs', i, b, c)])
            for hl in range(2):
                of, ofk = self.xbuf()
                self.copy('act', of[:, 0:nt], self.ps[ob][:, hl * 32:hl * 32 + nt], [('ps', ob)], [ofk])
                sq, sqk = self.bbuf()
                self.act(sq[:, 0:nt], of[:, 0:nt], AF.Square, [ofk], [sqk])
                b2 = self.bank('st')
                self.mm(self.ps[b2][:, 0:nt], self.ones_bf[:, :], sq[:, 0:nt], True, True, [sqk, ('c', 'ones')], [('ps', b2)])
                rs, rk = self.fbuf()
                self.act(rs[:, 0:nt], self.ps[b2][:, 0:nt], AF.Ln, [('ps', b2), ('c', 'eps')], [rk], bias=self.eps_col[:, 0:1], scale=1.0 / 128)
                self.act(rs[:, 0:nt], rs[:, 0:nt], AF.Exp, [rk], [rk], scale=-0.5)
                b_r = self.bank('pj')
                for kc in range(KC):
                    self.mm(self.ps[b_r][:, 0:nt], w2[:, kc, 256 + 128 * hl:256 + 128 * (hl + 1)], self.hT[:, kc, tok0:tok0 + nt], kc == 0, kc == KC - 1,
                            wk2 + [('h', kc, 0)], [('ps', b_r)])
                sg, sgk = self.fbuf()
                self.act(sg[:, 0:nt], self.ps[b_r][:, 0:nt], AF.Exp, [('ps', b_r)], [sgk], scale=-1.0)
                self.ts('dve', sg[:, 0:nt], sg[:, 0:nt], 1.0, None, ALU.add, None, [sgk], [sgk])
                self.recip(sg[:, 0:nt], sg[:, 0:nt], [sgk], [sgk])
                self.tt('dve', sg[:, 0:nt], self.ps[b_r][:, 0:nt], sg[:, 0:nt], ALU.mult, [('ps', b_r), sgk], [sgk])
                self.stt('dve', of[:, 0:nt], of[:, 0:nt], onc, rs[:, 0:nt], ALU.mult, ALU.mult, [ofk, rk, ('c', 'cga')], [ofk])
                self.tt('dve', mixq[:, hl, tok0:tok0 + nt], of[:, 0:nt], sg[:, 0:nt], ALU.mult, [ofk, sgk], [self.ak(('mq', hl, 0))])

    def fox_sample(self, P, l):
        i = l // 2; tiles = P.tiles
        PL = 2048; TK = PL + 32
        self.rmsnorm(P, self.gmix, l)
        self.fence()
        VA, mixq = self.VA, self.mixq
        QA = self.arena[:, 0:128].rearrange("p (h t) -> p h t", h=2)
        KA = self.arena[:, 128:128 + 2 * TK].rearrange("p (h t) -> p h t", h=2)
        ident = ('c', 'ident')
        self.memset('pool', QA[64:70, :, :], 1.0, [self.ak(('QAa', 0)), self.ak(('QAa', 1))])
        self.memset('pool', KA[64:70, :, :], -1.0, [self.ak(('KAa', 0)), self.ak(('KAa', 1))])
        self.memset('pool', VA[:, :, :, 64:128], 1.0, [self.ak(('VAo',))])
        cp = self.xscr
        wk, wv = self.take()
        bank = self.bank('st')
        for kc in range(KC):
            self.mm(self.ps[bank][0:16, 0:64], wv[:, kc, 0:16], self.hT[:, kc, 0:64], kc == 0, kc == KC - 1, wk + [('h', kc, 0)], [('ps', bank)])
        nls, nlk = self.fbuf()
        self.act(nls[0:16, 0:64], self.ps[bank][0:16, 0:64], AF.Exp, [('ps', bank), ('c', 'cg')], [nlk], bias=self.cg[0:16, 4 + i:5 + i], scale=-1.0)
        self.act(nls[0:16, 0:64], nls[0:16, 0:64], AF.Ln, [nlk, ('c', 'one')], [nlk], bias=self.one_col[0:16, 0:1], scale=1.0)
        bank = self.bank('tr')
        self.tr(self.ps[bank][0:64, 0:16], nls[0:16, 0:64], self.ident[0:16, 0:16], [nlk, ident], [('ps', bank)])
        st, sk = self.obuf()
        self.amul(st[0:64, 0:16], self.ps[bank][0:64, 0:16], -1.0, [('ps', bank)], [sk])
        for b in range(self.n_seq):
            self.dma('sp', self.o_cfs[i, b], st[32 * b:32 * b + 32, 0:16], [sk], [('ocfs', i, b)])
        cs = [cp[32 + 32 * j:48 + 32 * j, :].bitcast(BF16)[:, 0:TK] for j in range(3)]
        ctiles = [(512 * j, 512) for j in range(4)] + [(PL, 32)]
        for b in range(self.n_seq):
            st, sk = self.obuf()
            for hf in range(2):
                self.dma('sp', st[:, 128 * hf:128 * (hf + 1)].rearrange("p (j h) -> p j h", h=16),
                         self.cache_c_logf[i, b].rearrange("(j p) h -> p j h", p=128)[:, 8 * hf:8 * (hf + 1), :], [], [sk])
            for g in range(4):
                bank = self.bank('tr')
                for q in range(4):
                    j = 4 * g + q
                    self.tr(self.ps[bank][0:16, q * 128:(q + 1) * 128], st[:, 16 * j:16 * (j + 1)], self.ident[:, :], [sk, ident], [('ps', bank)])
                self.amul(cp[0:16, 512 * g:512 * (g + 1)], self.ps[bank][0:16, :], -1.0, [('ps', bank)], [self.ak(('cpos', g))])
            self.copy('dve', cp[0:16, PL:TK], nls[0:16, 32 * b:32 * b + 32], [nlk], [self.ak(('cpos', 4))])
            for ti, (t0, n) in enumerate(ctiles):
                c = cp[0:16, t0:t0 + n]
                init = 0.0 if ti == 0 else cp[0:16, t0 - 1:t0]
                rd = [('cpos', ti), ('c', 'onesF')] + ([('cpos', ti - 1)] if ti else [])
                self.scan(c, self.onesF[0:16, 0:n], c, init, rd, [('cpos', ti)])
            for ti, (t0, n) in enumerate(ctiles):
                c = cp[0:16, t0:t0 + n]
                for j in range(2):
                    mb, mk = self.bbuf()
                    self.copy('dve', mb[0:16, 0:n], c, [('cpos', ti)], [mk])
                    self.copy('pool', cs[j][:, t0:t0 + n], mb[0:16, 0:n], [mk], [self.ak(('cs', ti))])
                    self.tt('dve', c, c, mb[0:16, 0:n], ALU.subtract, [('cpos', ti), mk], [('cpos', ti)])
                self.copy('dve', cs[2][:, t0:t0 + n], c, [('cpos', ti)], [self.ak(('cs', ti))])
            cskeys = [('cs', ti) for ti in range(5)]
            for hp in range(8):
                wk, wv = self.take()
                for hl in range(2):
                    h = 2 * hp + hl
                    for j in range(3):
                        self.dma('sp', KA[64 + j:65 + j, hl, 0:TK], cs[j][h:h + 1, 0:TK], cskeys, [self.ak(('KAa', hl))])
                        self.dma('sp', QA[67 + j:68 + j, hl, 0:32], cs[j][h:h + 1, PL:TK], cskeys, [self.ak(('QAa', hl))])
                cv = self.cache_c_v[i, b].rearrange("(j p) h d -> p j h d", p=128)
                for hf in range(2):
                    for hl in range(2):
                        self.dma('pool', VA[:, 8 * hf:8 * (hf + 1), hl, 0:64], cv[:, 8 * hf:8 * (hf + 1), 2 * hp + hl, :], [], [self.ak(('VA', hf))])
                ck = self.cache_c_k[i, b].rearrange("(j p) h d -> p j (h d)", p=128)
                kst = []
                for g in range(4):
                    st, sk = self.obuf()
                    self.dma('sp', st[:, :].rearrange("p (j f) -> p j f", j=4), ck[:, 4 * g:4 * (g + 1), 128 * hp:128 * (hp + 1)], [], [sk])
                    kst.append((st, sk))
                c0 = 32 * b
                pr = {}
                for which in range(2):
                    bank = self.bank('pjw')
                    for kc in range(KC):
                        self.mm(self.ps[bank][:, 0:32], wv[:, kc, which * 128:(which + 1) * 128], self.hT[:, kc, c0:c0 + 32], kc == 0, kc == KC - 1,
                                wk + [('h', kc, 0)], [('ps', bank)])
                    sq, sqk = self.bbuf()
                    self.act(sq[:, 0:32], self.ps[bank][:, 0:32], AF.Square, [('ps', bank)], [sqk])
                    pr[which] = (bank, sq, sqk)
                bank_v = self.bank('pjw')
                for kc in range(KC):
                    self.mm(self.ps[bank_v][0:32, 0:128], self.hT[:, kc, c0:c0 + 32], wv[:, kc, 256:384], kc == 0, kc == KC - 1, wk + [('h', kc, 0)], [('ps', bank_v)])
                for g in range(4):
                    st, sk = kst[g]
                    bank = self.bank('st')
                    for q in range(4):
                        self.tr(self.ps[bank][:, q * 128:(q + 1) * 128], st[:, q * 128:(q + 1) * 128], self.ident[:, :], [sk, ident], [('ps', bank)])
                    for hl in range(2):
                        self.copy('act' if hl else 'dve', KA[0:64, hl, 512 * g:512 * (g + 1)], self.ps[bank][64 * hl:64 * hl + 64, :], [('ps', bank)],
                                  [self.ak(('KA', hl, g))])
                kf_item = None
                for which in range(2):
                    gcol = self.cg[:, which * 2 + i:which * 2 + i + 1]
                    bank, sq, sqk = pr[which]
                    b2 = self.bank('st')
                    self.mm(self.ps[b2][:, 0:32], self.blockones[:, :], sq[:, 0:32], True, True, [sqk, ('c', 'bo')], [('ps', b2)])
                    rs, rk = self.fbuf()
                    self.act(rs[:, 0:32], self.ps[b2][:, 0:32], AF.Ln, [('ps', b2), ('c', 'eps')], [rk], bias=self.eps_col[:, 0:1], scale=1.0 / HD)
                    self.act(rs[:, 0:32], rs[:, 0:32], AF.Exp, [rk], [rk], scale=-0.5)
                    for hl in range(2):
                        p0 = 64 * hl
                        dst = QA[0:64, hl, 0:32] if which == 0 else KA[0:64, hl, PL:TK]
                        dkey = self.ak(('QA', hl, 0)) if which == 0 else self.ak(('KA', hl, 4))
                        self.stt('dve', dst, self.ps[bank][p0:p0 + 64, 0:32], gcol[p0:p0 + 64, :], rs[p0:p0 + 64, 0:32], ALU.mult, ALU.mult,
                                 [('ps', bank), rk, ('c', 'cg')], [dkey])
                    if which == 1:
                        kf, kfk = self.fbuf()
                        self.stt('dve', kf[:, 0:32], self.ps[bank][:, 0:32], gcol, rs[:, 0:32], ALU.mult, ALU.mult, [('ps', bank), rk, ('c', 'cg')], [kfk])
                        kf_item = (kf, kfk)
                st, sk = self.obuf()
                self.copy('act', st[0:32, 0:128], self.ps[bank_v][0:32, 0:128], [('ps', bank_v)], [sk])
                self.copy('dve', VA[0:32, 16, :, 0:64], self.ps[bank_v][0:32, 0:128].rearrange("p (h d) -> p h d", h=2), [('ps', bank_v)], [self.ak(('VA', 2))])
                self.dma('sp', self.o_cvs[i, b, :, 2 * hp:2 * hp + 2, :].rearrange("t h d -> t (h d)"), st[0:32, 0:128], [sk], [('ocvs', i, b, hp)])
                obs2 = [self.bank('oa'), self.bank('oa')]
                pend = []
                LOOK = 4
                nkt = 17

                def pv2(item, idx, obs2=obs2):
                    (j, nk), pt, ptk = item
                    for hl in range(2):
                        self.mm(self.ps[obs2[hl]][:, 0:32], VA[0:nk, j, hl, :], pt[0:nk, 32 * hl:32 * hl + 32], idx == 0, idx == nkt - 1,
                                [ptk, ('VA', 0 if j < 8 else (1 if j < 16 else 2)), ('VAo',)], [('ps', obs2[hl])])

                for j in range(17):
                    nk = 128 if j < 16 else 32
                    sbk = self.bank('sc4')
                    for hl in range(2):
                        kkey = ('KA', hl, j // 4) if j < 16 else ('KA', hl, 4)
                        self.mm(self.ps[sbk][0:nk, 32 * hl:32 * hl + 32], KA[0:70, hl, 128 * j:128 * j + nk], QA[0:70, hl, 0:32], True, True,
                                [kkey, ('KAa', hl), ('QA', hl, 0), ('QAa', hl)], [('ps', sbk)])
                    pt, ptk = self.pbuf()
                    self.act(pt[0:nk, 0:64], self.ps[sbk][0:nk, 0:64], AF.Exp, [('ps', sbk)], [ptk])
                    if j == 16:
                        self.tt('dve', pt[0:32, 0:64].rearrange("p (h q) -> p h q", h=2), pt[0:32, 0:64].rearrange("p (h q) -> p h q", h=2),
                                self.trimask[0:32, 0:32].unsqueeze(1).to_broadcast([32, 2, 32]), ALU.mult, [ptk, ('c', 'tri')], [ptk])
                    pend.append((((j, nk), pt, ptk), j))
                    if len(pend) > LOOK:
                        it, ix = pend.pop(0)
                        pv2(it, ix)
                for it, ix in pend:
                    pv2(it, ix)
                for hl in range(2):
                    ob = obs2[hl]
                    rc, rck = self.fbuf()
                    self.recip(rc[0:64, 0:32], self.ps[ob][64:128, 0:32], [('ps', ob)], [rck])
                    self.tt('dve', mixq[64 * hl:64 * hl + 64, hp % 2, c0:c0 + 32], self.ps[ob][0:64, 0:32], rc[0:64, 0:32], ALU.mult,
                            [('ps', ob), rck], [self.ak(('mq', hp % 2, 0))])
                self.tm_out(self.o_cks[i, b], hp, 0, 32, kf_item[0], kf_item[1])
                if hp % 2 == 1:
                    wk2, wo = self.take()
                    for dc in range(KC):
                        bank = self.bank('pjw')
                        for kc in range(2):
                            self.mm(self.ps[bank][:, 0:32], wo[:, kc, dc * 128:(dc + 1) * 128], mixq[:, kc, c0:c0 + 32], kc == 0, kc == 1,
                                    wk2 + [('mq', kc, 0)], [('ps', bank)])
                        self.tt('dve', self.xT[:, dc, c0:c0 + 32], self.xT[:, dc, c0:c0 + 32], self.ps[bank][:, 0:32], ALU.add,
                                [('ps', bank), ('x', dc, 0)], [('x', dc, 0)])

    def scan(self, out, data0, data1, init, r, w):
        self.R.op('dve', lambda e: e.tensor_tensor_scan(out=out, data0=data0, data1=data1, initial=init, op0=ALU.mult, op1=ALU.add), r, w)

    def emit(self):
        nc = self.nc
        R = self.R
        R.finalize()
        sem_tl = {}
        for e in ENG:
            for ep in range(R.nep[e]):
                sem_tl[(e, ep)] = self.es.enter_context(nc.semaphore(f"tl_{e}_{ep}"))
        sem_dma = {}
        for e in ('sp', 'act', 'pool'):
            for i in range(NDSEM):
                if R.ring[e]['count'][i] > 0:
                    sem_dma[(e, i)] = self.es.enter_context(nc.semaphore(f"dma_{e}_{i}"))
        block = self.es.enter_context(nc.Block())

        def run(e, engine):
            for ins in R.q[e]:
                for k, v in ins.waits.items():
                    sem = sem_tl[(k[1], k[2])] if k[0] == 'tl' else sem_dma[(k[1], k[2])]
                    engine.wait_ge(sem, v)
                bi = ins.fn(engine)
                if ins.dma:
                    bi.then_inc(sem_dma[ins.sem], 16)
                elif ins.inc:
                    bi.then_inc(sem_tl[(e, ins.ep)], 1)
            if e == 'sp':
                for (q, i), sem in sem_dma.items():
                    engine.wait_ge(sem, R.ring[q]['count'][i])

        block.tensor(lambda t: run('pe', t))
        block.scalar(lambda t: run('act', t))
        block.vector(lambda t: run('dve', t))
        block.gpsimd(lambda t: run('pool', t))
        block.sync(lambda t: run('sp', t))
        self.es.close()
        self.counts = {e: len(R.q[e]) for e in ENG}


_W_NAMES = ["meta_tokens", "norm_mix", "norm_mlp", "w_in_ab", "qnorm_a", "knorm_a", "sink_a", "w_gate_b", "b_gate_b", "onorm_b", "w_out_ab",
            "w_in_c", "b_f_c", "qnorm_c", "knorm_c", "w_out_c", "w_up", "w_down"]
_OUT_NAMES = ["y_prompt", "y_sample", "new_a_k_p", "new_a_v_p", "new_b_p", "new_c_k_p", "new_c_v_p", "new_c_logf_p",
              "new_a_k_s", "new_a_v_s", "new_b_s", "new_c_k_s", "new_c_v_s", "new_c_logf_s"]


def kernel(**inputs):
    inp = {k: np.ascontiguousarray(np.asarray(v), dtype=np.float32) for k, v in inputs.items()}
    nb = 2
    b = Builder(layers=(0, 1, 2, 3), n_seq=nb, do_sample=True, mixers=True, do_mlp=True, dbg=False, x_full=False, do_prompt=True)
    nc = b.build()
    in_maps = []
    for c in range(NCORES):
        sl = slice(nb * c, nb * (c + 1))
        m = {"x_prompt": np.ascontiguousarray(inp["x_prompt"][sl]), "x_sample": np.ascontiguousarray(inp["x_sample"][sl])}
        for k in ["cache_a_k", "cache_a_v", "state_b", "cache_c_k", "cache_c_v", "cache_c_logf"]:
            m[k] = np.ascontiguousarray(inp[k][:, sl])
        for k in _W_NAMES:
            m[k] = inp[k]
        in_maps.append(m)
    res = run_bass_kernel_spmd(nc, in_maps, core_ids=list(range(NCORES)))
    outs = []
    for idx, name in enumerate(_OUT_NAMES):
        axis = 0 if idx < 2 else 1
        outs.append(np.concatenate([np.asarray(r[name], dtype=np.float32) for r in res.results], axis=axis))
    return tuple(outs)
```

```python
import math
import numpy as np
from contextlib import ExitStack
import concourse.bass as bass
import concourse.mybir as mybir
from concourse.bass_utils import run_bass_kernel_spmd

F32 = mybir.dt.float32
BF16 = mybir.dt.bfloat16
AF = mybir.ActivationFunctionType
ALU = mybir.AluOpType

D = 1024
KC = 8
SEQ = 2048
NMETA = 16
TP = NMETA + SEQ
DEPTH = 4
DFF = 4096
EPS = 1e-6
NCORES = 8
P_AB = 2320
P_C = 3088

ENG = ('pe', 'act', 'dve', 'pool', 'sp')
EPOCH = 16000
NDSEM = 8


class Ins:
    __slots__ = ('eng', 'fn', 'deps', 'inc', 'dma', 'sem', 'cnt', 'waits', 'ep', 'force')


class Rec:
    def __init__(self):
        self.q = {e: [] for e in ENG}
        self.lastw = {}
        self.readers = {}
        self.ring = {e: {'next': 0, 'last': [None] * NDSEM, 'count': [0] * NDSEM} for e in ('sp', 'act', 'pool')}

    def op(self, eng, fn, reads=(), writes=(), dma=False, force=()):
        ins = Ins()
        ins.eng = eng; ins.fn = fn; ins.dma = dma; ins.inc = False; ins.deps = {}; ins.sem = None; ins.cnt = 0
        ins.force = set(force)
        deps = ins.deps
        for d in force:
            deps[d] = True
        for k in reads:
            w = self.lastw.get(k)
            if w is not None:
                deps[w] = True
            if k[0] == 'ps':
                for r in self.readers.get(k, ()):
                    if r.eng != eng:
                        deps[r] = True
        for k in writes:
            w = self.lastw.get(k)
            if w is not None and w not in deps:
                deps[w] = False
            for r in self.readers.get(k, ()):
                if r not in deps:
                    deps[r] = False
        if dma:
            ring = self.ring[eng]
            s = ring['next']; ring['next'] = (s + 1) % NDSEM
            prev = ring['last'][s]
            if prev is not None:
                deps[prev] = True
            ring['count'][s] += 16
            ins.sem = (eng, s); ins.cnt = ring['count'][s]; ring['last'][s] = ins
        for k in writes:
            self.lastw[k] = ins
            self.readers[k] = []
        for k in reads:
            lst = self.readers.get(k)
            if lst is None:
                self.readers[k] = [ins]
            else:
                if not dma:
                    lst[:] = [r for r in lst if r.dma or r.eng != eng]
                lst.append(ins)
        self.q[eng].append(ins)
        return ins

    @staticmethod
    def _needed(ins, d, raw):
        if d.dma or d in ins.force:
            return True
        if d.eng == ins.eng:
            if ins.eng == 'pe':
                return False
            return True
        return True

    def finalize(self):
        for e in ENG:
            for ins in self.q[e]:
                for d, raw in ins.deps.items():
                    if not d.dma and self._needed(ins, d, raw):
                        d.inc = True
        self.nep = {}
        for e in ENG:
            c = 0; ep = 0
            for ins in self.q[e]:
                if ins.dma:
                    continue
                if ins.inc:
                    c += 1
                    if c > EPOCH:
                        ep += 1; c = 1
                ins.cnt = c; ins.ep = ep
            self.nep[e] = ep + 1
        for e in ENG:
            waited = {}
            for ins in self.q[e]:
                w = {}
                for d, raw in ins.deps.items():
                    if not self._needed(ins, d, raw):
                        continue
                    if d.dma:
                        key = ('dma',) + d.sem; val = d.cnt
                    else:
                        key = ('tl', d.eng, d.ep); val = d.cnt
                    if waited.get(key, 0) >= val:
                        continue
                    if w.get(key, 0) < val:
                        w[key] = val
                for k, v in w.items():
                    waited[k] = v
                ins.waits = w


class Pass:
    pass


N_EVEN = 2
N_ODD = 2
NH_C = 16
HD = 64
SCALE = 0.125
NW = 3
PF = 2
ARENA = 16768


class Builder:
    def __init__(self, layers=(0, 1, 2, 3), n_seq=2, do_sample=True, mixers=True, do_mlp=True, dbg=False, x_full=False, do_prompt=True):
        self.layers = layers; self.n_seq = n_seq; self.do_sample = do_sample; self.mixers = mixers; self.dbg = dbg
        self.do_mlp = do_mlp; self.x_full = x_full; self.do_prompt = do_prompt
        self.nc = bass.Bass("TRN2", target_bir_lowering=False)
        self.R = Rec()
        self.es = ExitStack()
        self._rr = {}
        self.arena_keys = set()
        self.ps_last = {}

    def sb(self, name, shape, dt):
        return self.es.enter_context(self.nc.sbuf_tensor(name, shape, dt))

    def din(self, name, shape, dt=F32):
        return self.nc.dram_tensor(name, list(shape), dt, kind="ExternalInput").ap()

    def dout(self, name, shape, dt=F32):
        return self.nc.dram_tensor(name, list(shape), dt, kind="ExternalOutput").ap()

    def rr(self, name, n):
        v = self._rr.get(name, 0) % n
        self._rr[name] = v + 1
        return v

    def bank(self, role):
        lst = self.bank_roles[role]
        return lst[self.rr('bank_' + role, len(lst))]

    def fbuf(self):
        i = self.rr('scrF', len(self.scrF))
        return self.scrF[i], ('sF', i)

    def bbuf(self):
        i = self.rr('scrB', len(self.scrB))
        return self.scrB[i], ('sB', i)

    def pbuf(self):
        i = self.rr('ptb', len(self.ptb))
        return self.ptb[i], ('ptb', i)

    def obuf(self):
        i = self.rr('ost', len(self.ost))
        return self.ost[i], ('ost', i)

    def ak(self, key):
        self.arena_keys.add(key)
        lf = getattr(self, 'last_fence', None)
        if lf is not None and key not in self.R.lastw:
            self.R.lastw[key] = lf
            self.R.readers[key] = []
        return key

    def fence(self):
        d = self.dummy
        self.last_fence = self.R.op('dve', lambda e: e.memset(d[:, 0:1], 0.0), (), list(self.arena_keys) + [('dummy',)])

    def _pe(self, fn, r, w, rg):
        force = []
        banks = [k[1] for k in w if k[0] == 'ps']
        for b in banks:
            last = self.ps_last.get(b)
            if last is not None and last[1] != rg:
                force.append(last[0])
        ins = self.R.op('pe', fn, r, w, force=force)
        for b in banks:
            self.ps_last[b] = (ins, rg)

    def mm(self, out, lhsT, rhs, start, stop, r, w, **kw):
        rg = kw['tile_position'][0] if 'tile_position' in kw else 0
        self._pe(lambda e: e.matmul(out, lhsT, rhs, start=start, stop=stop, **kw), r, w, rg)

    def tr(self, out, in_, ident, r, w):
        self._pe(lambda e: e.transpose(out, in_, ident), r, w, 0)

    def act(self, out, in_, func, r, w, bias=None, scale=None):
        kw = {}
        if bias is not None:
            kw['bias'] = bias
        if scale is not None:
            kw['scale'] = scale
        self.R.op('act', lambda e: e.activation(out=out, in_=in_, func=func, **kw), r, w)

    def tt(self, eng, out, in0, in1, op, r, w):
        self.R.op(eng, lambda e: e.tensor_tensor(out=out, in0=in0, in1=in1, op=op), r, w)

    def ts(self, eng, out, in0, s1, s2, op0, op1, r, w):
        if op1 is None:
            self.R.op(eng, lambda e: e.tensor_scalar(out=out, in0=in0, scalar1=s1, scalar2=None, op0=op0), r, w)
        else:
            self.R.op(eng, lambda e: e.tensor_scalar(out=out, in0=in0, scalar1=s1, scalar2=s2, op0=op0, op1=op1), r, w)

    def stt(self, eng, out, in0, scalar, in1, op0, op1, r, w):
        self.R.op(eng, lambda e: e.scalar_tensor_tensor(out=out, in0=in0, scalar=scalar, in1=in1, op0=op0, op1=op1), r, w)

    def copy(self, eng, out, in_, r, w):
        if eng == 'act':
            self.R.op('act', lambda e: e.copy(out=out, in_=in_), r, w)
        else:
            self.R.op(eng, lambda e: e.tensor_copy(out=out, in_=in_), r, w)

    def recip(self, out, in_, r, w, eng='dve'):
        if eng == 'act':
            self.act(out, in_, AF.Ln, r, w)
            self.act(out, out, AF.Exp, w, w, scale=-1.0)
        else:
            self.R.op('dve', lambda e: e.reciprocal(out=out, in_=in_), r, w)

    def memset(self, eng, ap, val, w):
        self.R.op(eng, lambda e: e.memset(ap, val), (), w)

    def dma(self, q, out, in_, r, w, slow=False):
        if slow:
            self.R.op(q, lambda e: e.dma_start(out=out, in_=in_, allow_slow_non_contiguous=True), r, w, dma=True)
        else:
            self.R.op(q, lambda e: e.dma_start(out=out, in_=in_), r, w, dma=True)

    def build(self):
        nc = self.nc
        ns = self.n_seq
        if self.x_full:
            self.x_full_in = self.din("x_full", [ns, TP, D])
        else:
            self.x_prompt = self.din("x_prompt", [ns, SEQ, D])
            self.meta_tokens = self.din("meta_tokens", [NMETA, D])
        self.norm_mix = self.din("norm_mix", [DEPTH, D])
        self.norm_mlp = self.din("norm_mlp", [DEPTH, D])
        self.w_up = self.din("w_up", [DEPTH, D, DFF])
        self.w_down = self.din("w_down", [DEPTH, DFF, D])
        self.w_in_c = self.din("w_in_c", [N_ODD, D, P_C])
        self.b_f_c = self.din("b_f_c", [N_ODD, NH_C])
        self.qnorm_c = self.din("qnorm_c", [N_ODD, HD])
        self.knorm_c = self.din("knorm_c", [N_ODD, HD])
        self.w_out_c = self.din("w_out_c", [N_ODD, D, D])
        self.w_in_ab = self.din("w_in_ab", [N_EVEN, D, P_AB])
        self.qnorm_a = self.din("qnorm_a", [N_EVEN, HD])
        self.knorm_a = self.din("knorm_a", [N_EVEN, HD])
        self.sink_a = self.din("sink_a", [N_EVEN, 8])
        self.w_gate_b = self.din("w_gate_b", [N_EVEN, 16, 256])
        self.b_gate_b = self.din("b_gate_b", [N_EVEN, 256])
        self.onorm_b = self.din("onorm_b", [N_EVEN, 128])
        self.w_out_ab = self.din("w_out_ab", [N_EVEN, D, D])
        self.x_sample = self.din("x_sample", [ns, 32, D])
        self.cache_a_k = self.din("cache_a_k", [N_EVEN, ns, 128, 2, HD])
        self.cache_a_v = self.din("cache_a_v", [N_EVEN, ns, 128, 2, HD])
        self.state_b = self.din("state_b", [N_EVEN, ns, 4, 64, 128])
        self.cache_c_k = self.din("cache_c_k", [N_ODD, ns, 2048, NH_C, HD])
        self.cache_c_v = self.din("cache_c_v", [N_ODD, ns, 2048, NH_C, HD])
        self.cache_c_logf = self.din("cache_c_logf", [N_ODD, ns, 2048, NH_C])
        self.y_prompt = self.dout("y_prompt", [ns, SEQ, D])
        self.y_sample = self.dout("y_sample", [ns, 32, D])
        self.o_aks = self.dout("new_a_k_s", [N_EVEN, ns, 128, 2, HD])
        self.o_avs = self.dout("new_a_v_s", [N_EVEN, ns, 128, 2, HD])
        self.o_bs = self.dout("new_b_s", [N_EVEN, ns, 4, 64, 128])
        self.o_cks = self.dout("new_c_k_s", [N_ODD, ns, 32, NH_C, HD])
        self.o_cvs = self.dout("new_c_v_s", [N_ODD, ns, 32, NH_C, HD])
        self.o_cfs = self.dout("new_c_logf_s", [N_ODD, ns, 32, NH_C])
        self.o_ak = self.dout("new_a_k_p", [N_EVEN, ns, 128, 2, HD])
        self.o_av = self.dout("new_a_v_p", [N_EVEN, ns, 128, 2, HD])
        self.o_b = self.dout("new_b_p", [N_EVEN, ns, 4, 64, 128])
        self.o_ck = self.dout("new_c_k_p", [N_ODD, ns, TP, NH_C, HD])
        self.o_cv = self.dout("new_c_v_p", [N_ODD, ns, TP, NH_C, HD])
        self.o_cf = self.dout("new_c_logf_p", [N_ODD, ns, TP, NH_C])
        if self.dbg:
            self.dbg_x = self.dout("dbg_x", [ns, 128, KC, TP])
            self.dbg_mix = self.dout("dbg_mix", [4, 128, 2, TP], BF16)

        self.xT = self.sb("xT", [128, KC, TP], F32)
        self.hT = self.sb("hT", [128, KC, TP], BF16)
        self.arena = self.sb("arena", [128, ARENA], BF16)
        ar = self.arena
        self.ub = [ar[:, i * 4 * TP:(i + 1) * 4 * TP].rearrange("p (k t) -> p k t", k=4) for i in range(2)]
        self.QA = ar[:, 0:2 * TP].rearrange("p (h t) -> p h t", h=2)
        self.KA = ar[:, 2 * TP:4 * TP].rearrange("p (h t) -> p h t", h=2)
        o = 4 * TP
        self.VA = ar[:, o:o + 17 * 256].rearrange("p (j h d) -> p j h d", j=17, h=2)
        o += 17 * 256
        self.mixq = ar[:, o:o + 2 * TP].rearrange("p (h t) -> p h t", h=2)
        o += 2 * TP
        assert o <= ARENA
        self.wsl = [self.sb(f"wsl{i}", [128, 4096], BF16) for i in range(NW)]
        self.scrF = [self.sb(f"scrF{i}", [128, 512], F32) for i in range(4)]
        self.scrB = [self.sb(f"scrB{i}", [128, 512], BF16) for i in range(4)]
        self.ptb = [self.sb(f"ptb{i}", [128, 512], BF16) for i in range(6)]
        self.amb = [self.sb(f"amb{i}", [128, 128], BF16) for i in range(3)]
        self.ost = [self.sb(f"ost{i}", [128, 512], F32) for i in range(4)]
        self.xscr = self.sb("xscr", [128, 2080], F32)
        self.cpos = self.xscr[0:16, :]
        self.rope = self.sb("rope", [128, 2, TP], BF16)
        self.Rm = self.sb("Rm", [128, 128], BF16)
        self.trimask2 = self.sb("trimask2", [128, 64], BF16)
        self.trimask3 = self.sb("trimask3", [64, 32], BF16)
        self.ropeS = self.sb("ropeS", [128, 2, 64], BF16)
        self.segmask = self.sb("segmask", [128, 512], F32)
        self.vsink = self.sb("vsink", [1, 128], BF16)
        self.cga = self.sb("cga", [128, 10], F32)
        self.sink16 = self.sb("sink16", [1, 16], F32)
        self.sinkrow = self.sb("sinkrow", [1, 2, 512], BF16)
        self.sinkrow_m = self.sb("sinkrow_m", [1, 2, 64], BF16)
        self.wgb = self.sb("wgb", [16, 2, 256], BF16)
        self.pcol = self.sb("pcol", [128, 4], F32)
        self.invc = self.sb("invc", [128, 1], F32)
        self.kint = self.ost[3][:, :].bitcast(mybir.dt.int32)
        self.EL = self.sb("EL", [128, 33], F32)
        self.Sf = self.sb("Sf", [128, 128], F32)
        self.Sb = self.sb("Sb", [128, 128], BF16)
        self.ones_bf = self.sb("ones_bf", [128, 128], BF16)
        self.onesF = self.sb("onesF", [128, 512], F32)
        self.blockones = self.sb("blockones", [128, 128], BF16)
        self.trimask = self.sb("trimask", [128, 128], BF16)
        self.ident = self.sb("ident", [128, 128], F32)
        self.eps_col = self.sb("eps_col", [128, 1], F32)
        self.one_col = self.sb("one_col", [128, 1], F32)
        self.dummy = self.sb("fence_dummy", [128, 1], F32)
        self.gmix = self.sb("gmix", [128, DEPTH, KC], F32)
        self.gmlp = self.sb("gmlp", [128, DEPTH, KC], F32)
        self.cg = self.sb("cg", [128, 8], F32)
        self.ps = [self.es.enter_context(nc.psum_tensor(f"ps{i}", [128, 512], F32)) for i in range(8)]
        self.roles_mlp = {'pj': [0, 1, 2, 3], 'st': [4, 5], 'tr': [6, 7]}
        self.roles_mix = {'pj': [0, 1], 'sc': [2, 3], 'oa': [4, 5], 'st': [6, 7], 'tr': [7], 'sc4': [2, 3, 6, 7], 'su': [6, 7], 'pjw': [0, 1, 2, 3, 4, 5], 'oa4': [4, 5, 0, 1]}
        self.bank_roles = self.roles_mlp

        self.consts()
        if self.mixers and any(l % 2 == 0 for l in self.layers):
            self.ab_consts()

        tiles = [(0, NMETA)] + [(NMETA + 512 * i, 512) for i in range(4)]
        if self.do_sample:
            self.sample_pass()
        for s in range(ns if self.do_prompt else 0):
            P = Pass(); P.s = s; P.T = TP; P.tiles = tiles
            self.make_plan(P)
            self.load_x(P)
            for l in self.layers:
                if self.mixers:
                    self.bank_roles = self.roles_mix
                    if l % 2 == 1:
                        self.fox_layer(P, l)
                    else:
                        self.ab_layer(P, l)
                    self.bank_roles = self.roles_mlp
                if self.dbg and not self.do_mlp:
                    self.dma('sp', self.dbg_x[s], self.xT[:], [('x', kc, ti) for kc in range(KC) for ti in range(len(tiles))], [('dbgx', s)])
                if self.do_mlp:
                    self.mlp(P, l)
            if self.dbg and self.do_mlp:
                self.dma('sp', self.dbg_x[s], self.xT[:], [('x', kc, ti) for kc in range(KC) for ti in range(len(tiles))], [('dbgx', s)])
            self.store_y(P)
        self.emit()
        return nc

    def consts(self):
        self.memset('pool', self.ones_bf[:], 1.0, [('c', 'ones')])
        self.memset('pool', self.ident[:], 0.0, [('c', 'ident')])
        ident = self.ident
        self.R.op('pool', lambda e: e.affine_select(out=ident[:], in_=ident[:], pattern=[[-1, 128]], compare_op=ALU.not_equal,
                                                   fill=1.0, base=0, channel_multiplier=1), [('c', 'ident')], [('c', 'ident')])
        self.memset('pool', self.eps_col[:], EPS, [('c', 'eps')])
        self.memset('pool', self.onesF[:], 1.0, [('c', 'onesF')])
        self.memset('pool', self.one_col[:], 1.0, [('c', 'one')])
        self.memset('pool', self.blockones[:], 0.0, [('c', 'bo')])
        self.memset('pool', self.blockones[0:64, 0:64], 1.0, [('c', 'bo')])
        self.memset('pool', self.blockones[64:128, 64:128], 1.0, [('c', 'bo')])
        self.memset('pool', self.trimask[:], 1.0, [('c', 'tri')])
        tri = self.trimask
        self.R.op('pool', lambda e: e.affine_select(out=tri[:], in_=tri[:], pattern=[[1, 128]], compare_op=ALU.is_ge,
                                                   fill=0.0, base=0, channel_multiplier=-1), [('c', 'tri')], [('c', 'tri')])
        self.load_cols(self.gmix[:].rearrange("p l k -> p (l k)"), [(0, 128, self.norm_mix.rearrange("l (k p) -> (l k) p", p=128))], DEPTH * KC, 'gmix')
        self.load_cols(self.gmlp[:].rearrange("p l k -> p (l k)"), [(0, 128, self.norm_mlp.rearrange("l (k p) -> (l k) p", p=128))], DEPTH * KC, 'gmlp')
        self.load_cols(self.cg[:, 0:2], [(0, 64, self.qnorm_c[:, :]), (64, 128, self.qnorm_c[:, :])], 2, 'cg')
        self.load_cols(self.cg[:, 2:4], [(0, 64, self.knorm_c[:, :]), (64, 128, self.knorm_c[:, :])], 2, 'cg')
        self.load_cols(self.cg[:, 4:6], [(0, 16, self.b_f_c[:, :])], 2, 'cg', cols=16)
        self.ts('dve', self.cg[:, 0:2], self.cg[:, 0:2], SCALE, None, ALU.mult, None, [('c', 'cg')], [('c', 'cg')])
        self.ts('dve', self.cg[0:16, 4:6], self.cg[0:16, 4:6], -1.0, None, ALU.mult, None, [('c', 'cg')], [('c', 'cg')])

    def load_cols(self, dst, parts, r, name, cols=128):
        st, sk = self.obuf()
        for (c0, c1, src) in parts:
            self.dma('sp', st[0:r, c0:c1], src, [], [sk])
        bank = self.bank('tr')
        self.tr(self.ps[bank][0:cols, 0:r], st[0:r, 0:cols], self.ident[0:r, 0:r], [sk, ('c', 'ident')], [('ps', bank)])
        self.copy('dve', dst[0:cols, :] if cols != 128 else dst, self.ps[bank][0:cols, 0:r], [('ps', bank)], [('c', name)])

    def load_x(self, P):
        s = P.s
        if self.x_full:
            src_meta = self.x_full_in[s, 0:NMETA, :]
            src_fr = self.x_full_in[s, NMETA:TP, :]
        else:
            src_meta = self.meta_tokens[:, :]
            src_fr = self.x_prompt[s]
        for half in range(2):
            st, sk = self.obuf()
            self.dma('sp', st[0:NMETA, :], src_meta[:, half * 512:(half + 1) * 512], [], [sk])
            bank = self.bank('tr')
            for q in range(4):
                self.tr(self.ps[bank][:, q * 128:q * 128 + NMETA], st[0:NMETA, q * 128:(q + 1) * 128], self.ident[0:NMETA, 0:NMETA],
                        [sk, ('c', 'ident')], [('ps', bank)])
            self.copy('act' if half else 'dve', self.xT[:, half * 4:(half + 1) * 4, 0:NMETA],
                      self.ps[bank][:, :].rearrange("p (q t) -> p q t", q=4)[:, :, 0:NMETA],
                      [('ps', bank)], [('x', half * 4 + q, 0) for q in range(4)])
        for j in range(SEQ // 128):
            ti = 1 + j // 4
            t0 = NMETA + j * 128
            for half in range(2):
                st, sk = self.obuf()
                self.dma('sp', st[:], src_fr[j * 128:(j + 1) * 128, half * 512:(half + 1) * 512], [], [sk])
                bank = self.bank('tr')
                for q in range(4):
                    self.tr(self.ps[bank][:, q * 128:(q + 1) * 128], st[:, q * 128:(q + 1) * 128], self.ident[:],
                            [sk, ('c', 'ident')], [('ps', bank)])
                self.copy('act' if half else 'dve', self.xT[:, half * 4:(half + 1) * 4, t0:t0 + 128],
                          self.ps[bank][:, :].rearrange("p (q t) -> p q t", q=4),
                          [('ps', bank)], [('x', half * 4 + q, ti) for q in range(4)])

    def store_y(self, P):
        s = P.s
        for j in range(SEQ // 128):
            ti = 1 + j // 4
            t0 = NMETA + j * 128
            for half in range(2):
                st, sk = self.obuf()
                bank = self.bank('tr')
                for q in range(4):
                    kc = half * 4 + q
                    self.tr(self.ps[bank][:, q * 128:(q + 1) * 128], self.xT[:, kc, t0:t0 + 128], self.ident[:],
                            [('x', kc, ti), ('c', 'ident')], [('ps', bank)])
                self.copy('act' if half else 'dve', st[:], self.ps[bank][:, :], [('ps', bank)], [sk])
                self.dma('sp', self.y_prompt[s, j * 128:(j + 1) * 128, half * 512:(half + 1) * 512], st[:], [sk], [('y', s, j, half)])

    def rmsnorm(self, P, gtab, l):
        for ti, (t0, n) in enumerate(P.tiles):
            bank = self.bank('st')
            for kc in range(KC):
                sq, sqk = self.bbuf()
                self.act(sq[:, 0:n], self.xT[:, kc, t0:t0 + n], AF.Square, [('x', kc, ti)], [sqk])
                self.mm(self.ps[bank][:, 0:n], self.ones_bf[:, :], sq[:, 0:n], kc == 0, kc == KC - 1,
                        [sqk, ('c', 'ones')], [('ps', bank)])
            rs, rk = self.fbuf()
            self.act(rs[:, 0:n], self.ps[bank][:, 0:n], AF.Ln, [('ps', bank), ('c', 'eps')], [rk], bias=self.eps_col[:, 0:1], scale=1.0 / D)
            self.act(rs[:, 0:n], rs[:, 0:n], AF.Exp, [rk], [rk], scale=-0.5)
            for kc in range(KC):
                self.stt('dve', self.hT[:, kc, t0:t0 + n], self.xT[:, kc, t0:t0 + n], gtab[:, l, kc:kc + 1], rs[:, 0:n], ALU.mult, ALU.mult,
                         [('x', kc, ti), rk, ('c', 'gmix'), ('c', 'gmlp')], [('h', kc, ti)])

    def wblock(self, k, c, parts):
        return {'k': k, 'c': c, 'parts': parts}

    def make_plan(self, P, sample=False):
        plan = []
        for l in self.layers:
            i = l // 2
            if self.mixers and l % 2 == 1:
                W = self.w_in_c[i]
                r3 = lambda ap: ap.rearrange("(k p) c -> p k c", p=128)
                plan.append(self.wblock(8, 16, [(0, 8, 0, 16, r3(W[:, 3072:3088]))]))
                for hp in [h_ for _ in range(self.n_seq if sample else 1) for h_ in range(8)]:
                    plan.append(self.wblock(8, 384, [(0, 8, 128 * j, 128 * (j + 1), r3(W[:, 1024 * j + 128 * hp:1024 * j + 128 * (hp + 1)])) for j in range(3)]))
                    if hp % 2 == 1:
                        plan.append(self.wblock(2, 1024, [(0, 2, 0, 1024, r3(self.w_out_c[i, 128 * (hp - 1):128 * (hp + 1), :]))]))
            if self.mixers and l % 2 == 0:
                W = self.w_in_ab[i]
                r3 = lambda ap: ap.rearrange("(k p) c -> p k c", p=128)
                plan.append(self.wblock(8, 384, [(0, 8, 0, 128, r3(W[:, 512:640])), (0, 8, 128, 256, r3(W[:, 640:768])),
                                                 (0, 8, 256, 320, r3(W[:, 576:640])), (0, 8, 320, 384, r3(W[:, 512:576]))]))
                for h in range(2):
                    plan.append(self.wblock(8, 256, [(0, 8, 0, 256, r3(W[:, 256 * h:256 * (h + 1)]))]))
                    plan.append(self.wblock(2, 1024, [(0, 2, 0, 1024, r3(self.w_out_ab[i, 256 * h:256 * (h + 1), :]))]))
                for c in range(2):
                    plan.append(self.wblock(8, 272, [(0, 8, 0, 128, r3(W[:, 768 + 128 * c:768 + 128 * (c + 1)])),
                                                     (0, 8, 128, 256, r3(W[:, 1024 + 128 * c:1024 + 128 * (c + 1)])),
                                                     (0, 8, 256, 272, r3(W[:, 2304:2320]))]))
                    plan.append(self.wblock(8, 512, [(0, 8, 0, 256, r3(W[:, 1280 + 256 * c:1280 + 256 * (c + 1)])),
                                                     (0, 8, 256, 512, r3(W[:, 1792 + 256 * c:1792 + 256 * (c + 1)]))]))
                    plan.append(self.wblock(2, 1024, [(0, 2, 0, 1024, r3(self.w_out_ab[i, 512 + 256 * c:512 + 256 * (c + 1), :]))]))
            if self.do_mlp:
                NG = DFF // 512
                upb = lambda g: self.wblock(8, 512, [(0, 4, 0, 512, self.w_up[l, 0:512, g * 512:(g + 1) * 512].rearrange("(k p) c -> p k c", p=128)),
                                                     (4, 8, 0, 512, self.w_up[l, 512:1024, g * 512:(g + 1) * 512].rearrange("(k p) c -> p k c", p=128))])
                dnb = lambda g: self.wblock(4, 1024, [(0, 2, 0, 1024, self.w_down[l, g * 512:g * 512 + 256, :].rearrange("(k p) c -> p k c", p=128)),
                                                      (2, 4, 0, 1024, self.w_down[l, g * 512 + 256:(g + 1) * 512, :].rearrange("(k p) c -> p k c", p=128))])
                plan.append(upb(0))
                for g in range(1, NG):
                    plan.append(upb(g))
                    plan.append(dnb(g - 1))
                plan.append(dnb(NG - 1))
        self.plan = plan
        self.wi = 0
        self.wissued = 0

    def take(self, keep=1):
        limit = min(len(self.plan), self.wi + PF + 1, self.wi - keep + 1 + NW)
        while self.wissued < limit:
            blk = self.plan[self.wissued]
            slot = self.rr('wsl', NW)
            k, c = blk['k'], blk['c']
            view = self.wsl[slot][:, 0:k * c].rearrange("p (k c) -> p k c", k=k)
            blk['slot'] = slot; blk['view'] = view
            blk['keys'] = [('w', slot, pi) for pi in range(len(blk['parts']))]
            for pi, (k0, k1, c0, c1, src) in enumerate(blk['parts']):
                self.dma('pool', view[:, k0:k1, c0:c1], src, [], [('w', slot, pi)])
            self.wissued += 1
        assert self.wissued > self.wi
        blk = self.plan[self.wi]
        self.wi += 1
        return blk['keys'], blk['view']

    def mlp(self, P, l):
        self.rmsnorm(P, self.gmlp, l)
        self.fence()
        tiles = P.tiles
        NG = DFF // 512

        def up(g):
            wk, wv = self.take()
            u = self.ub[g % 2]
            for mc in range(4):
                for ti, (t0, n) in enumerate(tiles):
                    bank = self.bank('pj')
                    for kc in range(KC):
                        self.mm(self.ps[bank][:, 0:n], wv[:, kc, mc * 128:(mc + 1) * 128], self.hT[:, kc, t0:t0 + n], kc == 0, kc == KC - 1,
                                wk + [('h', kc, ti)], [('ps', bank)])
                    tb, tk = self.fbuf()
                    self.act(tb[:, 0:n], self.ps[bank][:, 0:n], AF.Relu, [('ps', bank)], [tk])
                    self.tt('dve', u[:, mc, t0:t0 + n], tb[:, 0:n], tb[:, 0:n], ALU.mult,
                            [tk], [self.ak(('u', g % 2, mc, ti))])

        def down(g):
            wk, wv = self.take()
            u = self.ub[g % 2]
            for dc in range(KC):
                for ti, (t0, n) in enumerate(tiles):
                    bank = self.bank('pj')
                    for kc in range(4):
                        self.mm(self.ps[bank][:, 0:n], wv[:, kc, dc * 128:(dc + 1) * 128], u[:, kc, t0:t0 + n], kc == 0, kc == 3,
                                wk + [('u', g % 2, kc, ti)], [('ps', bank)])
                    self.tt('dve', self.xT[:, dc, t0:t0 + n], self.xT[:, dc, t0:t0 + n], self.ps[bank][:, 0:n], ALU.add,
                            [('ps', bank), ('x', dc, ti)], [('x', dc, ti)])

        up(0)
        for g in range(1, NG):
            up(g)
            down(g - 1)
        down(NG - 1)

    def amul(self, out, in_, m, r, w):
        self.R.op('act', lambda e: e.mul(out=out, in_=in_, mul=m), r, w)

    def fox_layer(self, P, l):
        i = l // 2; s = P.s; tiles = P.tiles
        NT = len(tiles)
        self.rmsnorm(P, self.gmix, l)
        self.fence()
        QA, KA, VA, mixq = self.QA, self.KA, self.VA, self.mixq
        ident = ('c', 'ident')
        self.memset('pool', QA[64:70, :, :], 1.0, [self.ak(('QAa', 0)), self.ak(('QAa', 1))])
        self.memset('pool', KA[64:70, :, :], -1.0, [self.ak(('KAa', 0)), self.ak(('KAa', 1))])
        self.memset('pool', VA[:, :, :, 64:128], 1.0, [self.ak(('VAo',))])

        wk, wv = self.take()
        cp = self.cpos
        for ti, (t0, n) in enumerate(tiles):
            bank = self.bank('st')
            for kc in range(KC):
                self.mm(self.ps[bank][0:16, 0:n], wv[:, kc, 0:16], self.hT[:, kc, t0:t0 + n], kc == 0, kc == KC - 1,
                        wk + [('h', kc, ti)], [('ps', bank)])
            eb, ek = self.fbuf()
            self.act(eb[0:16, 0:n], self.ps[bank][0:16, 0:n], AF.Exp, [('ps', bank), ('c', 'cg')], [ek], bias=self.cg[0:16, 4 + i:5 + i], scale=-1.0)
            self.act(cp[0:16, t0:t0 + n], eb[0:16, 0:n], AF.Ln, [ek, ('c', 'one')], [self.ak(('cpos', ti))], bias=self.one_col[0:16, 0:1], scale=1.0)
        bank = self.bank('tr')
        self.tr(self.ps[bank][0:16, 0:16], cp[0:16, 0:16], self.ident[0:16, 0:16], [('cpos', 0), ident], [('ps', bank)])
        for j in range(16):
            tok0 = NMETA + 128 * j
            self.tr(self.ps[bank][:, 16 * (j + 1):16 * (j + 2)], cp[0:16, tok0:tok0 + 128], self.ident[0:16, 0:16],
                    [('cpos', 1 + j // 4), ident], [('ps', bank)])
        st, sk = self.obuf()
        self.amul(st[0:16, 0:16], self.ps[bank][0:16, 0:16], -1.0, [('ps', bank)], [sk])
        self.amul(st[:, 16:272], self.ps[bank][:, 16:272], -1.0, [('ps', bank)], [sk])
        self.dma('sp', self.o_cf[i, s, 0:16, :], st[0:16, 0:16], [sk], [('ocf', i, s, 0)])
        for hf in range(2):
            self.dma('sp', self.o_cf[i, s, 16 + 1024 * hf:16 + 1024 * (hf + 1), :].rearrange("(j t) h -> t j h", t=128),
                     st[:, 16 + 128 * hf:16 + 128 * (hf + 1)].rearrange("p (j h) -> p j h", h=16), [sk], [('ocf', i, s, 1 + hf)])
        xb = self.xscr
        cs = [xb[32 + 32 * j:48 + 32 * j, :].bitcast(BF16)[:, 0:TP] for j in range(3)]
        onesF = self.onesF
        for ti, (t0, n) in enumerate(tiles):
            c = cp[0:16, t0:t0 + n]
            init = 0.0 if ti == 0 else cp[0:16, t0 - 1:t0]
            rd = [('cpos', ti), ('c', 'onesF')] + ([('cpos', ti - 1)] if ti else [])
            self.R.op('dve', (lambda c=c, n=n, init=init: (lambda e: e.tensor_tensor_scan(out=c, data0=onesF[0:16, 0:n], data1=c, initial=init,
                                                                                          op0=ALU.mult, op1=ALU.add)))(), rd, [self.ak(('cpos', ti))])
        for ti, (t0, n) in enumerate(tiles):
            c = cp[0:16, t0:t0 + n]
            for j in range(2):
                mb, mk = self.bbuf()
                self.copy('dve', mb[0:16, 0:n], c, [self.ak(('cpos', ti))], [mk])
                self.copy('pool', cs[j][:, t0:t0 + n], mb[0:16, 0:n], [mk], [self.ak(('cs', ti))])
                self.tt('dve', c, c, mb[0:16, 0:n], ALU.subtract, [('cpos', ti), mk], [self.ak(('cpos', ti))])
            self.copy('dve', cs[2][:, t0:t0 + n], c, [self.ak(('cpos', ti))], [self.ak(('cs', ti))])
        cskeys = [('cs', ti) for ti in range(NT)]

        for hp in range(8):
            wk, wv = self.take()
            for hl in range(2):
                h = 2 * hp + hl
                for j in range(3):
                    self.dma('sp', KA[64 + j:65 + j, hl, :], cs[j][h:h + 1, :], cskeys, [self.ak(('KAa', hl))])
                    self.dma('sp', QA[67 + j:68 + j, hl, :], cs[j][h:h + 1, :], cskeys, [self.ak(('QAa', hl))])
            units = [(which, ti) for which in range(2) for ti in range(NT)]
            st1 = {}; st2 = {}

            def s1(u):
                which, ti = units[u]
                t0, n = tiles[ti]
                bank = self.bank('pjw')
                for kc in range(KC):
                    self.mm(self.ps[bank][:, 0:n], wv[:, kc, which * 128:(which + 1) * 128], self.hT[:, kc, t0:t0 + n], kc == 0, kc == KC - 1,
                            wk + [('h', kc, ti)], [('ps', bank)])
                sq, sqk = self.bbuf()
                self.act(sq[:, 0:n], self.ps[bank][:, 0:n], AF.Square, [('ps', bank)], [sqk])
                st1[u] = (bank, sq, sqk)

            def s2(u):
                which, ti = units[u]
                t0, n = tiles[ti]
                bank, sq, sqk = st1.pop(u)
                gcol = self.cg[:, which * 2 + i:which * 2 + i + 1]
                dstT = QA if which == 0 else KA
                kname = 'QA' if which == 0 else 'KA'
                b2 = self.bank('st')
                self.mm(self.ps[b2][:, 0:n], self.blockones[:, :], sq[:, 0:n], True, True, [sqk, ('c', 'bo')], [('ps', b2)])
                rs, rk = self.fbuf()
                self.act(rs[:, 0:n], self.ps[b2][:, 0:n], AF.Ln, [('ps', b2), ('c', 'eps')], [rk], bias=self.eps_col[:, 0:1], scale=1.0 / HD)
                self.act(rs[:, 0:n], rs[:, 0:n], AF.Exp, [rk], [rk], scale=-0.5)
                for hl in range(2):
                    p0 = 64 * hl
                    self.stt('dve', dstT[0:64, hl, t0:t0 + n], self.ps[bank][p0:p0 + 64, 0:n], gcol[p0:p0 + 64, :], rs[p0:p0 + 64, 0:n],
                             ALU.mult, ALU.mult, [('ps', bank), rk, ('c', 'cg')], [self.ak((kname, hl, ti))])
                if which == 1:
                    kf, kfk = self.fbuf()
                    self.stt('dve', kf[:, 0:n], self.ps[bank][:, 0:n], gcol, rs[:, 0:n], ALU.mult, ALU.mult,
                             [('ps', bank), rk, ('c', 'cg')], [kfk])
                    st2[u] = (kf, kfk)

            def s3(u):
                if u in st2:
                    which, ti = units[u]
                    t0, n = tiles[ti]
                    kf, kfk = st2.pop(u)
                    self.tm_out(self.o_ck[i, s], hp, t0, n, kf, kfk)

            NU = len(units)
            for u in range(NU + 2):
                if u < NU:
                    s1(u)
                if 0 <= u - 1 < NU:
                    s2(u - 1)
                if 0 <= u - 2 < NU:
                    s3(u - 2)
            for G in range(5):
                bank = self.bank('pjw')
                if G == 0:
                    toks = [(0, NMETA, 0)]
                else:
                    toks = [(NMETA + 128 * (4 * (G - 1) + g), 128, G) for g in range(4)]
                for g, (tok0, nt, ti) in enumerate(toks):
                    for kc in range(KC):
                        self.mm(self.ps[bank][0:nt, g * 128:(g + 1) * 128], self.hT[:, kc, tok0:tok0 + nt], wv[:, kc, 256:384], kc == 0, kc == KC - 1,
                                wk + [('h', kc, ti)], [('ps', bank)])
                st, sk = self.obuf()
                ov = self.o_cv[i, s]
                if G == 0:
                    self.copy('act', st[0:16, 0:128], self.ps[bank][0:16, 0:128], [('ps', bank)], [sk])
                    self.copy('dve', VA[0:16, 0, :, 0:64], self.ps[bank][0:16, 0:128].rearrange("p (h d) -> p h d", h=2), [('ps', bank)], [self.ak(('VA', 0))])
                    self.dma('sp', ov[0:16, 2 * hp:2 * hp + 2, :].rearrange("t h d -> t (h d)"), st[0:16, 0:128], [sk], [('ocv', i, s, hp, 0)])
                else:
                    self.copy('act', st[:], self.ps[bank][:, :], [('ps', bank)], [sk])
                    self.copy('dve', VA[:, 1 + 4 * (G - 1):1 + 4 * G, :, 0:64], self.ps[bank][:, :].rearrange("p (g h d) -> p g h d", g=4, h=2),
                              [('ps', bank)], [self.ak(('VA', G))])
                    tb = NMETA + 512 * (G - 1)
                    self.dma('sp', ov[tb:tb + 512, 2 * hp:2 * hp + 2, :].rearrange("(g t) h d -> t g (h d)", t=128),
                             st[:].rearrange("p (g f) -> p g f", g=4), [sk], [('ocv', i, s, hp, G)])
            for hl in range(2):
                self.fox_attend(P, hp, hl)
            if hp % 2 == 1:
                wk2, wo = self.take()
                for dc in range(KC):
                    for ti, (t0, n) in enumerate(tiles):
                        bank = self.bank('pjw')
                        for kc in range(2):
                            self.mm(self.ps[bank][:, 0:n], wo[:, kc, dc * 128:(dc + 1) * 128], mixq[:, kc, t0:t0 + n], kc == 0, kc == 1,
                                    wk2 + [('mq', kc, ti)], [('ps', bank)])
                        self.tt('dve', self.xT[:, dc, t0:t0 + n], self.xT[:, dc, t0:t0 + n], self.ps[bank][:, 0:n], ALU.add,
                                [('ps', bank), ('x', dc, ti)], [('x', dc, ti)])

    def tm_out(self, odram, hp, t0, n, src, srck):
        bank = self.bank('tr')
        st, sk = self.obuf()
        ident = ('c', 'ident')
        if n < 128:
            self.tr(self.ps[bank][0:n, 0:128], src[:, 0:n], self.ident[:, :], [srck, ident], [('ps', bank)])
            self.copy('act', st[0:n, 0:128], self.ps[bank][0:n, 0:128], [('ps', bank)], [sk])
            self.dma('sp', odram[t0:t0 + n, 2 * hp:2 * hp + 2, :].rearrange("t h d -> t (h d)"), st[0:n, 0:128], [sk], [('otm', id(odram), hp, t0)])
        else:
            for g in range(4):
                self.tr(self.ps[bank][:, g * 128:(g + 1) * 128], src[:, g * 128:(g + 1) * 128], self.ident[:, :], [srck, ident], [('ps', bank)])
            self.copy('act', st[:], self.ps[bank][:, :], [('ps', bank)], [sk])
            self.dma('sp', odram[t0:t0 + 512, 2 * hp:2 * hp + 2, :].rearrange("(g t) h d -> t g (h d)", t=128),
                     st[:].rearrange("p (g f) -> p g f", g=4), [sk], [('otm', id(odram), hp, t0)])

    def fox_attend(self, P, hp, hl):
        QA, KA, VA, mixq = self.QA, self.KA, self.VA, self.mixq
        qts = [(0, NMETA)] + [(NMETA + 512 * i, 512) for i in range(4)]
        for qi, (q0, qn) in enumerate(qts):
            if qi == 0:
                kts = [(0, 0, NMETA, 0, True, 0, 0)]
            else:
                kts = [(0, 0, NMETA, 0, False, 0, 0)]
                for j in range(4 * (qi - 1)):
                    kts.append((1 + j, NMETA + 128 * j, 128, 0, False, 1 + j // 4, 1 + j // 4))
                for d in range(4):
                    j = 4 * (qi - 1) + d
                    kts.append((1 + j, NMETA + 128 * j, 128, 128 * d, True, qi, qi))
            ob = self.bank('oa4')
            nk_t = len(kts)
            pend = []
            LOOK = 4

            def pv(item, idx):
                (vt, tok0, nk, co, diag, pti, vg), pt, ptk = item
                self.mm(self.ps[ob][:, co:qn], VA[0:nk, vt, hl, :], pt[0:nk, co:qn], idx == 0, idx == nk_t - 1,
                        [ptk, ('VA', vg), ('VAo',)], [('ps', ob)])

            for idx, kt in enumerate(kts):
                (vt, tok0, nk, co, diag, pti, vg) = kt
                sbk = self.bank('sc4')
                self.mm(self.ps[sbk][0:nk, co:qn], KA[0:70, hl, tok0:tok0 + nk], QA[0:70, hl, q0 + co:q0 + qn], True, True,
                        [('KA', hl, pti), ('KAa', hl), ('QA', hl, qi), ('QAa', hl)], [('ps', sbk)])
                pt, ptk = self.pbuf()
                self.act(pt[0:nk, co:qn], self.ps[sbk][0:nk, co:qn], AF.Exp, [('ps', sbk)], [ptk])
                if diag:
                    w = min(128, qn - co)
                    self.tt('dve', pt[0:nk, co:co + w], pt[0:nk, co:co + w], self.trimask[0:nk, 0:w], ALU.mult, [ptk, ('c', 'tri')], [ptk])
                pend.append(((kt, pt, ptk), idx))
                if len(pend) > LOOK:
                    it, ix = pend.pop(0)
                    pv(it, ix)
            for it, ix in pend:
                pv(it, ix)
            rc, rck = self.fbuf()
            self.recip(rc[0:64, 0:qn], self.ps[ob][64:128, 0:qn], [('ps', ob)], [rck])
            self.tt('dve', mixq[64 * hl:64 * hl + 64, hp % 2, q0:q0 + qn], self.ps[ob][0:64, 0:qn], rc[0:64, 0:qn], ALU.mult,
                    [('ps', ob), rck], [self.ak(('mq', hp % 2, qi))])

    def xbuf(self):
        i = self.rr('xscr', 4)
        return self.xscr[:, i * 512:(i + 1) * 512], self.ak(('xs', i))

    def ab_consts(self):
        pcol = self.pcol; invc = self.invc
        self.R.op('pool', lambda e: e.iota(pcol[:, 0:1], pattern=[[0, 1]], base=0, channel_multiplier=1, allow_small_or_imprecise_dtypes=True), [], [('c', 'pcol')])
        ki = self.kint
        self.ts('dve', pcol[:, 1:2], pcol[:, 0:1], -15.5, 1.0 / 32, ALU.add, ALU.mult, [('c', 'pcol')], [('c', 'pcol1')])
        self.copy('dve', ki[:, 0:1], pcol[:, 1:2], [('c', 'pcol1')], [('ost', 3)])
        self.copy('dve', pcol[:, 2:3], ki[:, 0:1], [('ost', 3)], [('c', 'pcol2')])
        self.stt('dve', pcol[:, 3:4], pcol[:, 2:3], -32.0, pcol[:, 0:1], ALU.mult, ALU.add, [('c', 'pcol2'), ('c', 'pcol')], [('c', 'pcol3')])
        self.act(invc[:, 0:1], pcol[:, 3:4], AF.Exp, [('c', 'pcol3')], [('c', 'inv')], scale=-math.log(10000.0) / 32)
        for ti in range(5):
            t0, n = (0, NMETA) if ti == 0 else (NMETA + 512 * (ti - 1), 512)
            pos, pk = self.fbuf()
            self.R.op('pool', (lambda pos=pos, t0=t0, n=n: (lambda e: e.iota(pos[:, 0:n], pattern=[[1, n]], base=t0, channel_multiplier=0,
                                                                             allow_small_or_imprecise_dtypes=True)))(), [], [pk])
            ang, ak_ = self.fbuf()
            self.ts('dve', ang[:, 0:n], pos[:, 0:n], invc[:, 0:1], None, ALU.mult, None, [pk, ('c', 'inv')], [ak_])
            for which in (1, 0):
                tq, tk = self.fbuf()
                off = 0.0 if which == 1 else math.pi / 2
                self.ts('dve', tq[:, 0:n], ang[:, 0:n], off, 1.0 / (2 * math.pi), ALU.add, ALU.mult, [ak_], [tk])
                self.copy('dve', ki[:, 0:n], tq[:, 0:n], [tk], [('ost', 3)])
                self.copy('dve', tq[:, 0:n], ki[:, 0:n], [('ost', 3)], [tk])
                self.stt('dve', tq[:, 0:n], tq[:, 0:n], -2 * math.pi, ang[:, 0:n], ALU.mult, ALU.add, [tk, ak_], [tk])
                if which == 0:
                    self.ts('dve', tq[:, 0:n], tq[:, 0:n], math.pi / 2, None, ALU.add, None, [tk], [tk])
                self.ts('dve', tq[:, 0:n], tq[:, 0:n], 3.14159, -3.14159, ALU.min, ALU.max, [tk], [tk])
                self.act(self.rope[:, which, t0:t0 + n], tq[:, 0:n], AF.Sin, [tk], [('c', 'rope', ti)])
        pos, pk = self.fbuf()
        self.R.op('pool', lambda e: e.iota(pos[:, 0:32], pattern=[[1, 32]], base=NMETA + 2048, channel_multiplier=0, allow_small_or_imprecise_dtypes=True), [], [pk])
        ang, ak_ = self.fbuf()
        self.ts('dve', ang[:, 0:32], pos[:, 0:32], invc[:, 0:1], None, ALU.mult, None, [pk, ('c', 'inv')], [ak_])
        for which in (1, 0):
            tq, tk = self.fbuf()
            off = 0.0 if which == 1 else math.pi / 2
            self.ts('dve', tq[:, 0:32], ang[:, 0:32], off, 1.0 / (2 * math.pi), ALU.add, ALU.mult, [ak_], [tk])
            self.copy('dve', ki[:, 0:32], tq[:, 0:32], [tk], [('ost', 3)])
            self.copy('dve', tq[:, 0:32], ki[:, 0:32], [('ost', 3)], [tk])
            self.stt('dve', tq[:, 0:32], tq[:, 0:32], -2 * math.pi, ang[:, 0:32], ALU.mult, ALU.add, [tk, ak_], [tk])
            if which == 0:
                self.ts('dve', tq[:, 0:32], tq[:, 0:32], math.pi / 2, None, ALU.add, None, [tk], [tk])
            self.ts('dve', tq[:, 0:32], tq[:, 0:32], 3.14159, -3.14159, ALU.min, ALU.max, [tk], [tk])
            for rep in range(2):
                self.act(self.ropeS[:, which, 32 * rep:32 * (rep + 1)], tq[:, 0:32], AF.Sin, [tk], [('c', 'ropeS')])
        t3 = self.trimask3
        self.memset('pool', t3[:], 1.0, [('c', 'tri3')])
        self.R.op('pool', lambda e: e.affine_select(out=t3[:], in_=t3[:], pattern=[[1, 32]], compare_op=ALU.is_ge, fill=0.0, base=32,
                                                   channel_multiplier=-1), [('c', 'tri3')], [('c', 'tri3')])
        self.copy('pool', t3[0:32, :], self.trimask[0:32, 0:32], [('c', 'tri')], [('c', 'tri3')])
        rm = self.Rm
        self.memset('pool', rm[:], 0.0, [('c', 'rm')])
        self.R.op('pool', lambda e: e.affine_select(out=rm[:], in_=rm[:], pattern=[[-1, 128]], compare_op=ALU.not_equal, fill=-1.0, base=-32,
                                                   channel_multiplier=1), [('c', 'rm')], [('c', 'rm')])
        self.R.op('pool', lambda e: e.affine_select(out=rm[:], in_=rm[:], pattern=[[-1, 128]], compare_op=ALU.not_equal, fill=1.0, base=32,
                                                   channel_multiplier=1), [('c', 'rm')], [('c', 'rm')])
        self.tt('pool', rm[:], rm[:], self.blockones[:], ALU.mult, [('c', 'rm'), ('c', 'bo')], [('c', 'rm')])
        t2 = self.trimask2
        self.memset('pool', t2[:], 1.0, [('c', 'tri2')])
        self.R.op('pool', lambda e: e.affine_select(out=t2[:], in_=t2[:], pattern=[[1, 64]], compare_op=ALU.is_ge, fill=0.0, base=64,
                                                   channel_multiplier=-1), [('c', 'tri2')], [('c', 'tri2')])
        self.copy('pool', t2[0:64, :], self.trimask[0:64, 0:64], [('c', 'tri')], [('c', 'tri2')])
        sm = self.segmask
        self.memset('pool', sm[:], 1.0, [('c', 'seg')])
        self.memset('pool', sm[:].rearrange("p (a b) -> p a b", b=64)[:, :, 0:1], 0.0, [('c', 'seg')])
        self.memset('pool', self.vsink[0:1, 0:64], 0.0, [('c', 'vsink')])
        self.memset('pool', self.vsink[0:1, 64:128], 1.0, [('c', 'vsink')])
        self.load_cols(self.cga[:, 0:2], [(0, 64, self.qnorm_a[:, :]), (64, 128, self.qnorm_a[:, :])], 2, 'cga')
        self.load_cols(self.cga[:, 2:4], [(0, 64, self.knorm_a[:, :]), (64, 128, self.knorm_a[:, :])], 2, 'cga')
        self.ts('dve', self.cga[:, 0:2], self.cga[:, 0:2], SCALE, None, ALU.mult, None, [('c', 'cga')], [('c', 'cga')])
        self.load_cols(self.cga[:, 4:8], [(0, 128, self.b_gate_b.rearrange("i (c p) -> (i c) p", p=128))], 4, 'cga')
        self.ts('dve', self.cga[:, 4:8], self.cga[:, 4:8], -1.0, None, ALU.mult, None, [('c', 'cga')], [('c', 'cga')])
        self.load_cols(self.cga[:, 8:10], [(0, 128, self.onorm_b[:, :])], 2, 'cga')
        self.dma('sp', self.sink16[0:1, :], self.sink_a.rearrange("(o i) h -> o (i h)", o=1), [], [('c', 'sink16')])
        self.dma('pool', self.wgb[:, :, :], self.w_gate_b.rearrange("i r f -> r i f"), [], [('c', 'wgb')])

    def ab_layer(self, P, l):
        i = l // 2; s = P.s; tiles = P.tiles
        NT = len(tiles)
        self.rmsnorm(P, self.gmix, l)
        self.fence()
        QA, KA, VA, mixq = self.QA, self.KA, self.VA, self.mixq
        ident = ('c', 'ident')
        self.memset('pool', VA[:, :, :, 64:128], 1.0, [self.ak(('VAo',))])
        for h in range(2):
            for p in range(2):
                for c in range(2):
                    idx = 8 * i + 4 * h + 2 * c + p
                    g = p * 2 + c
                    self.act(self.sinkrow[0:1, h, g * 128:(g + 1) * 128], self.onesF[0:1, 0:128], AF.Exp, [('c', 'onesF'), ('c', 'sink16')],
                             [('sinkrow', h)], bias=self.sink16[0:1, idx:idx + 1], scale=0.0)
                    self.act(self.sinkrow_m[0:1, h, g * 16:(g + 1) * 16], self.onesF[0:1, 0:16], AF.Exp, [('c', 'onesF'), ('c', 'sink16')],
                             [('sinkrow', h)], bias=self.sink16[0:1, idx:idx + 1], scale=0.0)
        wk, wv = self.take()
        gk = self.cga[:, 2 + i:3 + i]
        kunits = []
        for a in range(2):
            for ti, (t0, n) in enumerate(tiles):
                outf = True if (a == 0 and ti == NT - 1) else None
                kunits.append(dict(wk=wk, wv=wv[:, :, (0 if a == 0 else 256):(128 if a == 0 else 384)], ti=ti, t0=t0, n=n, gcol=gk,
                                   dst=KA[:, a, t0:t0 + n], dkey=self.ak(('KA', a, ti)), outf=outf, ropetab=None))
        self.qk_rope_many(P, kunits, i)
        for G in range(5):
            bank = self.bank('pj')
            if G == 0:
                toks = [(0, NMETA, 0)]
            else:
                toks = [(NMETA + 128 * (4 * (G - 1) + g), 128, G) for g in range(4)]
            for g, (tok0, nt, ti) in enumerate(toks):
                for kc in range(KC):
                    self.mm(self.ps[bank][0:nt, g * 128:(g + 1) * 128], self.hT[:, kc, tok0:tok0 + nt], wv[:, kc, 128:256], kc == 0, kc == KC - 1,
                            wk + [('h', kc, ti)], [('ps', bank)])
            if G == 0:
                self.copy('dve', VA[0:16, 0, :, 0:64], self.ps[bank][0:16, 0:128].rearrange("p (h d) -> p h d", h=2), [('ps', bank)], [self.ak(('VA', 0))])
            else:
                self.copy('dve', VA[:, 1 + 4 * (G - 1):1 + 4 * G, :, 0:64], self.ps[bank][:, :].rearrange("p (g h d) -> p g h d", g=4, h=2),
                          [('ps', bank)], [self.ak(('VA', G))])
                if G == 4:
                    st, sk = self.obuf()
                    self.copy('act', st[:, 0:128], self.ps[bank][:, 384:512], [('ps', bank)], [sk])
                    self.dma('sp', self.o_av[i, s].rearrange("t h d -> t (h d)"), st[:, 0:128], [sk], [('oav', i, s)])
        gq = self.cga[:, i:i + 1]
        for h in range(2):
            wkq, wq = self.take()
            qunits = []
            for c in range(2):
                for ti, (t0, n) in enumerate(tiles):
                    qunits.append(dict(wk=wkq, wv=wq[:, :, 128 * c:128 * (c + 1)], ti=ti, t0=t0, n=n, gcol=gq, dst=QA[:, c, t0:t0 + n],
                                       dkey=self.ak(('QA', c, ti)), outf=None, ropetab=None))
            self.qk_rope_many(P, qunits, i)
            self.swa_attend(P, h)
            self.out_quad(P, self.take(), tiles)
        self.fence()
        for c in range(2):
            self.gla_pair(P, l, c)
            self.out_quad(P, self.take(), tiles)

    def out_quad(self, P, blk, tiles):
        wk2, wo = blk
        if self.dbg:
            qi_ = getattr(self, '_dbgq', 0)
            self._dbgq = qi_ + 1
            if qi_ < 4:
                self.dma('sp', self.dbg_mix[qi_], self.mixq[:, :, :], [('mq', kc, ti) for kc in range(2) for ti in range(len(tiles))], [('dbgmix', qi_)])
        for dc in range(KC):
            for ti, (t0, n) in enumerate(tiles):
                bank = self.bank('pjw')
                for kc in range(2):
                    self.mm(self.ps[bank][:, 0:n], wo[:, kc, dc * 128:(dc + 1) * 128], self.mixq[:, kc, t0:t0 + n], kc == 0, kc == 1,
                            wk2 + [('mq', kc, ti)], [('ps', bank)])
                self.tt('dve', self.xT[:, dc, t0:t0 + n], self.xT[:, dc, t0:t0 + n], self.ps[bank][:, 0:n], ALU.add,
                        [('ps', bank), ('x', dc, ti)], [('x', dc, ti)])

    def qk_rope_many(self, P, units, i):
        sa = {}; sb_ = {}

        def A(u):
            wk, wv, ti, t0, n = u['wk'], u['wv'], u['ti'], u['t0'], u['n']
            bank = self.bank('pjw')
            for kc in range(KC):
                self.mm(self.ps[bank][:, 0:n], wv[:, kc, :], self.hT[:, kc, t0:t0 + n], kc == 0, kc == KC - 1, wk + [('h', kc, ti)], [('ps', bank)])
            sq, sqk = self.bbuf()
            self.act(sq[:, 0:n], self.ps[bank][:, 0:n], AF.Square, [('ps', bank)], [sqk])
            sa[id(u)] = (bank, sq, sqk)

        def B(u):
            n = u['n']
            bank, sq, sqk = sa.pop(id(u))
            b2 = self.bank('st')
            self.mm(self.ps[b2][:, 0:n], self.blockones[:, :], sq[:, 0:n], True, True, [sqk, ('c', 'bo')], [('ps', b2)])
            rs, rk = self.fbuf()
            self.act(rs[:, 0:n], self.ps[b2][:, 0:n], AF.Ln, [('ps', b2), ('c', 'eps')], [rk], bias=self.eps_col[:, 0:1], scale=1.0 / HD)
            self.act(rs[:, 0:n], rs[:, 0:n], AF.Exp, [rk], [rk], scale=-0.5)
            qn, qk_ = self.xbuf()
            self.stt('dve', qn[:, 0:n], self.ps[bank][:, 0:n], u['gcol'], rs[:, 0:n], ALU.mult, ALU.mult, [('ps', bank), rk, ('c', 'cga')], [qk_])
            qb, qbk = self.bbuf()
            self.copy('act', qb[:, 0:n], qn[:, 0:n], [qk_], [qbk])
            sb_[id(u)] = (qn, qk_, qb, qbk)

        def C(u):
            ti, t0, n, dst, dkey, outf, ropetab = u['ti'], u['t0'], u['n'], u['dst'], u['dkey'], u['outf'], u['ropetab']
            qn, qk_, qb, qbk = sb_.pop(id(u))
            b3 = self.bank('st')
            self.mm(self.ps[b3][:, 0:n], self.Rm[:, :], qb[:, 0:n], True, True, [qbk, ('c', 'rm')], [('ps', b3)])
            if ropetab is None:
                rcos, rsin, rkeys = self.rope[:, 0, t0:t0 + n], self.rope[:, 1, t0:t0 + n], [('c', 'rope', ti)]
            else:
                rcos, rsin, rkeys = ropetab
            self.tt('pool', qn[:, 0:n], qn[:, 0:n], rcos, ALU.mult, [qk_] + rkeys, [qk_])
            rb_, rbk = self.fbuf()
            self.tt('dve', rb_[:, 0:n], self.ps[b3][:, 0:n], rsin, ALU.mult, [('ps', b3)] + rkeys, [rbk])
            self.tt('dve', dst, qn[:, 0:n], rb_[:, 0:n], ALU.add, [qk_, rbk], [dkey])
            if outf == 'sample':
                kf, kfk = self.fbuf()
                self.tt('dve', kf[:, 0:n], qn[:, 0:n], rb_[:, 0:n], ALU.add, [qk_, rbk], [kfk])
                bank2 = self.bank('tr')
                self.tr(self.ps[bank2][0:n, 0:128], kf[:, 0:n], self.ident[:, :], [kfk, ('c', 'ident')], [('ps', bank2)])
                st, sk = self.obuf()
                self.copy('act', st[0:n, 0:128], self.ps[bank2][0:n, 0:128], [('ps', bank2)], [sk])
                for b in range(self.n_seq):
                    self.dma('sp', self.o_aks[i, b, 96:128].rearrange("t h d -> t (h d)"), st[32 * b:32 * b + 32, 0:128], [sk], [('oaks', i, b)])
            elif outf:
                kf, kfk = self.fbuf()
                self.tt('dve', kf[:, 0:128], qn[:, n - 128:n], rb_[:, n - 128:n], ALU.add, [qk_, rbk], [kfk])
                bank2 = self.bank('tr')
                self.tr(self.ps[bank2][:, 0:128], kf[:, 0:128], self.ident[:, :], [kfk, ('c', 'ident')], [('ps', bank2)])
                st, sk = self.obuf()
                self.copy('act', st[:, 0:128], self.ps[bank2][:, 0:128], [('ps', bank2)], [sk])
                self.dma('sp', self.o_ak[i, P.s].rearrange("t h d -> t (h d)"), st[:, 0:128], [sk], [('oak', i, P.s)])

        NU = len(units)
        for j in range(NU + 2):
            if j < NU:
                A(units[j])
            if 0 <= j - 1 < NU:
                B(units[j - 1])
            if 0 <= j - 2 < NU:
                C(units[j - 2])

    def qk_rope(self, P, wk, wv, ti, t0, n, gcol, dst, dkey, outf, i, ropetab=None):
        self.qk_rope_many(P, [dict(wk=wk, wv=wv, ti=ti, t0=t0, n=n, gcol=gcol, dst=dst, dkey=dkey, outf=outf, ropetab=ropetab)], i)

    def swa_attend(self, P, h):
        QA, KA, VA, mixq = self.QA, self.KA, self.VA, self.mixq
        tl = []
        for qj in range(-1, 16):
            if qj < 0:
                tl.append(dict(q0=0, nq=NMETA, qti=0, kts=[(0, 0, NMETA, None, 0)], meta=True))
            else:
                q0, nq, qti = NMETA + 128 * qj, 128, 1 + qj // 4
                if qj == 0:
                    kts = [(0, 0, NMETA, None, 0), (1, NMETA, 128, (64, 0), 1)]
                else:
                    kts = [(qj, NMETA + 128 * (qj - 1), 128, (0, 64), 1 + (qj - 1) // 4), (1 + qj, NMETA + 128 * qj, 128, (64, 0), 1 + qj // 4)]
                tl.append(dict(q0=q0, nq=nq, qti=qti, kts=kts, meta=False))

        def X(t):
            q0, nq, qti = t['q0'], t['nq'], t['qti']
            W = 4 * nq
            pts = []
            for idx, (vt, tok0, nk, zq, kti) in enumerate(t['kts']):
                pt, ptk = self.pbuf()
                for p in range(2):
                    a = h ^ p
                    sbk = self.bank('sc4')
                    self.mm(self.ps[sbk][0:nk, 0:2 * nq], KA[64 * p:64 * p + 64, a, tok0:tok0 + nk], QA[64 * p:64 * p + 64, :, q0:q0 + nq],
                            True, True, [('KA', a, kti), ('QA', 0, qti), ('QA', 1, qti)], [('ps', sbk)], tile_position=(64 * p, 0))
                    self.act(pt[0:nk, p * 2 * nq:(p + 1) * 2 * nq], self.ps[sbk][0:nk, 0:2 * nq], AF.Exp, [('ps', sbk)], [ptk])
                if zq is not None:
                    r0, c0 = zq
                    self.memset('pool', pt[r0:r0 + 64, 0:W].rearrange("p (g q) -> p g q", q=nq)[:, :, c0:c0 + 64], 0.0, [ptk])
                pts.append((pt, ptk))
            t['pts'] = pts

        def Y(t):
            q0, nq, qti = t['q0'], t['nq'], t['qti']
            W = 4 * nq
            ob = self.bank('oa4')
            for idx, (vt, tok0, nk, zq, kti) in enumerate(t['kts']):
                pt, ptk = t['pts'][idx]
                self.mm(self.ps[ob][:, 0:W], VA[0:nk, vt, h, :], pt[0:nk, 0:W], idx == 0, False, [ptk, ('VA', kti), ('VAo',)], [('ps', ob)])
            srow = self.sinkrow_m[0:1, h, 0:W] if t['meta'] else self.sinkrow[0:1, h, 0:W]
            self.mm(self.ps[ob][:, 0:W], self.vsink[0:1, 0:128], srow, False, True, [('sinkrow', h), ('c', 'vsink')], [('ps', ob)])
            rc, rck = self.fbuf()
            self.recip(rc[0:64, 0:W], self.ps[ob][64:128, 0:W], [('ps', ob)], [rck], eng='act')
            for p in range(2):
                self.tt('dve', mixq[64 * p:64 * p + 64, :, q0:q0 + nq], self.ps[ob][0:64, p * 2 * nq:(p + 1) * 2 * nq].rearrange("p (c q) -> p c q", c=2),
                        rc[0:64, p * 2 * nq:(p + 1) * 2 * nq].rearrange("p (c q) -> p c q", c=2), ALU.mult,
                        [('ps', ob), rck], [self.ak(('mq', 0, qti)), self.ak(('mq', 1, qti))])

        X(tl[0])
        for ix, t in enumerate(tl):
            if ix + 1 < len(tl):
                X(tl[ix + 1])
            Y(t)

    def gla_pair(self, P, l, c):
        i = l // 2; s = P.s; tiles = P.tiles
        ar = self.arena
        QG = ar[:, 0:TP]; KG = ar[:, TP:2 * TP]
        KT = ar[:, 2 * TP:2 * TP + 17 * 128].rearrange("p (j f) -> p j f", j=17)
        o = 2 * TP + 17 * 128
        VB = ar[:, o:o + 17 * 256].rearrange("p (j f) -> p j f", j=17)
        mixq = self.mixq
        EL = self.EL; Sf = self.Sf; Sb = self.Sb
        wk1, w1 = self.take()
        wk2, w2 = self.take(keep=2)
        negbg = self.cga[:, 4 + 2 * i + c:5 + 2 * i + c]
        onc = self.cga[:, 8 + i:9 + i]
        self.memset('pool', Sf[:, :], 0.0, [('Sf',)])
        self.memset('pool', Sb[:, :], 0.0, [('Sb',)])
        pa = {}; pb = {}

        def A1(ti):
            t0, n = tiles[ti]
            b_g = self.bank('st')
            for kc in range(KC):
                self.mm(self.ps[b_g][0:16, 0:n], w1[:, kc, 256:272], self.hT[:, kc, t0:t0 + n], kc == 0, kc == KC - 1, wk1 + [('h', kc, ti)], [('ps', b_g)])
            gl, glk = self.bbuf()
            self.copy('dve', gl[0:16, 0:n], self.ps[b_g][0:16, 0:n], [('ps', b_g)], [glk])
            b_q = self.bank('pjw')
            for kc in range(KC):
                self.mm(self.ps[b_q][:, 0:n], w1[:, kc, 0:128], self.hT[:, kc, t0:t0 + n], kc == 0, kc == KC - 1, wk1 + [('h', kc, ti)], [('ps', b_q)])
            b_k = self.bank('pjw')
            for kc in range(KC):
                self.mm(self.ps[b_k][:, 0:n], w1[:, kc, 128:256], self.hT[:, kc, t0:t0 + n], kc == 0, kc == KC - 1, wk1 + [('h', kc, ti)], [('ps', b_k)])
            pa[ti] = (gl, glk, b_q, b_k)

        def B1(ti):
            t0, n = tiles[ti]
            cw = 64 if n >= 64 else n
            nchk = n // cw
            gl, glk, b_q, b_k = pa.pop(ti)
            b_z = self.bank('st')
            self.mm(self.ps[b_z][:, 0:n], self.wgb[0:16, i, 128 * c:128 * (c + 1)], gl[0:16, 0:n], True, True, [glk, ('c', 'wgb')], [('ps', b_z)])
            cum, cumk = self.xbuf()
            self.act(cum[:, 0:n], self.ps[b_z][:, 0:n], AF.Exp, [('ps', b_z), ('c', 'cga')], [cumk], bias=negbg, scale=-1.0)
            self.act(cum[:, 0:n], cum[:, 0:n], AF.Ln, [cumk, ('c', 'one')], [cumk], bias=self.one_col[:, 0:1], scale=1.0)
            self.scan(cum[:, 0:n], self.segmask[:, 0:n], cum[:, 0:n], 0.0, [cumk, ('c', 'seg')], [cumk])
            eb, ebk = self.xbuf()
            self.act(eb[:, 0:n], cum[:, 0:n], AF.Exp, [cumk], [ebk], scale=-1.0 / 16)
            c0 = 0 if ti == 0 else 1 + 8 * (ti - 1)
            self.copy('dve', EL[:, c0:c0 + nchk], eb[:, 0:n].rearrange("p (a b) -> p a b", b=cw)[:, :, cw - 1], [ebk], [('EL', ti)])
            self.stt('dve', QG[:, t0:t0 + n], self.ps[b_q][:, 0:n], SCALE, eb[:, 0:n], ALU.mult, ALU.mult, [('ps', b_q), ebk], [self.ak(('QG', ti))])
            en, enk = self.xbuf()
            self.act(en[:, 0:n], cum[:, 0:n], AF.Exp, [cumk], [enk], scale=1.0 / 16)
            self.tt('dve', KG[:, t0:t0 + n], self.ps[b_k][:, 0:n], en[:, 0:n], ALU.mult, [('ps', b_k), enk], [self.ak(('KG', ti))])
            c3 = cum[:, 0:n].rearrange("p (a b) -> p a b", b=cw)
            self.tt('dve', c3, c3, c3[:, :, cw - 1:cw].to_broadcast([128, nchk, cw]), ALU.subtract, [cumk], [cumk])
            self.act(en[:, 0:n], cum[:, 0:n], AF.Exp, [cumk, enk], [enk], scale=1.0 / 16)
            kh, khk = self.fbuf()
            self.tt('dve', kh[:, 0:n], self.ps[b_k][:, 0:n], en[:, 0:n], ALU.mult, [('ps', b_k), enk], [khk])
            pb[ti] = (kh, khk)

        def C1(ti):
            t0, n = tiles[ti]
            kh, khk = pb.pop(ti)
            b_t = self.bank('tr')
            if ti == 0:
                self.tr(self.ps[b_t][0:n, 0:128], kh[:, 0:n], self.ident[:, :], [khk, ('c', 'ident')], [('ps', b_t)])
                self.copy('act', KT[0:n, 0, :], self.ps[b_t][0:n, 0:128], [('ps', b_t)], [self.ak(('KT', 0))])
            else:
                for g in range(4):
                    self.tr(self.ps[b_t][:, g * 128:(g + 1) * 128], kh[:, g * 128:(g + 1) * 128], self.ident[:, :], [khk, ('c', 'ident')], [('ps', b_t)])
                self.copy('act', KT[:, 1 + 4 * (ti - 1):1 + 4 * ti, :], self.ps[b_t][:, :].rearrange("p (g f) -> p g f", g=4), [('ps', b_t)], [self.ak(('KT', ti))])

        NTl = len(tiles)
        for j in range(NTl + 2):
            if j < NTl:
                A1(j)
            if 0 <= j - 1 < NTl:
                B1(j - 1)
            if 0 <= j - 2 < NTl:
                C1(j - 2)
        for G in range(9):
            bank = self.bank('pjw')
            if G == 0:
                toks = [(0, NMETA, 0, 0)]
            else:
                toks = [(NMETA + 128 * (2 * (G - 1) + g), 128, 1 + (2 * (G - 1) + g) // 4, 1 + 2 * (G - 1) + g) for g in range(2)]
            for g, (tok0, nt, ti, vt) in enumerate(toks):
                for kc in range(KC):
                    self.mm(self.ps[bank][0:nt, g * 256:(g + 1) * 256], self.hT[:, kc, tok0:tok0 + nt], w2[:, kc, 0:256], kc == 0, kc == KC - 1,
                            wk2 + [('h', kc, ti)], [('ps', bank)])
            if G == 0:
                self.copy('act', VB[0:NMETA, 0, :], self.ps[bank][0:NMETA, 0:256], [('ps', bank)], [self.ak(('VB', 0))])
            else:
                self.copy('act' if G % 2 else 'dve', VB[:, 1 + 2 * (G - 1):1 + 2 * G, :], self.ps[bank][:, :].rearrange("p (g f) -> p g f", g=2),
                          [('ps', bank)], [self.ak(('VB', G))])
        roles = self.bank_roles
        at_bs = roles['sc']
        chunks = []
        for ti, (t0, n) in enumerate(tiles):
            for q in range(1 if ti == 0 else 8):
                if ti == 0:
                    chunks.append(dict(ti=0, q=0, nidx=0, tok0=0, nt=NMETA, vt=0, p0=0, vg=0))
                else:
                    fj = 4 * (ti - 1) + q // 2
                    chunks.append(dict(ti=ti, q=q, nidx=1 + 8 * (ti - 1) + q, tok0=t0 + 64 * q, nt=64, vt=1 + fj, p0=64 * (q % 2), vg=1 + fj // 2))

        def stage_a(ch):
            tok0, nt, p0, vt, ti_, vg = ch['tok0'], ch['nt'], ch['p0'], ch['vt'], ch['ti'], ch['vg']
            ai = self.rr('amb', 3)
            am, amk = self.amb[ai], ('amb', ai)
            for hl in range(2):
                at_b = at_bs[hl]
                self.mm(self.ps[at_b][p0:p0 + nt, 0:nt], KG[64 * hl:64 * hl + 64, tok0:tok0 + nt], QG[64 * hl:64 * hl + 64, tok0:tok0 + nt],
                        True, True, [('KG', ti_), ('QG', ti_)], [('ps', at_b)], tile_position=(64 * hl, p0))
                self.tt('dve', am[p0:p0 + nt, hl * 64:hl * 64 + nt], self.ps[at_b][p0:p0 + nt, 0:nt], self.trimask2[p0:p0 + nt, 0:nt], ALU.mult,
                        [('ps', at_b), ('c', 'tri2')], [amk])
            su_b = self.bank('su')
            for hl in range(2):
                self.mm(self.ps[su_b][64 * hl:64 * hl + 64, 0:128], KT[p0:p0 + nt, vt, 64 * hl:64 * hl + 64], VB[p0:p0 + nt, vt, 128 * hl:128 * hl + 128],
                        True, True, [('KT', ti_), ('VB', vg)], [('ps', su_b)], tile_position=(p0, 64 * hl))
            ch['am'] = (am, amk); ch['su'] = su_b

        stage_a(chunks[0])
        obs = None
        for ci, ch in enumerate(chunks):
            ti, q, nidx, tok0, nt, vt, p0, vg = ch['ti'], ch['q'], ch['nidx'], ch['tok0'], ch['nt'], ch['vt'], ch['p0'], ch['vg']
            t0, n = tiles[ti]
            if q == 0:
                obs = [self.bank('oa') for _ in range(1 if ti == 0 else 2)]
            if ci + 1 < len(chunks):
                stage_a(chunks[ci + 1])
            am, amk = ch['am']; su_b = ch['su']
            ob = obs[q // 4]
            col0 = (q % 4) * 128
            for hl in range(2):
                first = (nidx == 0)
                self.mm(self.ps[ob][:, col0 + hl * 64:col0 + hl * 64 + nt], VB[p0:p0 + nt, vt, 128 * hl:128 * hl + 128],
                        am[p0:p0 + nt, hl * 64:hl * 64 + nt], True, first, [amk, ('VB', vg)], [('ps', ob)], tile_position=(p0, 0))
                if not first:
                    self.mm(self.ps[ob][:, col0 + hl * 64:col0 + hl * 64 + nt], Sb[64 * hl:64 * hl + 64, :], QG[64 * hl:64 * hl + 64, tok0:tok0 + nt],
                            False, True, [('Sb',), ('QG', ti)], [('ps', ob)], tile_position=(64 * hl, 0))
            self.stt('dve', Sf[:, :], Sf[:, :], EL[:, nidx:nidx + 1], self.ps[su_b][:, 0:128], ALU.mult, ALU.add, [('Sf',), ('EL', ti), ('ps', su_b)], [('Sf',)])
            self.copy('act', Sb[:, :], Sf[:, :], [('Sf',)], [('Sb',)])
            if q != (0 if ti == 0 else 7):
                continue
            fin = {}
            for hl in range(2):
                of, ofk = self.xbuf()
                if ti == 0:
                    self.copy('act', of[:, 0:n], self.ps[obs[0]][:, hl * 64:hl * 64 + n], [('ps', obs[0])], [ofk])
                else:
                    for bq in range(2):
                        self.copy('act', of[:, bq * 256:(bq + 1) * 256].rearrange("p (a t) -> p a t", a=4),
                                  self.ps[obs[bq]][:, :].rearrange("p (a h t) -> p a h t", a=4, h=2)[:, :, hl, :], [('ps', obs[bq])], [ofk])
                sq, sqk = self.bbuf()
                self.act(sq[:, 0:n], of[:, 0:n], AF.Square, [ofk], [sqk])
                fin[hl] = [of, ofk, sq, sqk]
            for hl in range(2):
                b_r = self.bank('pj')
                for kc in range(KC):
                    self.mm(self.ps[b_r][:, 0:n], w2[:, kc, 256 + 128 * hl:256 + 128 * (hl + 1)], self.hT[:, kc, t0:t0 + n], kc == 0, kc == KC - 1,
                            wk2 + [('h', kc, ti)], [('ps', b_r)])
                sg, sgk = self.fbuf()
                self.act(sg[:, 0:n], self.ps[b_r][:, 0:n], AF.Exp, [('ps', b_r)], [sgk], scale=-1.0)
                self.act(sg[:, 0:n], sg[:, 0:n], AF.Ln, [sgk, ('c', 'one')], [sgk], bias=self.one_col[:, 0:1], scale=1.0)
                self.act(sg[:, 0:n], sg[:, 0:n], AF.Exp, [sgk], [sgk], scale=-1.0)
                self.tt('dve', sg[:, 0:n], self.ps[b_r][:, 0:n], sg[:, 0:n], ALU.mult, [('ps', b_r), sgk], [sgk])
                fin[hl] += [sg, sgk]
            for hl in range(2):
                of, ofk, sq, sqk, sg, sgk = fin[hl]
                b2 = self.bank('pj')
                self.mm(self.ps[b2][:, 0:n], self.ones_bf[:, :], sq[:, 0:n], True, True, [sqk, ('c', 'ones')], [('ps', b2)])
                rs, rk = self.fbuf()
                self.act(rs[:, 0:n], self.ps[b2][:, 0:n], AF.Ln, [('ps', b2), ('c', 'eps')], [rk], bias=self.eps_col[:, 0:1], scale=1.0 / 128)
                self.act(rs[:, 0:n], rs[:, 0:n], AF.Exp, [rk], [rk], scale=-0.5)
                self.stt('dve', of[:, 0:n], of[:, 0:n], onc, rs[:, 0:n], ALU.mult, ALU.mult, [ofk, rk, ('c', 'cga')], [ofk])
                self.tt('dve', mixq[:, hl, t0:t0 + n], of[:, 0:n], sg[:, 0:n], ALU.mult, [ofk, sgk], [self.ak(('mq', hl, ti))])
        self.dma('sp', self.o_b[i, s, 2 * c:2 * c + 2].rearrange("h k v -> (h k) v"), Sf[:, :], [('Sf',)], [('ob', i, s, c)])

    def sample_pass(self):
        P = Pass(); P.s = -1; P.T = 64; P.tiles = [(0, 64)]
        self.make_plan(P, sample=True)
        ns = self.n_seq
        for half in range(2):
            st, sk = self.obuf()
            self.dma('sp', st[0:64, :], self.x_sample.rearrange("b t d -> (b t) d")[:, half * 512:(half + 1) * 512], [], [sk])
            bank = self.bank('tr')
            for q in range(4):
                self.tr(self.ps[bank][:, q * 128:q * 128 + 64], st[0:64, q * 128:(q + 1) * 128], self.ident[0:64, 0:64], [sk, ('c', 'ident')], [('ps', bank)])
            self.copy('act' if half else 'dve', self.xT[:, half * 4:(half + 1) * 4, 0:64],
                      self.ps[bank][:, :].rearrange("p (q t) -> p q t", q=4)[:, :, 0:64], [('ps', bank)], [('x', half * 4 + q, 0) for q in range(4)])
        for l in self.layers:
            if self.mixers:
                self.bank_roles = self.roles_mix
                if l % 2 == 1:
                    self.fox_sample(P, l)
                else:
                    self.ab_sample(P, l)
                self.bank_roles = self.roles_mlp
            if self.do_mlp:
                self.mlp(P, l)
        for half in range(2):
            st, sk = self.obuf()
            bank = self.bank('tr')
            for q in range(4):
                kc = half * 4 + q
                self.tr(self.ps[bank][0:64, q * 128:(q + 1) * 128], self.xT[:, kc, 0:64], self.ident[:, :], [('x', kc, 0), ('c', 'ident')], [('ps', bank)])
            self.copy('act' if half else 'dve', st[0:64, :], self.ps[bank][0:64, :], [('ps', bank)], [sk])
            self.dma('sp', self.y_sample.rearrange("b t d -> (b t) d")[:, half * 512:(half + 1) * 512], st[0:64, :], [sk], [('ys', half)])

    def ab_sample(self, P, l):
        i = l // 2; tiles = P.tiles
        self.rmsnorm(P, self.gmix, l)
        self.fence()
        QA, KA, VA, mixq = self.QA, self.KA, self.VA, self.mixq
        self.memset('pool', VA[:, 0:4, :, 64:128], 1.0, [self.ak(('VAo',))])
        for h in range(2):
            for p in range(2):
                for c in range(2):
                    idx = 8 * i + 4 * h + 2 * c + p
                    g = p * 2 + c
                    self.act(self.sinkrow[0:1, h, g * 128:(g + 1) * 128], self.onesF[0:1, 0:128], AF.Exp, [('c', 'onesF'), ('c', 'sink16')],
                             [('sinkrow', h)], bias=self.sink16[0:1, idx:idx + 1], scale=0.0)
        ropeS = (self.ropeS[:, 0, :], self.ropeS[:, 1, :], [('c', 'ropeS')])
        wk, wv = self.take()
        gk = self.cga[:, 2 + i:3 + i]
        for a in range(2):
            self.qk_rope(P, wk, wv[:, :, (0 if a == 0 else 256):(128 if a == 0 else 384)], 0, 0, 64, gk, KA[:, a, 0:64],
                         self.ak(('KA', a, 0)), 'sample' if a == 0 else None, i, ropetab=ropeS)
        bank = self.bank('pj')
        for kc in range(KC):
            self.mm(self.ps[bank][0:64, 0:128], self.hT[:, kc, 0:64], wv[:, kc, 128:256], kc == 0, kc == KC - 1, wk + [('h', kc, 0)], [('ps', bank)])
        self.copy('dve', VA[0:64, 0, :, 0:64], self.ps[bank][0:64, 0:128].rearrange("p (h d) -> p h d", h=2), [('ps', bank)], [self.ak(('VA', 0))])
        st, sk = self.obuf()
        self.copy('act', st[0:64, 0:128], self.ps[bank][0:64, 0:128], [('ps', bank)], [sk])
        for b in range(self.n_seq):
            self.dma('sp', self.o_avs[i, b, 96:128].rearrange("t h d -> t (h d)"), st[32 * b:32 * b + 32, 0:128], [sk], [('oavs', i, b)])
        for b in range(self.n_seq):
            self.dma('sp', self.o_aks[i, b, 0:96], self.cache_a_k[i, b, 32:128], [], [('oaks0', i, b)])
            self.dma('sp', self.o_avs[i, b, 0:96], self.cache_a_v[i, b, 32:128], [], [('oavs0', i, b)])
            self.dma('pool', VA[:, 2 + b, :, 0:64], self.cache_a_v[i, b], [], [self.ak(('VA', 2 + b))])
            for a in range(2):
                st, sk = self.obuf()
                if a == 0:
                    self.dma('sp', st[:, 0:128], self.cache_a_k[i, b].rearrange("t h d -> t (h d)"), [], [sk])
                else:
                    self.dma('sp', st[:, 0:64], self.cache_a_k[i, b, :, 1, :], [], [sk])
                    self.dma('sp', st[:, 64:128], self.cache_a_k[i, b, :, 0, :], [], [sk])
                bank = self.bank('tr')
                self.tr(self.ps[bank][:, 0:128], st[:, 0:128], self.ident[:, :], [sk, ('c', 'ident')], [('ps', bank)])
                self.copy('act', KA[:, a, 64 + 128 * b:64 + 128 * (b + 1)], self.ps[bank][:, 0:128], [('ps', bank)], [self.ak(('KAc', a, b))])
        gq = self.cga[:, i:i + 1]
        for h in range(2):
            wkq, wq = self.take()
            for c in range(2):
                self.qk_rope(P, wkq, wq[:, :, 128 * c:128 * (c + 1)], 0, 0, 64, gq, QA[:, c, 0:64], self.ak(('QA', c, 0)), None, i, ropetab=ropeS)
            for b in range(self.n_seq):
                nq = 32; W = 128; q0 = 32 * b
                ob = self.bank('oa')
                kts = [(2 + b, 64 + 128 * b, 128, 0, ('KAc', b)), (0, 32 * b, 32, 32 * b, ('KA', 0))]
                for idx, (vt, tok0, nk, pb, kk) in enumerate(kts):
                    pt, ptk = self.bbuf()
                    for p in range(2):
                        a = h ^ p
                        sbk = self.bank_roles['sc'][p]
                        kkey = (kk[0], a, kk[1])
                        self.mm(self.ps[sbk][pb:pb + nk, 0:2 * nq], KA[64 * p:64 * p + 64, a, tok0:tok0 + nk], QA[64 * p:64 * p + 64, :, q0:q0 + nq],
                                True, True, [kkey, ('QA', 0, 0), ('QA', 1, 0)], [('ps', sbk)], tile_position=(64 * p, pb))
                        self.act(pt[pb:pb + nk, p * 2 * nq:(p + 1) * 2 * nq], self.ps[sbk][pb:pb + nk, 0:2 * nq], AF.Exp, [('ps', sbk)], [ptk])
                    self.mm(self.ps[ob][:, 0:W], VA[pb:pb + nk, vt, h, :], pt[pb:pb + nk, 0:W], idx == 0, False, [ptk, ('VA', vt), ('VAo',)], [('ps', ob)],
                            tile_position=(pb, 0))
                srow = self.sinkrow[0:1, h, :].rearrange("o (g q) -> o g q", q=128)[:, :, 0:nq]
                self.mm(self.ps[ob][:, 0:W], self.vsink[0:1, 0:128], srow, False, True, [('sinkrow', h), ('c', 'vsink')], [('ps', ob)])
                rc, rck = self.fbuf()
                self.recip(rc[0:64, 0:W], self.ps[ob][64:128, 0:W], [('ps', ob)], [rck])
                for p in range(2):
                    self.tt('dve', mixq[64 * p:64 * p + 64, :, q0:q0 + nq], self.ps[ob][0:64, p * 2 * nq:(p + 1) * 2 * nq].rearrange("p (c q) -> p c q", c=2),
                            rc[0:64, p * 2 * nq:(p + 1) * 2 * nq].rearrange("p (c q) -> p c q", c=2), ALU.mult,
                            [('ps', ob), rck], [self.ak(('mq', 0, 0)), self.ak(('mq', 1, 0))])
            self.out_quad(P, self.take(), tiles)
        self.fence()
        for c in range(2):
            self.gla_sample_pair(P, l, c)
            self.out_quad(P, self.take(), tiles)

    def gla_sample_pair(self, P, l, c):
        i = l // 2
        ar = self.arena
        QG = ar[:, 0:TP]; KG = ar[:, TP:2 * TP]
        KT = ar[:, 2 * TP:2 * TP + 17 * 128].rearrange("p (j f) -> p j f", j=17)
        o = 2 * TP + 17 * 128
        VB = ar[:, o:o + 17 * 256].rearrange("p (j f) -> p j f", j=17)
        mixq = self.mixq
        EL = self.EL; Sf = self.Sf; Sb = self.Sb
        wk1, w1 = self.take()
        wk2, w2 = self.take(keep=2)
        negbg = self.cga[:, 4 + 2 * i + c:5 + 2 * i + c]
        onc = self.cga[:, 8 + i:9 + i]
        n = 64; cw = 32; nchk = 2; t0 = 0; ti = 0
        b_g = self.bank('st')
        for kc in range(KC):
            self.mm(self.ps[b_g][0:16, 0:n], w1[:, kc, 256:272], self.hT[:, kc, 0:n], kc == 0, kc == KC - 1, wk1 + [('h', kc, 0)], [('ps', b_g)])
        gl, glk = self.bbuf()
        self.copy('dve', gl[0:16, 0:n], self.ps[b_g][0:16, 0:n], [('ps', b_g)], [glk])
        b_z = self.bank('st')
        self.mm(self.ps[b_z][:, 0:n], self.wgb[0:16, i, 128 * c:128 * (c + 1)], gl[0:16, 0:n], True, True, [glk, ('c', 'wgb')], [('ps', b_z)])
        cum, cumk = self.xbuf()
        self.act(cum[:, 0:n], self.ps[b_z][:, 0:n], AF.Exp, [('ps', b_z), ('c', 'cga')], [cumk], bias=negbg, scale=-1.0)
        self.act(cum[:, 0:n], cum[:, 0:n], AF.Ln, [cumk, ('c', 'one')], [cumk], bias=self.one_col[:, 0:1], scale=1.0)
        self.scan(cum[:, 0:n], self.segmask[:, 32:96], cum[:, 0:n], 0.0, [cumk, ('c', 'seg')], [cumk])
        eb, ebk = self.xbuf()
        self.act(eb[:, 0:n], cum[:, 0:n], AF.Exp, [cumk], [ebk], scale=-1.0 / 16)
        self.copy('dve', EL[:, 0:nchk], eb[:, 0:n].rearrange("p (a b) -> p a b", b=cw)[:, :, cw - 1], [ebk], [('EL', 0)])
        b_q = self.bank('pj')
        for kc in range(KC):
            self.mm(self.ps[b_q][:, 0:n], w1[:, kc, 0:128], self.hT[:, kc, 0:n], kc == 0, kc == KC - 1, wk1 + [('h', kc, 0)], [('ps', b_q)])
        self.stt('dve', QG[:, 0:n], self.ps[b_q][:, 0:n], SCALE, eb[:, 0:n], ALU.mult, ALU.mult, [('ps', b_q), ebk], [self.ak(('QG', 0))])
        b_k = self.bank('pj')
        for kc in range(KC):
            self.mm(self.ps[b_k][:, 0:n], w1[:, kc, 128:256], self.hT[:, kc, 0:n], kc == 0, kc == KC - 1, wk1 + [('h', kc, 0)], [('ps', b_k)])
        en, enk = self.xbuf()
        self.act(en[:, 0:n], cum[:, 0:n], AF.Exp, [cumk], [enk], scale=1.0 / 16)
        self.tt('dve', KG[:, 0:n], self.ps[b_k][:, 0:n], en[:, 0:n], ALU.mult, [('ps', b_k), enk], [self.ak(('KG', 0))])
        c3 = cum[:, 0:n].rearrange("p (a b) -> p a b", b=cw)
        e3 = en[:, 0:n].rearrange("p (a b) -> p a b", b=cw)
        self.tt('dve', e3, c3, c3[:, :, cw - 1:cw].to_broadcast([128, nchk, cw]), ALU.subtract, [cumk, enk], [enk])
        self.act(en[:, 0:n], en[:, 0:n], AF.Exp, [enk], [enk], scale=1.0 / 16)
        kh, khk = self.fbuf()
        self.tt('dve', kh[:, 0:n], self.ps[b_k][:, 0:n], en[:, 0:n], ALU.mult, [('ps', b_k), enk], [khk])
        b_t = self.bank('tr')
        self.tr(self.ps[b_t][0:n, 0:128], kh[:, 0:n], self.ident[:, :], [khk, ('c', 'ident')], [('ps', b_t)])
        self.copy('act', KT[0:n, 0, :], self.ps[b_t][0:n, 0:128], [('ps', b_t)], [self.ak(('KT', 0))])
        bank = self.bank('pj')
        for kc in range(KC):
            self.mm(self.ps[bank][0:n, 0:256], self.hT[:, kc, 0:n], w2[:, kc, 0:256], kc == 0, kc == KC - 1, wk2 + [('h', kc, 0)], [('ps', bank)])
        self.copy('act', VB[0:n, 0, :], self.ps[bank][0:n, 0:256], [('ps', bank)], [self.ak(('VB', 0))])
        roles = self.bank_roles
        at_bs, su_b = roles['sc'], roles['tr'][0]
        for b in range(self.n_seq):
            self.dma('sp', Sf[:, :], self.state_b[i, b, 2 * c:2 * c + 2].rearrange("h k v -> (h k) v"), [], [('Sf',)])
            self.copy('act', Sb[:, :], Sf[:, :], [('Sf',)], [('Sb',)])
            tok0, nt, p0 = 32 * b, 32, 32 * b
            am, amk = self.bbuf()
            for hl in range(2):
                at_b = at_bs[hl]
                self.mm(self.ps[at_b][p0:p0 + nt, 0:nt], KG[64 * hl:64 * hl + 64, tok0:tok0 + nt], QG[64 * hl:64 * hl + 64, tok0:tok0 + nt],
                        True, True, [('KG', 0), ('QG', 0)], [('ps', at_b)], tile_position=(64 * hl, p0))
                self.tt('dve', am[p0:p0 + nt, hl * 32:hl * 32 + nt], self.ps[at_b][p0:p0 + nt, 0:nt], self.trimask3[p0:p0 + nt, 0:nt], ALU.mult,
                        [('ps', at_b), ('c', 'tri3')], [amk])
            ob = self.bank('oa')
            for hl in range(2):
                self.mm(self.ps[ob][:, hl * 32:hl * 32 + nt], VB[p0:p0 + nt, 0, 128 * hl:128 * hl + 128], am[p0:p0 + nt, hl * 32:hl * 32 + nt], True, False,
                        [amk, ('VB', 0)], [('ps', ob)], tile_position=(p0, 0))
                self.mm(self.ps[ob][:, hl * 32:hl * 32 + nt], Sb[64 * hl:64 * hl + 64, :], QG[64 * hl:64 * hl + 64, tok0:tok0 + nt], False, True,
                        [('Sb',), ('QG', 0)], [('ps', ob)], tile_position=(64 * hl, 0))
            for hl in range(2):
                self.mm(self.ps[su_b][64 * hl:64 * hl + 64, 0:128], KT[p0:p0 + nt, 0, 64 * hl:64 * hl + 64], VB[p0:p0 + nt, 0, 128 * hl:128 * hl + 128],
                        True, True, [('KT', 0), ('VB', 0)], [('ps', su_b)], tile_position=(p0, 64 * hl))
            self.stt('dve', Sf[:, :], Sf[:, :], EL[:, b:b + 1], self.ps[su_b][:, 0:128], ALU.mult, ALU.add, [('Sf',), ('EL', 0), ('ps', su_b)], [('Sf',)])
            self.dma('sp', self.o_bs[i, b, 2 * c:2 * c + 2].rearrange("h k v -> (h k) v"), Sf[:, :], [('Sf',)], [('obs', i, b, c)])
            for hl in range(2):
                of, ofk = self.xbuf()
                self.copy('act', of[:, 0:nt], self.ps[ob][:, hl * 32:hl * 32 + nt], [('ps', ob)], [ofk])
                sq, sqk = self.bbuf()
                self.act(sq[:, 0:nt], of[:, 0:nt], AF.Square, [ofk], [sqk])
                b2 = self.bank('st')
                self.mm(self.ps[b2][:, 0:nt], self.ones_bf[:, :], sq[:, 0:nt], True, True, [sqk, ('c', 'ones')], [('ps', b2)])
                rs, rk = self.fbuf()
                self.act(rs[:, 0:nt], self.ps[b2][:, 0:nt], AF.Ln, [('ps', b2), ('c', 'eps')], [rk], bias=self.eps_col[:, 0:1], scale=1.0 / 128)
                self.act(rs[:, 0:nt], rs[:, 0:nt], AF.Exp, [rk], [rk], scale=-0.5)
                b_r = self.bank('pj')
                for kc in range(KC):
                    self.mm(self.ps[b_r][:, 0:nt], w2[:, kc, 256 + 128 * hl:256 + 128 * (hl + 1)], self.hT[:, kc, tok0:tok0 + nt], kc == 0, kc == KC - 1,
                            wk2 + [('h', kc, 0)], [('ps', b_r)])
                sg, sgk = self.fbuf()
                self.act(sg[:, 0:nt], self.ps[b_r][:, 0:nt], AF.Exp, [('ps', b_r)], [sgk], scale=-1.0)
                self.ts('dve', sg[:, 0:nt], sg[:, 0:nt], 1.0, None, ALU.add, None, [sgk], [sgk])
                self.recip(sg[:, 0:nt], sg[:, 0:nt], [sgk], [sgk])
                self.tt('dve', sg[:, 0:nt], self.ps[b_r][:, 0:nt], sg[:, 0:nt], ALU.mult, [('ps', b_r), sgk], [sgk])
                self.stt('dve', of[:, 0:nt], of[:, 0:nt], onc, rs[:, 0:nt], ALU.mult, ALU.mult, [ofk, rk, ('c', 'cga')], [ofk])
                self.tt('dve', mixq[:, hl, tok0:tok0 + nt], of[:, 0:nt], sg[:, 0:nt], ALU.mult, [ofk, sgk], [self.ak(('mq', hl, 0))])

    def fox_sample(self, P, l):
        i = l // 2; tiles = P.tiles
        PL = 2048; TK = PL + 32
        self.rmsnorm(P, self.gmix, l)
        self.fence()
        VA, mixq = self.VA, self.mixq
        QA = self.arena[:, 0:128].rearrange("p (h t) -> p h t", h=2)
        KA = self.arena[:, 128:128 + 2 * TK].rearrange("p (h t) -> p h t", h=2)
        ident = ('c', 'ident')
        self.memset('pool', QA[64:70, :, :], 1.0, [self.ak(('QAa', 0)), self.ak(('QAa', 1))])
        self.memset('pool', KA[64:70, :, :], -1.0, [self.ak(('KAa', 0)), self.ak(('KAa', 1))])
        self.memset('pool', VA[:, :, :, 64:128], 1.0, [self.ak(('VAo',))])
        cp = self.xscr
        wk, wv = self.take()
        bank = self.bank('st')
        for kc in range(KC):
            self.mm(self.ps[bank][0:16, 0:64], wv[:, kc, 0:16], self.hT[:, kc, 0:64], kc == 0, kc == KC - 1, wk + [('h', kc, 0)], [('ps', bank)])
        nls, nlk = self.fbuf()
        self.act(nls[0:16, 0:64], self.ps[bank][0:16, 0:64], AF.Exp, [('ps', bank), ('c', 'cg')], [nlk], bias=self.cg[0:16, 4 + i:5 + i], scale=-1.0)
        self.act(nls[0:16, 0:64], nls[0:16, 0:64], AF.Ln, [nlk, ('c', 'one')], [nlk], bias=self.one_col[0:16, 0:1], scale=1.0)
        bank = self.bank('tr')
        self.tr(self.ps[bank][0:64, 0:16], nls[0:16, 0:64], self.ident[0:16, 0:16], [nlk, ident], [('ps', bank)])
        st, sk = self.obuf()
        self.amul(st[0:64, 0:16], self.ps[bank][0:64, 0:16], -1.0, [('ps', bank)], [sk])
        for b in range(self.n_seq):
            self.dma('sp', self.o_cfs[i, b], st[32 * b:32 * b + 32, 0:16], [sk], [('ocfs', i, b)])
        cs = [cp[32 + 32 * j:48 + 32 * j, :].bitcast(BF16)[:, 0:TK] for j in range(3)]
        ctiles = [(512 * j, 512) for j in range(4)] + [(PL, 32)]
        for b in range(self.n_seq):
            st, sk = self.obuf()
            for hf in range(2):
                self.dma('sp', st[:, 128 * hf:128 * (hf + 1)].rearrange("p (j h) -> p j h", h=16),
                         self.cache_c_logf[i, b].rearrange("(j p) h -> p j h", p=128)[:, 8 * hf:8 * (hf + 1), :], [], [sk])
            for g in range(4):
                bank = self.bank('tr')
                for q in range(4):
                    j = 4 * g + q
                    self.tr(self.ps[bank][0:16, q * 128:(q + 1) * 128], st[:, 16 * j:16 * (j + 1)], self.ident[:, :], [sk, ident], [('ps', bank)])
                self.amul(cp[0:16, 512 * g:512 * (g + 1)], self.ps[bank][0:16, :], -1.0, [('ps', bank)], [self.ak(('cpos', g))])
            self.copy('dve', cp[0:16, PL:TK], nls[0:16, 32 * b:32 * b + 32], [nlk], [self.ak(('cpos', 4))])
            for ti, (t0, n) in enumerate(ctiles):
                c = cp[0:16, t0:t0 + n]
                init = 0.0 if ti == 0 else cp[0:16, t0 - 1:t0]
                rd = [('cpos', ti), ('c', 'onesF')] + ([('cpos', ti - 1)] if ti else [])
                self.scan(c, self.onesF[0:16, 0:n], c, init, rd, [('cpos', ti)])
            for ti, (t0, n) in enumerate(ctiles):
                c = cp[0:16, t0:t0 + n]
                for j in range(2):
                    mb, mk = self.bbuf()
                    self.copy('dve', mb[0:16, 0:n], c, [('cpos', ti)], [mk])
                    self.copy('pool', cs[j][:, t0:t0 + n], mb[0:16, 0:n], [mk], [self.ak(('cs', ti))])
                    self.tt('dve', c, c, mb[0:16, 0:n], ALU.subtract, [('cpos', ti), mk], [('cpos', ti)])
                self.copy('dve', cs[2][:, t0:t0 + n], c, [('cpos', ti)], [self.ak(('cs', ti))])
            cskeys = [('cs', ti) for ti in range(5)]
            for hp in range(8):
                wk, wv = self.take()
                for hl in range(2):
                    h = 2 * hp + hl
                    for j in range(3):
                        self.dma('sp', KA[64 + j:65 + j, hl, 0:TK], cs[j][h:h + 1, 0:TK], cskeys, [self.ak(('KAa', hl))])
                        self.dma('sp', QA[67 + j:68 + j, hl, 0:32], cs[j][h:h + 1, PL:TK], cskeys, [self.ak(('QAa', hl))])
                cv = self.cache_c_v[i, b].rearrange("(j p) h d -> p j h d", p=128)
                for hf in range(2):
                    for hl in range(2):
                        self.dma('pool', VA[:, 8 * hf:8 * (hf + 1), hl, 0:64], cv[:, 8 * hf:8 * (hf + 1), 2 * hp + hl, :], [], [self.ak(('VA', hf))])
                ck = self.cache_c_k[i, b].rearrange("(j p) h d -> p j (h d)", p=128)
                kst = []
                for g in range(4):
                    st, sk = self.obuf()
                    self.dma('sp', st[:, :].rearrange("p (j f) -> p j f", j=4), ck[:, 4 * g:4 * (g + 1), 128 * hp:128 * (hp + 1)], [], [sk])
                    kst.append((st, sk))
                c0 = 32 * b
                pr = {}
                for which in range(2):
                    bank = self.bank('pjw')
                    for kc in range(KC):
                        self.mm(self.ps[bank][:, 0:32], wv[:, kc, which * 128:(which + 1) * 128], self.hT[:, kc, c0:c0 + 32], kc == 0, kc == KC - 1,
                                wk + [('h', kc, 0)], [('ps', bank)])
                    sq, sqk = self.bbuf()
                    self.act(sq[:, 0:32], self.ps[bank][:, 0:32], AF.Square, [('ps', bank)], [sqk])
                    pr[which] = (bank, sq, sqk)
                bank_v = self.bank('pjw')
                for kc in range(KC):
                    self.mm(self.ps[bank_v][0:32, 0:128], self.hT[:, kc, c0:c0 + 32], wv[:, kc, 256:384], kc == 0, kc == KC - 1, wk + [('h', kc, 0)], [('ps', bank_v)])
                for g in range(4):
                    st, sk = kst[g]
                    bank = self.bank('st')
                    for q in range(4):
                        self.tr(self.ps[bank][:, q * 128:(q + 1) * 128], st[:, q * 128:(q + 1) * 128], self.ident[:, :], [sk, ident], [('ps', bank)])
                    for hl in range(2):
                        self.copy('act' if hl else 'dve', KA[0:64, hl, 512 * g:512 * (g + 1)], self.ps[bank][64 * hl:64 * hl + 64, :], [('ps', bank)],
                                  [self.ak(('KA', hl, g))])
                kf_item = None
                for which in range(2):
                    gcol = self.cg[:, which * 2 + i:which * 2 + i + 1]
                    bank, sq, sqk = pr[which]
                    b2 = self.bank('st')
                    self.mm(self.ps[b2][:, 0:32], self.blockones[:, :], sq[:, 0:32], True, True, [sqk, ('c', 'bo')], [('ps', b2)])
                    rs, rk = self.fbuf()
                    self.act(rs[:, 0:32], self.ps[b2][:, 0:32], AF.Ln, [('ps', b2), ('c', 'eps')], [rk], bias=self.eps_col[:, 0:1], scale=1.0 / HD)
                    self.act(rs[:, 0:32], rs[:, 0:32], AF.Exp, [rk], [rk], scale=-0.5)
                    for hl in range(2):
                        p0 = 64 * hl
                        dst = QA[0:64, hl, 0:32] if which == 0 else KA[0:64, hl, PL:TK]
                        dkey = self.ak(('QA', hl, 0)) if which == 0 else self.ak(('KA', hl, 4))
                        self.stt('dve', dst, self.ps[bank][p0:p0 + 64, 0:32], gcol[p0:p0 + 64, :], rs[p0:p0 + 64, 0:32], ALU.mult, ALU.mult,
                                 [('ps', bank), rk, ('c', 'cg')], [dkey])
                    if which == 1:
                        kf, kfk = self.fbuf()
                        self.stt('dve', kf[:, 0:32], self.ps[bank][:, 0:32], gcol, rs[:, 0:32], ALU.mult, ALU.mult, [('ps', bank), rk, ('c', 'cg')], [kfk])
                        kf_item = (kf, kfk)
                st, sk = self.obuf()
                self.copy('act', st[0:32, 0:128], self.ps[bank_v][0:32, 0:128], [('ps', bank_v)], [sk])
                self.copy('dve', VA[0:32, 16, :, 0:64], self.ps[bank_v][0:32, 0:128].rearrange("p (h d) -> p h d", h=2), [('ps', bank_v)], [self.ak(('VA', 2))])
                self.dma('sp', self.o_cvs[i, b, :, 2 * hp:2 * hp + 2, :].rearrange("t h d -> t (h d)"), st[0:32, 0:128], [sk], [('ocvs', i, b, hp)])
                obs2 = [self.bank('oa'), self.bank('oa')]
                pend = []
                LOOK = 4
                nkt = 17

                def pv2(item, idx, obs2=obs2):
                    (j, nk), pt, ptk = item
                    for hl in range(2):
                        self.mm(self.ps[obs2[hl]][:, 0:32], VA[0:nk, j, hl, :], pt[0:nk, 32 * hl:32 * hl + 32], idx == 0, idx == nkt - 1,
                                [ptk, ('VA', 0 if j < 8 else (1 if j < 16 else 2)), ('VAo',)], [('ps', obs2[hl])])

                for j in range(17):
                    nk = 128 if j < 16 else 32
                    sbk = self.bank('sc4')
                    for hl in range(2):
                        kkey = ('KA', hl, j // 4) if j < 16 else ('KA', hl, 4)
                        self.mm(self.ps[sbk][0:nk, 32 * hl:32 * hl + 32], KA[0:70, hl, 128 * j:128 * j + nk], QA[0:70, hl, 0:32], True, True,
                                [kkey, ('KAa', hl), ('QA', hl, 0), ('QAa', hl)], [('ps', sbk)])
                    pt, ptk = self.pbuf()
                    self.act(pt[0:nk, 0:64], self.ps[sbk][0:nk, 0:64], AF.Exp, [('ps', sbk)], [ptk])
                    if j == 16:
                        self.tt('dve', pt[0:32, 0:64].rearrange("p (h q) -> p h q", h=2), pt[0:32, 0:64].rearrange("p (h q) -> p h q", h=2),
                                self.trimask[0:32, 0:32].unsqueeze(1).to_broadcast([32, 2, 32]), ALU.mult, [ptk, ('c', 'tri')], [ptk])
                    pend.append((((j, nk), pt, ptk), j))
                    if len(pend) > LOOK:
                        it, ix = pend.pop(0)
                        pv2(it, ix)
                for it, ix in pend:
                    pv2(it, ix)
                for hl in range(2):
                    ob = obs2[hl]
                    rc, rck = self.fbuf()
                    self.recip(rc[0:64, 0:32], self.ps[ob][64:128, 0:32], [('ps', ob)], [rck])
                    self.tt('dve', mixq[64 * hl:64 * hl + 64, hp % 2, c0:c0 + 32], self.ps[ob][0:64, 0:32], rc[0:64, 0:32], ALU.mult,
                            [('ps', ob), rck], [self.ak(('mq', hp % 2, 0))])
                self.tm_out(self.o_cks[i, b], hp, 0, 32, kf_item[0], kf_item[1])
                if hp % 2 == 1:
                    wk2, wo = self.take()
                    for dc in range(KC):
                        bank = self.bank('pjw')
                        for kc in range(2):
                            self.mm(self.ps[bank][:, 0:32], wo[:, kc, dc * 128:(dc + 1) * 128], mixq[:, kc, c0:c0 + 32], kc == 0, kc == 1,
                                    wk2 + [('mq', kc, 0)], [('ps', bank)])
                        self.tt('dve', self.xT[:, dc, c0:c0 + 32], self.xT[:, dc, c0:c0 + 32], self.ps[bank][:, 0:32], ALU.add,
                                [('ps', bank), ('x', dc, 0)], [('x', dc, 0)])

    def scan(self, out, data0, data1, init, r, w):
        self.R.op('dve', lambda e: e.tensor_tensor_scan(out=out, data0=data0, data1=data1, initial=init, op0=ALU.mult, op1=ALU.add), r, w)

    def emit(self):
        nc = self.nc
        R = self.R
        R.finalize()
        sem_tl = {}
        for e in ENG:
            for ep in range(R.nep[e]):
                sem_tl[(e, ep)] = self.es.enter_context(nc.semaphore(f"tl_{e}_{ep}"))
        sem_dma = {}
        for e in ('sp', 'act', 'pool'):
            for i in range(NDSEM):
                if R.ring[e]['count'][i] > 0:
                    sem_dma[(e, i)] = self.es.enter_context(nc.semaphore(f"dma_{e}_{i}"))
        block = self.es.enter_context(nc.Block())

        def run(e, engine):
            for ins in R.q[e]:
                for k, v in ins.waits.items():
                    sem = sem_tl[(k[1], k[2])] if k[0] == 'tl' else sem_dma[(k[1], k[2])]
                    engine.wait_ge(sem, v)
                bi = ins.fn(engine)
                if ins.dma:
                    bi.then_inc(sem_dma[ins.sem], 16)
                elif ins.inc:
                    bi.then_inc(sem_tl[(e, ins.ep)], 1)
            if e == 'sp':
                for (q, i), sem in sem_dma.items():
                    engine.wait_ge(sem, R.ring[q]['count'][i])

        block.tensor(lambda t: run('pe', t))
        block.scalar(lambda t: run('act', t))
        block.vector(lambda t: run('dve', t))
        block.gpsimd(lambda t: run('pool', t))
        block.sync(lambda t: run('sp', t))
        self.es.close()
        self.counts = {e: len(R.q[e]) for e in ENG}


_W_NAMES = ["meta_tokens", "norm_mix", "norm_mlp", "w_in_ab", "qnorm_a", "knorm_a", "sink_a", "w_gate_b", "b_gate_b", "onorm_b", "w_out_ab",
            "w_in_c", "b_f_c", "qnorm_c", "knorm_c", "w_out_c", "w_up", "w_down"]
_OUT_NAMES = ["y_prompt", "y_sample", "new_a_k_p", "new_a_v_p", "new_b_p", "new_c_k_p", "new_c_v_p", "new_c_logf_p",
              "new_a_k_s", "new_a_v_s", "new_b_s", "new_c_k_s", "new_c_v_s", "new_c_logf_s"]


def kernel(**inputs):
    inp = {k: np.ascontiguousarray(np.asarray(v), dtype=np.float32) for k, v in inputs.items()}
    nb = 2
    b = Builder(layers=(0, 1, 2, 3), n_seq=nb, do_sample=True, mixers=True, do_mlp=True, dbg=False, x_full=False, do_prompt=True)
    nc = b.build()
    in_maps = []
    for c in range(NCORES):
        sl = slice(nb * c, nb * (c + 1))
        m = {"x_prompt": np.ascontiguousarray(inp["x_prompt"][sl]), "x_sample": np.ascontiguousarray(inp["x_sample"][sl])}
        for k in ["cache_a_k", "cache_a_v", "state_b", "cache_c_k", "cache_c_v", "cache_c_logf"]:
            m[k] = np.ascontiguousarray(inp[k][:, sl])
        for k in _W_NAMES:
            m[k] = inp[k]
        in_maps.append(m)
    res = run_bass_kernel_spmd(nc, in_maps, core_ids=list(range(NCORES)))
    outs = []
    for idx, name in enumerate(_OUT_NAMES):
        axis = 0 if idx < 2 else 1
        outs.append(np.concatenate([np.asarray(r[name], dtype=np.float32) for r in res.results], axis=axis))
    return tuple(outs)
```
